# Optimizing a Trainium2 kernel written in Bass

```python
import math
import jax
import jax.numpy as jnp
from jax import lax
import numpy as np

D_MODEL = 1024
BATCH = 16
SEQ = 4096
DEPTH = 4
DEC_BATCH = 32
DEC_SEQ = 2048
PAST_LEN = 128

MLA_HEADS = 8
Q_LORA_RANK = 256
KV_LORA_RANK = 128
QK_NOPE_DIM = 64
QK_ROPE_DIM = 32
V_HEAD_DIM = 64
ROPE_THETA = 10000.0
MLA_Q_BLOCK = 128

DIL_PATTERNS = ((128, 1), (512, 4), (2048, 16))
DIL_HEADS_PER_GROUP = 4
DIL_HEADS = DIL_HEADS_PER_GROUP * len(DIL_PATTERNS)
DIL_HEAD_DIM = 128

D_FF = 2816
RMS_EPS = 1e-6
NEG_INF = -1e30
N_BRANCHES = 2

MLA_OUT = MLA_HEADS * V_HEAD_DIM
DIL_OUT = DIL_HEADS_PER_GROUP * DIL_HEAD_DIM
MLA_IN = Q_LORA_RANK + KV_LORA_RANK + QK_ROPE_DIM
DIL_QKV = DIL_HEADS * DIL_HEAD_DIM
SPLIT_POINTS = (Q_LORA_RANK, Q_LORA_RANK + KV_LORA_RANK, MLA_IN, MLA_IN + DIL_QKV, MLA_IN + 2 * DIL_QKV, MLA_IN + 3 * DIL_QKV)
W_IN_COLS = MLA_IN + 3 * DIL_QKV + N_BRANCHES * D_MODEL

kernel_name = 'hybrid_mla_dilated_encoder'


def _rmsnorm(x, g):
    xf = x.astype(jnp.float32)
    xf = xf * lax.rsqrt(jnp.mean(xf * xf, axis=-1, keepdims=True) + RMS_EPS)
    return (xf * g.astype(jnp.float32)).astype(x.dtype)


def _swiglu(x, w_gate, w_up, w_down):
    return (jax.nn.silu(x @ w_gate) * (x @ w_up)) @ w_down


def _rope_tables(seq):
    pos = jnp.arange(seq, dtype=jnp.float32)
    inv_freq = 1.0 / (ROPE_THETA ** (jnp.arange(0, QK_ROPE_DIM, 2, dtype=jnp.float32) / QK_ROPE_DIM))
    ang = pos[:, None] * inv_freq[None, :]
    return jnp.cos(ang), jnp.sin(ang)


def _apply_rope(x, cos, sin):
    half = x.shape[-1] // 2
    xf = x.astype(jnp.float32)
    x1, x2 = xf[..., :half], xf[..., half:]
    return jnp.concatenate([x1 * cos - x2 * sin, x2 * cos + x1 * sin], axis=-1).astype(x.dtype)


def _alibi_slopes(n):
    return 2.0 ** (-8.0 * jnp.arange(1, n + 1, dtype=jnp.float32) / n)


def _mla_attention(c_q, c_kv, k_pe, q_a_norm, w_q_up, kv_a_norm, w_kv_up):
    b, s, _ = c_q.shape
    q = (_rmsnorm(c_q, q_a_norm) @ w_q_up).reshape(b, s, MLA_HEADS, QK_NOPE_DIM + QK_ROPE_DIM)
    kv = (_rmsnorm(c_kv, kv_a_norm) @ w_kv_up).reshape(b, s, MLA_HEADS, QK_NOPE_DIM + V_HEAD_DIM)
    q_nope, q_pe = q[..., :QK_NOPE_DIM], q[..., QK_NOPE_DIM:]
    k_nope, v = kv[..., :QK_NOPE_DIM], kv[..., QK_NOPE_DIM:]
    cos, sin = _rope_tables(s)
    q_pe = _apply_rope(q_pe, cos[:, None, :], sin[:, None, :])
    k_pe = _apply_rope(k_pe, cos, sin)
    scale = (QK_NOPE_DIM + QK_ROPE_DIM) ** -0.5
    n_blk = s // MLA_Q_BLOCK

    def to_blocks(t):
        return t.reshape(b, n_blk, MLA_Q_BLOCK, MLA_HEADS, t.shape[-1]).transpose(1, 0, 2, 3, 4)

    def attend_block(qs):
        qn, qr = qs
        logits = (jnp.einsum('bqhd,bkhd->bhqk', qn, k_nope, preferred_element_type=jnp.float32)
                  + jnp.einsum('bqhd,bkd->bhqk', qr, k_pe, preferred_element_type=jnp.float32))
        p = jax.nn.softmax(logits * scale, axis=-1).astype(v.dtype)
        return jnp.einsum('bhqk,bkhd->bqhd', p, v)

    o = lax.map(attend_block, (to_blocks(q_nope), to_blocks(q_pe)))
    return o.transpose(1, 0, 2, 3, 4).reshape(b, s, MLA_OUT)


def _dilated_group_attention(q, k, v, window, dilation, slopes):
    b, s, hg, dh = q.shape
    half = window // (2 * dilation)
    seg = s // dilation
    n_blk = -(-seg // half)
    seg_p = n_blk * half

    def strided(t):
        return t.reshape(b, seg, dilation, hg, dh).transpose(0, 2, 1, 3, 4)

    qs = jnp.pad(strided(q), ((0, 0), (0, 0), (0, seg_p - seg), (0, 0), (0, 0))).reshape(b, dilation, n_blk, half, hg, dh)
    kv_pad = ((0, 0), (0, 0), (half, seg_p - seg + half), (0, 0), (0, 0))
    kp = jnp.pad(strided(k), kv_pad)
    vp = jnp.pad(strided(v), kv_pad)

    def key_blocks(t):
        return jnp.concatenate([t[:, :, o * half:o * half + seg_p].reshape(b, dilation, n_blk, half, hg, dh) for o in range(3)], axis=3)

    kb, vb = key_blocks(kp), key_blocks(vp)
    logits = jnp.einsum('brnqhd,brnkhd->brnhqk', qs, kb, preferred_element_type=jnp.float32) * (dh ** -0.5)
    rel = (jnp.arange(3 * half)[None, :] - half) - jnp.arange(half)[:, None]
    kpos = jnp.arange(n_blk)[:, None] * half + jnp.arange(3 * half)[None, :] - half
    allowed = (jnp.abs(rel) <= half)[None] & ((kpos >= 0) & (kpos < seg))[:, None, :]
    alibi = -slopes[:, None, None] * (dilation * jnp.abs(rel)).astype(jnp.float32)[None]
    logits = jnp.where(allowed[None, None, :, None], logits + alibi[None, None, None], NEG_INF)
    m = jnp.max(logits, axis=-1)
    p = jnp.exp(logits - m[..., None])
    l = jnp.sum(p, axis=-1)
    o = jnp.einsum('brnhqk,brnkhd->brnqhd', p.astype(v.dtype), vb)
    o = (o.astype(jnp.float32) / jnp.swapaxes(l, -1, -2)[..., None]).astype(v.dtype)
    lse = jnp.swapaxes(m + jnp.log(l), -1, -2)
    o = o.reshape(b, dilation, seg_p, hg, dh)[:, :, :seg].transpose(0, 2, 1, 3, 4).reshape(b, s, hg, dh)
    lse = lse.reshape(b, dilation, seg_p, hg)[:, :, :seg].transpose(0, 2, 1, 3).reshape(b, s, hg)
    return o, lse


def _hybrid_mixer(h, w_in, b_gate, q_a_norm, w_q_up, kv_a_norm, w_kv_up, w_branch_mla, w_branch_dil, w_out):
    b, s, _ = h.shape
    z = h @ w_in
    c_q, c_kv, k_pe, dq, dk, dv, gate_logits = jnp.split(z, SPLIT_POINTS, axis=-1)
    o_mla = _mla_attention(c_q, c_kv, k_pe, q_a_norm, w_q_up, kv_a_norm, w_kv_up)
    dq = dq.reshape(b, s, DIL_HEADS, DIL_HEAD_DIM)
    dk = dk.reshape(b, s, DIL_HEADS, DIL_HEAD_DIM)
    dv = dv.reshape(b, s, DIL_HEADS, DIL_HEAD_DIM)
    slopes = _alibi_slopes(DIL_HEADS)
    outs, lses = [], []
    for g, (window, dilation) in enumerate(DIL_PATTERNS):
        hs = slice(g * DIL_HEADS_PER_GROUP, (g + 1) * DIL_HEADS_PER_GROUP)
        o_g, lse_g = _dilated_group_attention(dq[:, :, hs], dk[:, :, hs], dv[:, :, hs], window, dilation, slopes[hs])
        outs.append(o_g)
        lses.append(lse_g)
    w = jax.nn.softmax(jnp.stack(lses, axis=0), axis=0)
    o_dil = jnp.sum(w[..., None] * jnp.stack(outs, axis=0).astype(jnp.float32), axis=0).astype(h.dtype).reshape(b, s, DIL_OUT)
    gates = jax.nn.sigmoid(gate_logits + b_gate)
    g_mla, g_dil = gates[..., :D_MODEL], gates[..., D_MODEL:]
    merged = g_mla * (o_mla @ w_branch_mla) + g_dil * (o_dil @ w_branch_dil)
    return merged @ w_out


def _trunk(x, layer_params, final_norm):
    (ffn1_norm, ffn1_w_gate, ffn1_w_up, ffn1_w_down, mix_norm, w_in, b_gate, q_a_norm, w_q_up,
     kv_a_norm, w_kv_up, w_branch_mla, w_branch_dil, w_out, ffn2_norm, ffn2_w_gate, ffn2_w_up, ffn2_w_down) = layer_params
    for l in range(DEPTH):
        x = x + 0.5 * _swiglu(_rmsnorm(x, ffn1_norm[l]), ffn1_w_gate[l], ffn1_w_up[l], ffn1_w_down[l])
        x = x + _hybrid_mixer(_rmsnorm(x, mix_norm[l]), w_in[l], b_gate[l], q_a_norm[l], w_q_up[l], kv_a_norm[l],
                              w_kv_up[l], w_branch_mla[l], w_branch_dil[l], w_out[l])
        x = x + 0.5 * _swiglu(_rmsnorm(x, ffn2_norm[l]), ffn2_w_gate[l], ffn2_w_up[l], ffn2_w_down[l])
    return _rmsnorm(x, final_norm)


def setup_inputs(seed: int = 0) -> dict:
    key = jax.random.key(seed)
    ks = jax.random.split(key, 22)

    def dense(k, shape):
        return jax.random.normal(k, shape, jnp.float32) * (shape[-2] ** -0.5)

    def gain(k, shape):
        return 1.0 + 0.02 * jax.random.normal(k, shape, jnp.float32)

    return {
        'x_prompt': jax.random.normal(ks[0], (BATCH, SEQ, D_MODEL), jnp.float32),
        'x_sample': jax.random.normal(ks[1], (DEC_BATCH, DEC_SEQ, D_MODEL), jnp.float32),
        'ffn1_norm': gain(ks[2], (DEPTH, D_MODEL)),
        'ffn1_w_gate': dense(ks[3], (DEPTH, D_MODEL, D_FF)),
        'ffn1_w_up': dense(ks[4], (DEPTH, D_MODEL, D_FF)),
        'ffn1_w_down': dense(ks[5], (DEPTH, D_FF, D_MODEL)),
        'mix_norm': gain(ks[6], (DEPTH, D_MODEL)),
        'w_in': dense(ks[7], (DEPTH, D_MODEL, W_IN_COLS)),
        'b_gate': 0.01 * jax.random.normal(ks[8], (DEPTH, N_BRANCHES * D_MODEL), jnp.float32),
        'q_a_norm': gain(ks[9], (DEPTH, Q_LORA_RANK)),
        'w_q_up': dense(ks[10], (DEPTH, Q_LORA_RANK, MLA_HEADS * (QK_NOPE_DIM + QK_ROPE_DIM))),
        'kv_a_norm': gain(ks[11], (DEPTH, KV_LORA_RANK)),
        'w_kv_up': dense(ks[12], (DEPTH, KV_LORA_RANK, MLA_HEADS * (QK_NOPE_DIM + V_HEAD_DIM))),
        'w_branch_mla': dense(ks[13], (DEPTH, MLA_OUT, D_MODEL)),
        'w_branch_dil': dense(ks[14], (DEPTH, DIL_OUT, D_MODEL)),
        'w_out': dense(ks[15], (DEPTH, D_MODEL, D_MODEL)),
        'ffn2_norm': gain(ks[16], (DEPTH, D_MODEL)),
        'ffn2_w_gate': dense(ks[17], (DEPTH, D_MODEL, D_FF)),
        'ffn2_w_up': dense(ks[18], (DEPTH, D_MODEL, D_FF)),
        'ffn2_w_down': dense(ks[19], (DEPTH, D_FF, D_MODEL)),
        'final_norm': gain(ks[20], (D_MODEL,)),
    }


def reference(x_prompt, x_sample, ffn1_norm, ffn1_w_gate, ffn1_w_up, ffn1_w_down, mix_norm, w_in, b_gate,
              q_a_norm, w_q_up, kv_a_norm, w_kv_up, w_branch_mla, w_branch_dil, w_out,
              ffn2_norm, ffn2_w_gate, ffn2_w_up, ffn2_w_down, final_norm):
    layer_params = (ffn1_norm, ffn1_w_gate, ffn1_w_up, ffn1_w_down, mix_norm, w_in, b_gate, q_a_norm, w_q_up,
                    kv_a_norm, w_kv_up, w_branch_mla, w_branch_dil, w_out, ffn2_norm, ffn2_w_gate, ffn2_w_up, ffn2_w_down)
    y_prompt = _trunk(x_prompt, layer_params, final_norm)
    y_sample = _trunk(x_sample, layer_params, final_norm)
    return (y_prompt, y_sample)
```

```python
import numpy as np
import ml_dtypes
from contextlib import ExitStack
import concourse.bass as bass
import concourse.mybir as mybir
from concourse.bass_utils import run_bass_kernel_spmd

F32 = mybir.dt.float32
BF16 = mybir.dt.bfloat16
AF = mybir.ActivationFunctionType
ALU = mybir.AluOpType

SELF_SYNC = True
D = 1024
KC = 8
FF = 2816
FC = 22
NH = 8
DH = 12
DIL = (1, 4, 16)
SMAX = 4096
EPS = 1e-6


class Dom:
    def __init__(self, sem, unit, name):
        self.sem, self.unit, self.count, self.name = sem, unit, 0, name


class Res:
    __slots__ = ("w", "r", "excl")

    def __init__(self, excl=False):
        self.w = None
        self.r = {}
        self.excl = excl


class Sched:
    ENG = ("pe", "act", "dve", "pool", "sp")

    def __init__(self, nc, stack, n_dma=56):
        self.nc = nc
        self.ops = {k: [] for k in self.ENG}
        self.doms = {}
        for k in ("pe", "act", "dve", "pool"):
            self.doms[k] = Dom(stack.enter_context(nc.semaphore("s_" + k)), 1, k)
        self.waited = {k: {} for k in self.ENG}
        self.pool = [Dom(stack.enter_context(nc.semaphore("d%d" % i)), 16, "d%d" % i) for i in range(n_dma)]
        self.pool_i = 0
        self.n_ins = {k: 0 for k in self.ENG}

    def new_pass(self):
        self.pool_i = 0

    def dma_dom(self):
        d = self.pool[self.pool_i]
        self.pool_i += 1
        return d

    def _need(self, eng, tok, waits, own=None):
        if tok is None:
            return
        dom, val = tok
        if dom is own and dom.unit == 16:
            return
        if dom is self.doms.get(eng):
            if eng == "pe" or not SELF_SYNC:
                return
        if self.waited[eng].get(dom, 0) >= val:
            return
        self.waited[eng][dom] = val
        waits.append((dom.sem, val * dom.unit))

    def dma(self, eng, dom, items):
        final = dom.count + len(items)
        tok = None
        for fn, reads, writes in items:
            tok = self.op(eng, fn, reads, writes, dom=dom, tokval=final)
        return tok

    def op(self, eng, fn, reads=(), writes=(), dom=None, signal=True, tokval=None):
        waits = []
        if any(r.excl for r in reads):
            writes = list(writes) + [r for r in reads if r.excl]
            reads = [r for r in reads if not r.excl]
        d = dom if dom is not None else self.doms[eng]
        for r in reads:
            self._need(eng, r.w, waits, d)
        for w in writes:
            self._need(eng, w.w, waits, d)
            for dd, v in w.r.items():
                self._need(eng, (dd, v), waits, d)
        if signal:
            d.count += 1
        tok = (d, d.count if signal else d.count + 1)
        if tokval is not None:
            tok = (d, tokval)
        for r in reads:
            if r.r.get(d, 0) < tok[1]:
                r.r[d] = tok[1]
        for w in writes:
            w.w = tok
            w.r = {}
        sem, unit = d.sem, d.unit
        self.n_ins[eng] += 1

        def run(e, waits=waits, fn=fn, signal=signal, sem=sem, unit=unit):
            for s, v in waits:
                e.wait_ge(s, v)
            ins = fn(e)
            if signal:
                ins.then_inc(sem, unit)
        self.ops[eng].append(run)
        return tok

    def barrier(self, engs=None):
        alld = list(self.doms.values()) + self.pool
        for eng in (engs or self.ENG):
            waits = []
            for d in alld:
                if d.count > 0:
                    self._need(eng, (d, d.count), waits)

            def run(e, waits=waits):
                for s, v in waits:
                    e.wait_ge(s, v)
            self.ops[eng].append(run)

    def emit(self):
        nc, ops = self.nc, self.ops
        with nc.Block() as block:
            @block.tensor
            def _(e):
                for f in ops["pe"]:
                    f(e)

            @block.scalar
            def _(e):
                for f in ops["act"]:
                    f(e)

            @block.vector
            def _(e):
                for f in ops["dve"]:
                    f(e)

            @block.gpsimd
            def _(e):
                for f in ops["pool"]:
                    f(e)

            @block.sync
            def _(e):
                for f in ops["sp"]:
                    f(e)


class Slot:
    def __init__(self, ap, dom=None, excl=False):
        self.ap, self.res, self.dom = ap, Res(excl), dom


class Pass:
    _uid = [0]

    def __init__(self, G, name):
        Pass._uid[0] += 1
        self.G, self.name = G, "%s_%d" % (name, Pass._uid[0])
        self.st = ExitStack()
        self.n = 0

    def __enter__(self):
        self.st.__enter__()
        self.G.S.barrier()
        self.G.S.new_pass()
        self.banks = []
        return self

    def __exit__(self, *a):
        return self.st.__exit__(*a)

    def sb(self, shape, dt, dma=False):
        self.n += 1
        t = self.st.enter_context(self.G.nc.sbuf_tensor("%s_%d" % (self.name, self.n), list(shape), dt))
        return Slot(t, self.G.S.dma_dom() if dma else None)

    def ring(self, n, shape, dt, dma=False):
        return [self.sb(shape, dt, dma) for _ in range(n)]

    def bank(self, dt=F32):
        self.n += 1
        shape = [128, 512] if dt == F32 else [128, 1024]
        t = self.st.enter_context(self.G.nc.psum_tensor("%s_p%d" % (self.name, self.n), shape, dt))
        return Slot(t, excl=True)


class G:
    pass


def load_w_cast(g, slot, src, ncols, split=1):
    S = g.S
    nch = src.shape[1]
    step = ncols // split
    items = []
    for c in range(nch):
        for h in range(split):
            items.append((lambda e, c=c, h=h: e.dma_start(out=slot.ap[:, c, h * step:(h + 1) * step],
                                                          in_=src[:, c, h * step:(h + 1) * step]),
                          [], [slot.res]))
    S.dma("pool", slot.dom, items)


def load_simple(g, slot, src, eng="sp", dst=None):
    d = slot.ap[:] if dst is None else dst
    g.S.dma(eng, slot.dom, [(lambda e: e.dma_start(out=d, in_=src), [], [slot.res])])


def rms_stats(g, xs, ntile, width, ss, rt, rstd, junk):
    S = g.S
    for j in range(ntile):
        ap, res = xs(j)
        S.op("act", lambda e, ap=ap, j=j: e.activation(out=junk.ap[:, 0:width], in_=ap, func=AF.Square,
                                                        accum_out=ss.ap[:, j:j + 1]),
             [res], [junk.res, ss.res])
    S.op("act", lambda e: e.activation(out=rt.ap[:, 0:ntile], in_=ss.ap[:, 0:ntile], func=AF.Sqrt,
                                       scale=1.0 / width, bias=g.eps.ap[:, 0:1]),
         [ss.res, g.eps.res], [rt.res])
    S.op("dve", lambda e: e.reciprocal(out=rstd.ap[:, 0:ntile], in_=rt.ap[:, 0:ntile]), [rt.res], [rstd.res])


def transpose_tile(g, src, nchunk, tpb, dst_ap, dst_res, cp_eng):
    S = g.S
    for c in range(nchunk):
        S.op("pe", lambda e, c=c: e.transpose(out=tpb.ap[:, c * 128:(c + 1) * 128],
                                              in_=src.ap[:, c * 128:(c + 1) * 128], identity=g.ident.ap[:]),
             [src.res, g.ident.res], [tpb.res], signal=(c == nchunk - 1))
    view = tpb.ap[:, 0:nchunk * 128].rearrange("p (c t) -> p c t", c=nchunk)
    if cp_eng == "act":
        S.op("act", lambda e: e.activation(out=dst_ap, in_=view, func=AF.Copy), [tpb.res], [dst_res])
    else:
        S.op("dve", lambda e: e.tensor_copy(out=dst_ap, in_=view), [tpb.res], [dst_res])


def norm_block(g, P, xb, ntile, gb, xn_ring, tpbs, xnT, stats, blk_parity):
    S = g.S
    ss, rt, rstd, junk = stats
    rms_stats(g, lambda j: (xb.ap[:, j, :], xb.res), ntile, D, ss, rt, rstd, junk)
    for j in range(ntile):
        xn = xn_ring[j % len(xn_ring)]
        S.op("dve", lambda e, j=j, xn=xn: e.scalar_tensor_tensor(out=xn.ap[:], in0=xb.ap[:, j, :],
                                                                 scalar=rstd.ap[:, j:j + 1], in1=gb.ap[:],
                                                                 op0=ALU.mult, op1=ALU.mult),
             [xb.res, rstd.res, gb.res], [xn.res])
        tpb = tpbs[j % len(tpbs)]
        transpose_tile(g, xn, KC, tpb, xnT.ap[:, :, j * 128:(j + 1) * 128], xnT.res, "act")


def ffn_pass(g, l, which, src, dst, T):
    S = g.S
    TB, NT = 256, 2
    wg_d, wu_d, wd_d, gn_d = (g.d["ffn%d_w_gate" % which], g.d["ffn%d_w_up" % which],
                              g.d["ffn%d_w_down" % which], g.d["ffn%d_norm" % which])
    with Pass(g, "f%d%d" % (l, which)) as P:
        wg = P.sb([128, KC, FF], BF16, dma=True)
        wu = P.sb([128, KC, FF], BF16, dma=True)
        wd = P.sb([128, FC, D], BF16, dma=True)
        gb = P.sb([128, D], F32, dma=True)
        load_w_cast(g, wg, wg_d[l].rearrange("(c p) f -> p c f", p=128), FF, split=2)
        load_w_cast(g, wu, wu_d[l].rearrange("(c p) f -> p c f", p=128), FF, split=2)
        load_w_cast(g, wd, wd_d[l].rearrange("(c p) f -> p c f", p=128), D, split=1)
        load_simple(g, gb, gn_d[l].partition_broadcast(128))
        xin = P.ring(2, [128, NT, D], F32, dma=True)
        xst = [P.G.S.dma_dom() for _ in range(2)]
        xn_ring = P.ring(2, [128, D], BF16)
        xnT_ring = P.ring(2, [128, KC, TB], BF16)
        hT = P.sb([128, FC, TB], BF16)
        hres = [Res() for _ in range(FC)]
        sg_ring = P.ring(2, [128, TB], F32)
        junk = P.sb([128, D], BF16)
        stats = [(P.sb([128, 4], F32), P.sb([128, 4], F32), P.sb([128, 4], F32), junk) for _ in range(2)]
        tpbs = [P.bank(BF16), P.bank(BF16)]
        pgs = [P.bank(), P.bank()]
        pus = [P.bank(), P.bank()]
        pys = [P.bank(), P.bank()]
        srcv = src.rearrange("(n j p) d -> n p j d", j=NT, p=128)
        dstv = dst.rearrange("(n j p) d -> n p j d", j=NT, p=128)
        nblk = T // TB
        for b in range(nblk):
            xb = xin[b % 2]
            S.dma("sp", xb.dom, [(lambda e, xb=xb, b=b: e.dma_start(out=xb.ap[:], in_=srcv[b]), [], [xb.res])])
            xnT = xnT_ring[b % 2]
            norm_block(g, P, xb, NT, gb, xn_ring, tpbs, xnT, stats[b % 2], b % 2)
            for f in range(FC):
                pg, pu, sg = pgs[f % 2], pus[f % 2], sg_ring[f % 2]
                for k in range(KC):
                    S.op("pe", lambda e, k=k, f=f, pg=pg, xnT=xnT: e.matmul(
                        pg.ap[:, 0:TB], lhsT=wg.ap[:, k, f * 128:(f + 1) * 128], rhs=xnT.ap[:, k, :],
                        start=(k == 0), stop=(k == KC - 1)), [wg.res, xnT.res], [pg.res], signal=(k == KC - 1))
                for k in range(KC):
                    S.op("pe", lambda e, k=k, f=f, pu=pu, xnT=xnT: e.matmul(
                        pu.ap[:, 0:TB], lhsT=wu.ap[:, k, f * 128:(f + 1) * 128], rhs=xnT.ap[:, k, :],
                        start=(k == 0), stop=(k == KC - 1)), [wu.res, xnT.res], [pu.res], signal=(k == KC - 1))
                S.op("act", lambda e, pg=pg, sg=sg: e.activation(out=sg.ap[:], in_=pg.ap[:, 0:TB], func=AF.Silu),
                     [pg.res], [sg.res])
                S.op("dve", lambda e, pu=pu, sg=sg, f=f: e.tensor_tensor(out=hT.ap[:, f, :], in0=pu.ap[:, 0:TB],
                                                                          in1=sg.ap[:], op=ALU.mult),
                     [pu.res, sg.res], [hres[f]])
            for j in range(NT):
                for hf in range(2):
                    py = pys[(j * 2 + hf) % 2]
                    for f in range(FC):
                        S.op("pe", lambda e, f=f, j=j, hf=hf, py=py: e.matmul(
                            py.ap[:], lhsT=hT.ap[:, f, j * 128:(j + 1) * 128], rhs=wd.ap[:, f, hf * 512:(hf + 1) * 512],
                            start=(f == 0), stop=(f == FC - 1)), [hres[f], wd.res], [py.res], signal=(f == FC - 1))
                    S.op("dve", lambda e, j=j, hf=hf, py=py, xb=xb: e.scalar_tensor_tensor(
                        out=xb.ap[:, j, hf * 512:(hf + 1) * 512], in0=py.ap[:], scalar=0.5,
                        in1=xb.ap[:, j, hf * 512:(hf + 1) * 512], op0=ALU.mult, op1=ALU.add),
                         [py.res], [xb.res])
            S.dma("sp", xst[b % 2], [(lambda e, xb=xb, b=b: e.dma_start(out=dstv[b], in_=xb.ap[:]), [xb.res], [])])


def final_pass(g, src, dst, T):
    S = g.S
    TB, NT = 256, 2
    with Pass(g, "fin") as P:
        gb = P.sb([128, D], F32, dma=True)
        load_simple(g, gb, g.d["final_norm"].partition_broadcast(128))
        xin = P.ring(3, [128, NT, D], F32, dma=True)
        xo = P.ring(3, [128, NT, D], F32, dma=True)
        junk = P.sb([128, D], BF16)
        stats = [(P.sb([128, 4], F32), P.sb([128, 4], F32), P.sb([128, 4], F32), junk) for _ in range(3)]
        srcv = src.rearrange("(n j p) d -> n p j d", j=NT, p=128)
        dstv = dst.rearrange("(n j p) d -> n p j d", j=NT, p=128)
        toks = []
        for b in range(T // TB):
            xb, ob = xin[b % 3], xo[b % 3]
            ss, rt, rstd, _ = stats[b % 3]
            S.dma("sp", xb.dom, [(lambda e, xb=xb, b=b: e.dma_start(out=xb.ap[:], in_=srcv[b]), [], [xb.res])])
            rms_stats(g, lambda j: (xb.ap[:, j, :], xb.res), NT, D, ss, rt, rstd, junk)
            for j in range(NT):
                S.op("dve", lambda e, j=j, xb=xb, ob=ob, rstd=rstd: e.scalar_tensor_tensor(
                    out=ob.ap[:, j, :], in0=xb.ap[:, j, :], scalar=rstd.ap[:, j:j + 1], in1=gb.ap[:],
                    op0=ALU.mult, op1=ALU.mult), [xb.res, rstd.res, gb.res], [ob.res])
            S.dma("sp", ob.dom, [(lambda e, ob=ob, b=b: e.dma_start(out=dstv[b], in_=ob.ap[:]), [ob.res], [])])


def mixer_p1(g, l, t0, Sq):
    S = g.S
    TB, NT = 256, 2
    d = g.d
    with Pass(g, "p1") as P:
        wzA = P.sb([128, KC, 384], BF16, dma=True)
        wkp = P.sb([128, KC, 192], BF16, dma=True)
        wdq = P.sb([128, KC, 1536], BF16, dma=True)
        wdk = P.sb([128, KC, 1536], BF16, dma=True)
        wdv = P.sb([128, KC, 1536], BF16, dma=True)
        wgt = P.sb([128, KC, 2048], BF16, dma=True)
        wq = P.sb([128, 2, NH * 192], BF16, dma=True)
        wkn = P.sb([128, 1, NH * 64], BF16, dma=True)
        wv = P.sb([128, 1, 512], BF16, dma=True)
        cv = lambda a: a[l].rearrange("(c p) f -> p c f", p=128)
        load_w_cast(g, wzA, cv(d["w_zA"]), 384)
        load_w_cast(g, wkp, cv(d["w_kpe2"]), 192)
        load_w_cast(g, wdq, cv(d["w_dq"]), 1536)
        load_w_cast(g, wdk, cv(d["w_dk"]), 1536)
        load_w_cast(g, wdv, cv(d["w_dv"]), 1536)
        load_w_cast(g, wgt, cv(d["w_gt"]), 2048)
        load_w_cast(g, wq, cv(d["w_q2"]), NH * 192)
        load_w_cast(g, wkn, cv(d["w_kn"]), NH * 64)
        load_w_cast(g, wv, cv(d["w_v"]), 512)
        gb = P.sb([128, D], F32, dma=True)
        gq = P.sb([128, 256], F32, dma=True)
        gk = P.sb([128, 128], F32, dma=True)
        bg = P.sb([128, 16], F32, dma=True)
        load_simple(g, gb, d["mix_norm"][l].partition_broadcast(128))
        load_simple(g, gq, d["q_a_norm"][l].partition_broadcast(128))
        load_simple(g, gk, d["kv_a_norm"][l].partition_broadcast(128))
        load_simple(g, bg, d["bgate2"][l])
        xin = P.ring(2, [128, NT, D], F32, dma=True)
        ctab = P.ring(2, [96, 2, TB], F32, dma=True)
        xn_ring = P.ring(2, [128, D], BF16)
        xnT_ring = P.ring(2, [128, KC, TB], BF16)
        junk = P.sb([128, D], BF16)
        stats = [(P.sb([128, 4], F32), P.sb([128, 4], F32), P.sb([128, 4], F32), junk) for _ in range(2)]
        st2 = [(P.sb([128, 4], F32), P.sb([128, 4], F32), P.sb([128, 4], F32), junk) for _ in range(2)]
        cn_ring = P.ring(2, [128, 384], BF16)
        cT = P.sb([128, 3, TB], BF16)
        kpeT = P.sb([96, TB], BF16)
        rt1 = P.ring(2, [96, TB], F32)
        rt2 = P.ring(2, [96, TB], F32)
        QT = P.ring(2, [96, NH, TB], BF16, dma=True)
        KT = P.ring(2, [96, NH, TB], BF16, dma=True)
        vst = P.ring(2, [128, NH, 128], BF16, dma=True)
        pair = P.ring(4, [128, 2, TB], BF16, dma=True)
        dvst = P.ring(4, [128, 512], BF16, dma=True)
        for v in vst:
            S.op("pool", lambda e, v=v: e.memset(v.ap[:, :, 64:128], 1.0), [], [v.res])
        tpbs = [P.bank(BF16), P.bank(BF16)]
        pbk = [P.bank() for _ in range(6)]
        nb = [0]

        def nextbank():
            nb[0] += 1
            return pbk[nb[0] % 6]
        pi = [0]
        di = [0]
        xv = g.xres.rearrange("(n j p) d -> n p j d", j=NT, p=128)
        qTv = g.qT.rearrange("h r s -> r h s")
        kTv = g.kT.rearrange("h r s -> r h s")
        vmv = g.vm.rearrange("(n p) (h c) -> n p h c", p=128, c=128)
        dvv = g.dv.rearrange("(n p) c -> n p c", p=128)
        for b in range(Sq // TB):
            s0 = b * TB
            gblk = (t0 + s0) // TB
            xb = xin[b % 2]
            ct = ctab[b % 2]
            S.dma("sp", xb.dom, [(lambda e, xb=xb, gblk=gblk: e.dma_start(out=xb.ap[:], in_=xv[gblk]), [], [xb.res])])
            S.dma("sp", ct.dom, [(lambda e, ct=ct, s0=s0: e.dma_start(out=ct.ap[64:96], in_=d["ropetab"][:, :, s0:s0 + TB]),
                                  [], [ct.res])])
            xnT = xnT_ring[b % 2]
            norm_block(g, P, xb, NT, gb, xn_ring, tpbs, xnT, stats[b % 2], b % 2)
            ss2, rtt2, rstd2, _ = st2[b % 2]
            for j in range(NT):
                pz = nextbank()
                cn = cn_ring[j % 2]
                for k in range(KC):
                    S.op("pe", lambda e, k=k, j=j, pz=pz, xnT=xnT: e.matmul(
                        pz.ap[:, 0:384], lhsT=xnT.ap[:, k, j * 128:(j + 1) * 128], rhs=wzA.ap[:, k, :],
                        start=(k == 0), stop=(k == KC - 1)), [xnT.res, wzA.res], [pz.res], signal=(k == KC - 1))
                S.op("act", lambda e, pz=pz, ss2=ss2: e.activation(out=junk.ap[:, 0:256], in_=pz.ap[:, 0:256], func=AF.Square,
                                                                  accum_out=ss2.ap[:, 0:1]), [pz.res], [junk.res, ss2.res])
                S.op("act", lambda e, pz=pz, ss2=ss2: e.activation(out=junk.ap[:, 0:128], in_=pz.ap[:, 256:384], func=AF.Square,
                                                                  accum_out=ss2.ap[:, 1:2]), [pz.res], [junk.res, ss2.res])
                S.op("act", lambda e, ss2=ss2, rtt2=rtt2: e.activation(out=rtt2.ap[:, 0:1], in_=ss2.ap[:, 0:1], func=AF.Sqrt,
                                                                      scale=1.0 / 256, bias=g.eps.ap[:, 0:1]),
                     [ss2.res, g.eps.res], [rtt2.res])
                S.op("act", lambda e, ss2=ss2, rtt2=rtt2: e.activation(out=rtt2.ap[:, 1:2], in_=ss2.ap[:, 1:2], func=AF.Sqrt,
                                                                      scale=1.0 / 128, bias=g.eps.ap[:, 0:1]),
                     [ss2.res, g.eps.res], [rtt2.res])
                S.op("dve", lambda e, rtt2=rtt2, rstd2=rstd2: e.reciprocal(out=rstd2.ap[:, 0:2], in_=rtt2.ap[:, 0:2]),
                     [rtt2.res], [rstd2.res])
                S.op("dve", lambda e, pz=pz, cn=cn, rstd2=rstd2: e.scalar_tensor_tensor(
                    out=cn.ap[:, 0:256], in0=pz.ap[:, 0:256], scalar=rstd2.ap[:, 0:1], in1=gq.ap[:],
                    op0=ALU.mult, op1=ALU.mult), [pz.res, rstd2.res, gq.res], [cn.res])
                S.op("dve", lambda e, pz=pz, cn=cn, rstd2=rstd2: e.scalar_tensor_tensor(
                    out=cn.ap[:, 256:384], in0=pz.ap[:, 256:384], scalar=rstd2.ap[:, 1:2], in1=gk.ap[:],
                    op0=ALU.mult, op1=ALU.mult), [pz.res, rstd2.res, gk.res], [cn.res])
                transpose_tile(g, cn, 3, tpbs[j % 2], cT.ap[:, :, j * 128:(j + 1) * 128], cT.res, "dve")
            pk = nextbank()
            for half in range(2):
                for k in range(KC):
                    S.op("pe", lambda e, k=k, half=half, pk=pk, xnT=xnT: e.matmul(
                        pk.ap[0:96, half * TB:(half + 1) * TB], lhsT=wkp.ap[:, k, half * 96:(half + 1) * 96],
                        rhs=xnT.ap[:, k, :], start=(k == 0), stop=(k == KC - 1)),
                         [xnT.res, wkp.res], [pk.res], signal=(k == KC - 1))
            a1, a2 = rt1[0], rt2[0]
            S.op("dve", lambda e, pk=pk, a1=a1, ct=ct: e.tensor_tensor(out=a1.ap[64:96], in0=pk.ap[64:96, 0:TB], in1=ct.ap[64:96, 0, :],
                                                                      op=ALU.mult), [pk.res, ct.res], [a1.res])
            S.op("dve", lambda e, pk=pk, a2=a2, ct=ct: e.tensor_tensor(out=a2.ap[64:96], in0=pk.ap[64:96, TB:2 * TB], in1=ct.ap[64:96, 1, :],
                                                                      op=ALU.mult), [pk.res, ct.res], [a2.res])
            S.op("pool", lambda e, a1=a1, a2=a2: e.tensor_tensor(out=kpeT.ap[64:96], in0=a1.ap[64:96], in1=a2.ap[64:96], op=ALU.add),
                 [a1.res, a2.res], [kpeT.res])
            qt, kt = QT[b % 2], KT[b % 2]
            for h in range(NH):
                pq = nextbank()
                for c in range(2):
                    S.op("pe", lambda e, c=c, h=h, pq=pq: e.matmul(
                        pq.ap[0:96, 0:TB], lhsT=wq.ap[:, c, h * 192:h * 192 + 96], rhs=cT.ap[:, c, :],
                        start=(c == 0), stop=(c == 1)), [wq.res, cT.res], [pq.res], signal=False)
                for c in range(2):
                    S.op("pe", lambda e, c=c, h=h, pq=pq: e.matmul(
                        pq.ap[0:96, TB:2 * TB], lhsT=wq.ap[:, c, h * 192 + 96:h * 192 + 192], rhs=cT.ap[:, c, :],
                        start=(c == 0), stop=(c == 1)), [wq.res, cT.res], [pq.res], signal=(c == 1))
                a1, a2 = rt1[h % 2], rt2[h % 2]
                S.op("act", lambda e, pq=pq, qt=qt, h=h: e.activation(out=qt.ap[0:64, h, :], in_=pq.ap[0:64, 0:TB], func=AF.Copy),
                     [pq.res], [qt.res])
                S.op("dve", lambda e, pq=pq, a1=a1, ct=ct: e.tensor_tensor(out=a1.ap[64:96], in0=pq.ap[64:96, 0:TB], in1=ct.ap[64:96, 0, :],
                                                                          op=ALU.mult), [pq.res, ct.res], [a1.res])
                S.op("dve", lambda e, pq=pq, a2=a2, ct=ct: e.tensor_tensor(out=a2.ap[64:96], in0=pq.ap[64:96, TB:2 * TB], in1=ct.ap[64:96, 1, :],
                                                                          op=ALU.mult), [pq.res, ct.res], [a2.res])
                S.op("pool", lambda e, a1=a1, a2=a2, qt=qt, h=h: e.tensor_tensor(out=qt.ap[64:96, h, :], in0=a1.ap[64:96], in1=a2.ap[64:96],
                                                                                op=ALU.add), [a1.res, a2.res], [qt.res])
                if h % 2 == 0:
                    pkn = nextbank()
                S.op("pe", lambda e, h=h, pkn=pkn: e.matmul(
                    pkn.ap[0:64, (h % 2) * TB:(h % 2 + 1) * TB], lhsT=wkn.ap[:, 0, h * 64:(h + 1) * 64], rhs=cT.ap[:, 2, :],
                    start=True, stop=True), [wkn.res, cT.res], [pkn.res])
                if h % 2 == 1:
                    S.op("act", lambda e, pkn=pkn, kt=kt, h=h: e.activation(
                        out=kt.ap[0:64, h - 1:h + 1, :], in_=pkn.ap[0:64, :].rearrange("p (a t) -> p a t", a=2),
                        func=AF.Copy), [pkn.res], [kt.res])
                S.op("pool", lambda e, kt=kt, h=h: e.tensor_copy(out=kt.ap[64:96, h, :], in_=kpeT.ap[64:96]), [kpeT.res], [kt.res])
            S.dma("sp", qt.dom, [(lambda e, qt=qt, s0=s0: e.dma_start(out=qTv[:, :, s0:s0 + TB], in_=qt.ap[:]), [qt.res], [])])
            S.dma("sp", kt.dom, [(lambda e, kt=kt, s0=s0: e.dma_start(out=kTv[:, :, s0:s0 + TB], in_=kt.ap[:]), [kt.res], [])])
            for j in range(NT):
                pv = nextbank()
                vs = vst[j % 2]
                S.op("pe", lambda e, j=j, pv=pv: e.matmul(pv.ap[:], lhsT=cT.ap[:, 2, j * 128:(j + 1) * 128], rhs=wv.ap[:, 0, :],
                                                         start=True, stop=True), [cT.res, wv.res], [pv.res])
                S.op("act", lambda e, pv=pv, vs=vs: e.activation(out=vs.ap[:, :, 0:64],
                                                                in_=pv.ap[:].rearrange("p (h c) -> p h c", c=64), func=AF.Copy),
                     [pv.res], [vs.res])
                S.dma("sp", vs.dom, [(lambda e, vs=vs, j=j, s0=s0: e.dma_start(out=vmv[(s0 // 128) + j], in_=vs.ap[:]), [vs.res], [])])
            for which, wsb, dstT in ((0, wdq, g.dqT), (1, wdk, g.dkT)):
                dview = dstT.rearrange("h p s -> p h s")
                for hp in range(DH // 2):
                    pb = nextbank()
                    for a in range(2):
                        hc = hp * 2 + a
                        for k in range(KC):
                            S.op("pe", lambda e, k=k, hc=hc, a=a, pb=pb, wsb=wsb, xnT=xnT: e.matmul(
                                pb.ap[:, a * TB:(a + 1) * TB], lhsT=wsb.ap[:, k, hc * 128:(hc + 1) * 128], rhs=xnT.ap[:, k, :],
                                start=(k == 0), stop=(k == KC - 1)), [wsb.res, xnT.res], [pb.res],
                                 signal=(k == KC - 1 and a == 1))
                    pr = pair[pi[0] % 4]
                    pi[0] += 1
                    if hp % 2 == 0:
                        S.op("act", lambda e, pb=pb, pr=pr: e.activation(out=pr.ap[:], in_=pb.ap[:].rearrange("p (a t) -> p a t", a=2),
                                                                        func=AF.Copy), [pb.res], [pr.res])
                    else:
                        S.op("dve", lambda e, pb=pb, pr=pr: e.tensor_copy(out=pr.ap[:], in_=pb.ap[:].rearrange("p (a t) -> p a t", a=2)),
                             [pb.res], [pr.res])
                    S.dma("sp", pr.dom, [(lambda e, pr=pr, hp=hp, dview=dview, s0=s0: e.dma_start(
                        out=dview[:, 2 * hp:2 * hp + 2, s0:s0 + TB], in_=pr.ap[:]), [pr.res], [])])
            for j in range(NT):
                for m in range(3):
                    pb = nextbank()
                    for k in range(KC):
                        S.op("pe", lambda e, k=k, j=j, m=m, pb=pb, xnT=xnT: e.matmul(
                            pb.ap[:], lhsT=xnT.ap[:, k, j * 128:(j + 1) * 128], rhs=wdv.ap[:, k, m * 512:(m + 1) * 512],
                            start=(k == 0), stop=(k == KC - 1)), [wdv.res, xnT.res], [pb.res], signal=(k == KC - 1))
                    dvs = dvst[di[0] % 4]
                    di[0] += 1
                    if m % 2 == 0:
                        S.op("dve", lambda e, pb=pb, dvs=dvs: e.tensor_copy(out=dvs.ap[:], in_=pb.ap[:]), [pb.res], [dvs.res])
                    else:
                        S.op("act", lambda e, pb=pb, dvs=dvs: e.activation(out=dvs.ap[:], in_=pb.ap[:], func=AF.Copy), [pb.res], [dvs.res])
                    S.dma("sp", dvs.dom, [(lambda e, dvs=dvs, j=j, m=m, s0=s0: e.dma_start(
                        out=dvv[(s0 // 128) + j][:, m * 512:(m + 1) * 512], in_=dvs.ap[:]), [dvs.res], [])])
            gview = g.gT.rearrange("c p s -> p c s")
            for cp in range(8):
                pb = nextbank()
                for a in range(2):
                    c = cp * 2 + a
                    for k in range(KC):
                        S.op("pe", lambda e, k=k, c=c, a=a, pb=pb, xnT=xnT: e.matmul(
                            pb.ap[:, a * TB:(a + 1) * TB], lhsT=wgt.ap[:, k, c * 128:(c + 1) * 128], rhs=xnT.ap[:, k, :],
                            start=(k == 0), stop=(k == KC - 1)), [wgt.res, xnT.res], [pb.res],
                             signal=(k == KC - 1 and a == 1))
                pr = pair[pi[0] % 4]
                pi[0] += 1
                for a in range(2):
                    c = cp * 2 + a
                    S.op("act", lambda e, pb=pb, pr=pr, a=a, c=c: e.activation(
                        out=pr.ap[:, a, :], in_=pb.ap[:, a * TB:(a + 1) * TB], func=AF.Sigmoid, bias=bg.ap[:, c:c + 1]),
                         [pb.res, bg.res], [pr.res])
                S.dma("sp", pr.dom, [(lambda e, pr=pr, cp=cp, s0=s0: e.dma_start(
                    out=gview[:, 2 * cp:2 * cp + 2, s0:s0 + TB], in_=pr.ap[:]), [pr.res], [])])


def mixer_p2(g, Sq):
    S = g.S
    QB = 512
    nkb = Sq // 128
    scale = 96.0 ** -0.5
    with Pass(g, "p2") as P:
        KT = P.sb([96, NH, Sq], BF16, dma=True)
        VA = P.sb([128, nkb, NH * 128], BF16, dma=True)
        load_simple(g, KT, g.kT.rearrange("h r s -> r h s")[:, :, 0:Sq])
        vv = g.vm.rearrange("(n p) c -> p n c", p=128)
        S.dma("sp", VA.dom, [(lambda e, a=a: e.dma_start(out=VA.ap[:, a * 8:(a + 1) * 8, :], in_=vv[:, a * 8:(a + 1) * 8, :]),
                              [], [VA.res]) for a in range(nkb // 8)])
        QT = P.ring(2, [96, NH, QB], BF16, dma=True)
        PT = P.ring(4, [128, QB], BF16)
        oT = P.ring(2, [64, NH, QB], BF16, dma=True)
        rl = P.ring(2, [64, QB], F32)
        pss = [P.bank() for _ in range(3)]
        pos = [P.bank() for _ in range(2)]
        qTv = g.qT.rearrange("h r s -> r h s")
        omv = g.omT.rearrange("h r s -> r h s")
        LAG = 2
        it = 0
        for qb in range(Sq // QB):
            q0 = qb * QB
            qt = QT[qb % 2]
            ot = oT[qb % 2]
            S.dma("sp", qt.dom, [(lambda e, qt=qt, q0=q0: e.dma_start(out=qt.ap[:], in_=qTv[:, :, q0:q0 + QB]), [], [qt.res])])
            for h in range(NH):
                po = pos[h % 2]
                for step in range(nkb + LAG):
                    if step < nkb:
                        kb = step
                        ps, pt = pss[(it + kb) % 3], PT[(it + kb) % 4]
                        S.op("pe", lambda e, ps=ps, kb=kb, h=h, qt=qt: e.matmul(
                            ps.ap[:], lhsT=KT.ap[:, h, kb * 128:(kb + 1) * 128], rhs=qt.ap[:, h, :], start=True, stop=True),
                             [KT.res, qt.res], [ps.res])
                        S.op("act", lambda e, ps=ps, pt=pt: e.activation(out=pt.ap[:], in_=ps.ap[:], func=AF.Exp, scale=scale),
                             [ps.res], [pt.res])
                    if step >= LAG:
                        kb = step - LAG
                        pt = PT[(it + kb) % 4]
                        S.op("pe", lambda e, po=po, kb=kb, h=h, pt=pt: e.matmul(
                            po.ap[:], lhsT=VA.ap[:, kb, h * 128:(h + 1) * 128], rhs=pt.ap[:], start=(kb == 0), stop=(kb == nkb - 1)),
                             [VA.res, pt.res], [po.res], signal=(kb == nkb - 1))
                it += nkb
                r = rl[h % 2]
                S.op("dve", lambda e, po=po, r=r: e.reciprocal(out=r.ap[:], in_=po.ap[64:128, :]), [po.res], [r.res])
                S.op("dve", lambda e, po=po, r=r, ot=ot, h=h: e.tensor_tensor(out=ot.ap[:, h, :], in0=po.ap[0:64, :], in1=r.ap[:],
                                                                              op=ALU.mult), [po.res, r.res], [ot.res])
            S.dma("sp", ot.dom, [(lambda e, ot=ot, q0=q0: e.dma_start(out=omv[:, :, q0:q0 + QB], in_=ot.ap[:]), [ot.res], [])])


def mixer_p3(g, Sq):
    S = g.S
    scale = 128.0 ** -0.5
    with Pass(g, "p3") as P:
        EF = P.sb([128, DH, 256], F32, dma=True)
        E1 = P.sb([64, DH, 128], F32, dma=True)
        E2 = P.sb([64, DH, 128], F32, dma=True)
        ones = P.sb([128, 128], BF16, dma=True)
        load_simple(g, EF, g.d["e_full"])
        load_simple(g, E1, g.d["e_first"])
        load_simple(g, E2, g.d["e_last"])
        load_simple(g, ones, g.d["ones"])
        QTr = P.ring(2, [128, Sq], BF16, dma=True)
        KTr = P.ring(2, [128, Sq], BF16, dma=True)
        Vr = P.ring(4, [128, 128], BF16, dma=True)
        acc = P.sb([128, 2, Sq], F32)
        od = P.ring(2, [128, Sq], BF16, dma=True)
        rcp = P.sb([128, Sq], F32)
        ex = P.ring(3, [128, 256], F32)
        PT = P.ring(4, [128, 256], BF16)
        pss = [P.bank() for _ in range(3)]
        pos = [P.bank() for _ in range(2)]
        pls = [P.bank() for _ in range(2)]
        hi = 0
        vi = 0
        ci = 0
        qi = 0
        for slot in range(4):
            for gi, dl in enumerate(DIL):
                head = gi * 4 + slot
                qt, kt = QTr[hi % 2], KTr[hi % 2]
                hi += 1
                S.dma("sp", qt.dom, [(lambda e, qt=qt, head=head: e.dma_start(out=qt.ap[:], in_=g.dqT[head][:, 0:Sq]), [], [qt.res])])
                S.dma("sp", kt.dom, [(lambda e, kt=kt, head=head: e.dma_start(out=kt.ap[:], in_=g.dkT[head][:, 0:Sq]), [], [kt.res])])
                seg = Sq // dl
                nqb = seg // 128
                for r in range(dl):
                    prev = None
                    for c in range(nqb + 1):
                        first, last = (c == 0), (c == nqb)
                        nk = 64 if (first or last) else 128
                        p0 = 0 if first else 128 * c - 64
                        qlo = 128 * (c - 1) if not first else 0
                        nq = 128 if (first or last) else 256
                        v = Vr[vi % 4]
                        vi += 1
                        row0 = p0 * dl + r
                        vsrc = g.dv[row0:row0 + (nk - 1) * dl + 1:dl, head * 128:(head + 1) * 128]
                        S.dma("sp", v.dom, [(lambda e, v=v, vsrc=vsrc, nk=nk: e.dma_start(out=v.ap[0:nk, :], in_=vsrc), [], [v.res])])
                        ps, xx, pt = pss[ci % 3], ex[ci % 3], PT[ci % 4]
                        ci += 1
                        kcols = kt.ap[:, p0 * dl + r:(p0 + nk - 1) * dl + r + 1:dl]
                        qcols = qt.ap[:, qlo * dl + r:(qlo + nq - 1) * dl + r + 1:dl]
                        S.op("pe", lambda e, ps=ps, kcols=kcols, qcols=qcols, nk=nk, nq=nq: e.matmul(
                            ps.ap[0:nk, 0:nq], lhsT=kcols, rhs=qcols, start=True, stop=True), [kt.res, qt.res], [ps.res])
                        S.op("act", lambda e, ps=ps, xx=xx, nk=nk, nq=nq: e.activation(out=xx.ap[0:nk, 0:nq], in_=ps.ap[0:nk, 0:nq],
                                                                                       func=AF.Exp, scale=scale), [ps.res], [xx.res])
                        if first:
                            em = E1.ap[:, head, :]
                        elif last:
                            em = E2.ap[:, head, :]
                        else:
                            em = EF.ap[:, head, :]
                        S.op("pool", lambda e, xx=xx, pt=pt, em=em, nk=nk, nq=nq: e.tensor_tensor(
                            out=pt.ap[0:nk, 0:nq], in0=xx.ap[0:nk, 0:nq], in1=em, op=ALU.mult), [xx.res, EF.res, E1.res, E2.res], [pt.res])
                        if prev is not None:
                            ppt, pv, pnk, pfirst = prev
                            q = c - 1
                            po, pl = pos[qi % 2], pls[qi % 2]
                            qi += 1
                            acol = 0 if pfirst else 128
                            S.op("pe", lambda e, po=po, pv=pv, ppt=ppt, pnk=pnk, acol=acol: e.matmul(
                                po.ap[:, 0:128], lhsT=pv.ap[0:pnk, :], rhs=ppt.ap[0:pnk, acol:acol + 128], start=True, stop=False),
                                 [pv.res, ppt.res], [po.res], signal=False)
                            S.op("pe", lambda e, po=po, v=v, pt=pt, nk=nk: e.matmul(
                                po.ap[:, 0:128], lhsT=v.ap[0:nk, :], rhs=pt.ap[0:nk, 0:128], start=False, stop=True),
                                 [v.res, pt.res], [po.res])
                            S.op("pe", lambda e, pl=pl, ppt=ppt, pnk=pnk, acol=acol: e.matmul(
                                pl.ap[:, 0:128], lhsT=ones.ap[0:pnk, :], rhs=ppt.ap[0:pnk, acol:acol + 128], start=True, stop=False),
                                 [ones.res, ppt.res], [pl.res], signal=False)
                            S.op("pe", lambda e, pl=pl, pt=pt, nk=nk: e.matmul(
                                pl.ap[:, 0:128], lhsT=ones.ap[0:nk, :], rhs=pt.ap[0:nk, 0:128], start=False, stop=True),
                                 [ones.res, pt.res], [pl.res])
                            lo = q * 128 * dl + r
                            hi_ = (q * 128 + 127) * dl + r + 1
                            for a, pp in ((0, po), (1, pl)):
                                dst = acc.ap[:, a, lo:hi_:dl]
                                if gi == 0:
                                    S.op("dve", lambda e, dst=dst, pp=pp: e.tensor_copy(out=dst, in_=pp.ap[:, 0:128]), [pp.res], [acc.res])
                                else:
                                    S.op("dve", lambda e, dst=dst, pp=pp: e.tensor_tensor(out=dst, in0=pp.ap[:, 0:128], in1=dst, op=ALU.add),
                                         [pp.res], [acc.res])
                        prev = (pt, v, nk, first)
            o = od[slot % 2]
            S.op("dve", lambda e: e.reciprocal(out=rcp.ap[:], in_=acc.ap[:, 1, :]), [acc.res], [rcp.res])
            S.op("dve", lambda e, o=o: e.tensor_tensor(out=o.ap[:], in0=acc.ap[:, 0, :], in1=rcp.ap[:], op=ALU.mult),
                 [acc.res, rcp.res], [o.res])
            S.dma("sp", o.dom, [(lambda e, o=o, slot=slot: e.dma_start(out=g.odT[slot][:, 0:Sq], in_=o.ap[:]), [o.res], [])])


def mixer_p4(g, l, t0, Sq):
    S = g.S
    TB, NT = 512, 4
    d = g.d
    with Pass(g, "p4") as P:
        wbm = P.sb([64, NH, D], BF16, dma=True)
        wbd = P.sb([128, 4, D], BF16, dma=True)
        wo = P.sb([128, KC, D], BF16, dma=True)
        load_w_cast(g, wbm, d["w_branch_mla"][l].rearrange("(h p) f -> p h f", p=64), D)
        load_w_cast(g, wbd, d["w_branch_dil"][l].rearrange("(c p) f -> p c f", p=128), D)
        load_w_cast(g, wo, d["w_out"][l].rearrange("(c p) f -> p c f", p=128), D)
        xin = P.ring(2, [128, NT, D], F32, dma=True)
        xst = [g.S.dma_dom() for _ in range(2)]
        om = P.ring(2, [64, NH, TB], BF16, dma=True)
        od = P.ring(2, [128, 4, TB], BF16, dma=True)
        gt = P.ring(2, [128, 16, TB], BF16, dma=True)
        mT = P.sb([128, KC, TB], BF16)
        mres = [Res() for _ in range(KC)]
        t1 = P.ring(2, [128, TB], F32)
        t2 = P.ring(2, [128, TB], F32)
        pbm = [P.bank(), P.bank()]
        pbd = [P.bank(), P.bank()]
        pys = [P.bank(), P.bank()]
        xv = g.xres.rearrange("(n j p) d -> n p j d", j=NT, p=128)
        omv = g.omT.rearrange("h r s -> r h s")
        odv = g.odT.rearrange("i p s -> p i s")
        gv = g.gT.rearrange("c p s -> p c s")
        for b in range(Sq // TB):
            s0 = b * TB
            gblk = (t0 + s0) // TB
            xb, o1, o2, gg = xin[b % 2], om[b % 2], od[b % 2], gt[b % 2]
            S.dma("sp", xb.dom, [(lambda e, xb=xb, gblk=gblk: e.dma_start(out=xb.ap[:], in_=xv[gblk]), [], [xb.res])])
            S.dma("sp", o1.dom, [(lambda e, o1=o1, s0=s0: e.dma_start(out=o1.ap[:], in_=omv[:, :, s0:s0 + TB]), [], [o1.res])])
            S.dma("sp", o2.dom, [(lambda e, o2=o2, s0=s0: e.dma_start(out=o2.ap[:], in_=odv[:, :, s0:s0 + TB]), [], [o2.res])])
            S.dma("sp", gg.dom, [(lambda e, gg=gg, s0=s0: e.dma_start(out=gg.ap[:], in_=gv[:, :, s0:s0 + TB]), [], [gg.res])])
            for c in range(KC):
                p1, p2, a1, a2 = pbm[c % 2], pbd[c % 2], t1[c % 2], t2[c % 2]
                for h in range(NH):
                    S.op("pe", lambda e, h=h, c=c, p1=p1, o1=o1: e.matmul(
                        p1.ap[:], lhsT=wbm.ap[:, h, c * 128:(c + 1) * 128], rhs=o1.ap[:, h, :], start=(h == 0), stop=(h == NH - 1)),
                         [wbm.res, o1.res], [p1.res], signal=(h == NH - 1))
                for i in range(4):
                    S.op("pe", lambda e, i=i, c=c, p2=p2, o2=o2: e.matmul(
                        p2.ap[:], lhsT=wbd.ap[:, i, c * 128:(c + 1) * 128], rhs=o2.ap[:, i, :], start=(i == 0), stop=(i == 3)),
                         [wbd.res, o2.res], [p2.res], signal=(i == 3))
                S.op("dve", lambda e, p1=p1, a1=a1, gg=gg, c=c: e.tensor_tensor(out=a1.ap[:], in0=p1.ap[:], in1=gg.ap[:, c, :], op=ALU.mult),
                     [p1.res, gg.res], [a1.res])
                S.op("dve", lambda e, p2=p2, a2=a2, gg=gg, c=c: e.tensor_tensor(out=a2.ap[:], in0=p2.ap[:], in1=gg.ap[:, 8 + c, :], op=ALU.mult),
                     [p2.res, gg.res], [a2.res])
                S.op("pool", lambda e, a1=a1, a2=a2, c=c: e.tensor_tensor(out=mT.ap[:, c, :], in0=a1.ap[:], in1=a2.ap[:], op=ALU.add),
                     [a1.res, a2.res], [mres[c]])
            for j in range(NT):
                for hf in range(2):
                    py = pys[(j * 2 + hf) % 2]
                    for c in range(KC):
                        S.op("pe", lambda e, c=c, j=j, hf=hf, py=py: e.matmul(
                            py.ap[:], lhsT=mT.ap[:, c, j * 128:(j + 1) * 128], rhs=wo.ap[:, c, hf * 512:(hf + 1) * 512],
                            start=(c == 0), stop=(c == KC - 1)), [mres[c], wo.res], [py.res], signal=(c == KC - 1))
                    S.op("dve", lambda e, j=j, hf=hf, py=py, xb=xb: e.tensor_tensor(
                        out=xb.ap[:, j, hf * 512:(hf + 1) * 512], in0=py.ap[:], in1=xb.ap[:, j, hf * 512:(hf + 1) * 512], op=ALU.add),
                         [py.res], [xb.res])
            S.dma("sp", xst[b % 2], [(lambda e, xb=xb, gblk=gblk: e.dma_start(out=xv[gblk], in_=xb.ap[:]), [xb.res], [])])


W_NAMES = ["ffn1_norm", "ffn1_w_gate", "ffn1_w_up", "ffn1_w_down", "mix_norm", "q_a_norm", "kv_a_norm",
           "w_branch_mla", "w_branch_dil", "w_out", "ffn2_norm", "ffn2_w_gate", "ffn2_w_up", "ffn2_w_down",
           "final_norm", "w_zA", "w_kpe2", "w_dq", "w_dk", "w_dv", "w_gt", "w_q2", "w_kn", "w_v", "bgate2",
           "ropetab", "e_full", "e_first", "e_last", "ident", "ones"]


def build_program(seqs, depth, shapes, dtypes):
    T = sum(seqs)
    nc = bass.Bass("TRN2", target_bir_lowering=False)
    g = G()
    g.nc = nc
    g.d = {}
    for n in W_NAMES:
        g.d[n] = nc.dram_tensor(n, list(shapes[n]), dtypes[n], kind="ExternalInput").ap()
    x_in = nc.dram_tensor("x", [T, D], F32, kind="ExternalInput").ap()
    y_out = nc.dram_tensor("y", [T, D], F32, kind="ExternalOutput").ap()
    g.xres = nc.dram_tensor("xres", [T, D], F32, kind="Internal").ap()
    sm = max(seqs)
    g.qT = nc.dram_tensor("s_qT", [NH, 96, sm], BF16, kind="Internal").ap()
    g.kT = nc.dram_tensor("s_kT", [NH, 96, sm], BF16, kind="Internal").ap()
    g.vm = nc.dram_tensor("s_vm", [sm, NH * 128], BF16, kind="Internal").ap()
    g.dqT = nc.dram_tensor("s_dqT", [DH, 128, sm], BF16, kind="Internal").ap()
    g.dkT = nc.dram_tensor("s_dkT", [DH, 128, sm], BF16, kind="Internal").ap()
    g.dv = nc.dram_tensor("s_dv", [sm, DH * 128], BF16, kind="Internal").ap()
    g.gT = nc.dram_tensor("s_gT", [16, 128, sm], BF16, kind="Internal").ap()
    g.omT = nc.dram_tensor("s_omT", [NH, 64, sm], BF16, kind="Internal").ap()
    g.odT = nc.dram_tensor("s_odT", [4, 128, sm], BF16, kind="Internal").ap()
    with ExitStack() as st:
        S = Sched(nc, st)
        g.S = S
        identt = st.enter_context(nc.sbuf_tensor("ident_sb", [128, 128], BF16))
        epst = st.enter_context(nc.sbuf_tensor("eps_sb", [128, 1], F32))
        g.ident = Slot(identt, S.pool[-1])
        g.eps = Slot(epst)
        S.dma("sp", g.ident.dom, [(lambda e: e.dma_start(out=identt[:], in_=g.d["ident"]), [], [g.ident.res])])
        S.op("pool", lambda e: e.memset(epst[:], EPS), [], [g.eps.res])
        S.pool = S.pool[:-1]
        for l in range(depth):
            ffn_pass(g, l, 1, x_in if l == 0 else g.xres, g.xres, T)
            t0 = 0
            for Sq in seqs:
                mixer_p1(g, l, t0, Sq)
                mixer_p2(g, Sq)
                mixer_p3(g, Sq)
                mixer_p4(g, l, t0, Sq)
                t0 += Sq
            ffn_pass(g, l, 2, g.xres, g.xres, T)
        final_pass(g, g.xres, y_out, T)
        S.barrier()
        S.emit()
    g.n_ins = dict(S.n_ins)
    return nc, g


def host_constants():
    pos = np.arange(SMAX, dtype=np.float32)
    inv = (1.0 / (np.float32(10000.0) ** (np.arange(0, 32, 2, dtype=np.float32) / np.float32(32)))).astype(np.float32)
    ang = (pos[:, None] * inv[None, :]).astype(np.float32)
    cos, sin = np.cos(ang).astype(np.float32).T, np.sin(ang).astype(np.float32).T
    rope = np.zeros((32, 2, SMAX), np.float32)
    rope[0:16, 0], rope[16:32, 0] = cos, cos
    rope[0:16, 1], rope[16:32, 1] = -sin, sin
    slopes = (2.0 ** (-8.0 * np.arange(1, DH + 1, dtype=np.float32) / DH)).astype(np.float32)
    k = np.arange(128)[:, None]
    q = np.arange(128)[None, :]
    e_full = np.zeros((128, DH, 256), np.float32)
    e_first = np.zeros((64, DH, 128), np.float32)
    e_last = np.zeros((64, DH, 128), np.float32)
    for h in range(DH):
        dl = DIL[h // 4]
        sl = slopes[h] * dl
        relB = k - q + 64
        e_full[:, h, 0:128] = np.where(k <= q, np.exp(-sl * np.abs(relB).astype(np.float32)), 0.0)
        relA = k - q - 64
        e_full[:, h, 128:256] = np.where(k >= q, np.exp(-sl * np.abs(relA).astype(np.float32)), 0.0)
        k6 = np.arange(64)[:, None]
        rel1 = k6 - q
        e_first[:, h, :] = np.where(np.abs(rel1) <= 64, np.exp(-sl * np.abs(rel1).astype(np.float32)), 0.0)
        rel2 = 64 + k6 - q
        e_last[:, h, :] = np.where(np.abs(rel2) <= 64, np.exp(-sl * np.abs(rel2).astype(np.float32)), 0.0)
    return {"ropetab": rope, "e_full": e_full, "e_first": e_first, "e_last": e_last,
            "ident": np.eye(128, dtype=np.float32).astype(ml_dtypes.bfloat16),
            "ones": np.ones((128, 128), np.float32).astype(ml_dtypes.bfloat16)}


def host_weights(inp):
    f = lambda a: np.ascontiguousarray(np.asarray(a, dtype=np.float32))
    w_in = f(inp["w_in"])
    L = w_in.shape[0]
    out = {}
    for n in ["ffn1_norm", "ffn1_w_gate", "ffn1_w_up", "ffn1_w_down", "mix_norm", "q_a_norm", "kv_a_norm",
              "w_branch_mla", "w_branch_dil", "w_out", "ffn2_norm", "ffn2_w_gate", "ffn2_w_up", "ffn2_w_down", "final_norm"]:
        out[n] = f(inp[n])
    out["w_zA"] = f(w_in[:, :, 0:384])
    z64 = np.zeros((L, w_in.shape[1], 64), np.float32)
    out["w_kpe2"] = f(np.concatenate([z64, w_in[:, :, 384:416], z64, w_in[:, :, 400:416], w_in[:, :, 384:400]], axis=2))
    out["w_dq"] = f(w_in[:, :, 416:1952])
    out["w_dk"] = f(w_in[:, :, 1952:3488])
    out["w_dv"] = f(w_in[:, :, 3488:5024])
    out["w_gt"] = f(w_in[:, :, 5024:7072])
    wq = f(inp["w_q_up"]).reshape(L, 256, NH, 96)
    nope, rope = wq[..., 0:64], wq[..., 64:96]
    out["w_q2"] = f(np.concatenate([nope, rope, np.zeros_like(nope), rope[..., 16:32], rope[..., 0:16]], axis=3).reshape(L, 256, NH * 192))
    wkv = f(inp["w_kv_up"]).reshape(L, 128, NH, 128)
    out["w_kn"] = f(wkv[..., 0:64].reshape(L, 128, NH * 64))
    out["w_v"] = f(wkv[..., 64:128].reshape(L, 128, NH * 64))
    out["bgate2"] = f(f(inp["b_gate"]).reshape(L, 16, 128).transpose(0, 2, 1))
    out.update(host_constants())
    return out


_CACHE = {}


def run_cores(xs, seqs, depth, W, trace=False):
    shapes = {k: v.shape for k, v in W.items()}
    dtypes = {k: (BF16 if v.dtype == ml_dtypes.bfloat16 else F32) for k, v in W.items()}
    key = (tuple(seqs), depth)
    if key not in _CACHE:
        _CACHE[key] = build_program(seqs, depth, shapes, dtypes)
    nc, g = _CACHE[key]
    in_maps = []
    for x in xs:
        m = dict(W)
        m["x"] = x
        in_maps.append(m)
    res = run_bass_kernel_spmd(nc, in_maps, core_ids=list(range(len(xs))), trace=trace)
    return [r["y"] for r in res.results], res


def kernel(**inp):
    xp = np.asarray(inp["x_prompt"], dtype=np.float32)
    xs_ = np.asarray(inp["x_sample"], dtype=np.float32)
    W = host_weights(inp)
    depth = W["w_zA"].shape[0]
    ncore = 8
    pb, sbn = xp.shape[0] // ncore, xs_.shape[0] // ncore
    seqs = [xp.shape[1]] * pb + [xs_.shape[1]] * sbn
    xs = []
    for c in range(ncore):
        xs.append(np.ascontiguousarray(np.concatenate(
            [xp[c * pb:(c + 1) * pb].reshape(-1, D), xs_[c * sbn:(c + 1) * sbn].reshape(-1, D)], axis=0)))
    ys, _ = run_cores(xs, seqs, depth, W)
    np_ = pb * xp.shape[1]
    y_prompt = np.concatenate([y[:np_].reshape(pb, xp.shape[1], D) for y in ys], axis=0)
    y_sample = np.concatenate([y[np_:].reshape(sbn, xs_.shape[1], D) for y in ys], axis=0)
    return (y_prompt.astype(np.float32), y_sample.astype(np.float32))
```

```python
import numpy as np
import ml_dtypes
from contextlib import ExitStack
import concourse.bass as bass
import concourse.mybir as mybir
from concourse.bass_utils import run_bass_kernel_spmd

F32 = mybir.dt.float32
BF16 = mybir.dt.bfloat16
AF = mybir.ActivationFunctionType
ALU = mybir.AluOpType

SELF_SYNC = True
D = 1024
KC = 8
FF = 2816
FC = 22
NH = 8
DH = 12
DIL = (1, 4, 16)
SMAX = 4096
EPS = 1e-6


class Dom:
    def __init__(self, sem, unit, name):
        self.sem, self.unit, self.count, self.name = sem, unit, 0, name


class Res:
    __slots__ = ("w", "r", "excl")

    def __init__(self, excl=False):
        self.w = None
        self.r = {}
        self.excl = excl


class Sched:
    ENG = ("pe", "act", "dve", "pool", "sp")

    def __init__(self, nc, stack, n_dma=56):
        self.nc = nc
        self.ops = {k: [] for k in self.ENG}
        self.doms = {}
        for k in ("pe", "act", "dve", "pool"):
            self.doms[k] = Dom(stack.enter_context(nc.semaphore("s_" + k)), 1, k)
        self.waited = {k: {} for k in self.ENG}
        self.pool = [Dom(stack.enter_context(nc.semaphore("d%d" % i)), 16, "d%d" % i) for i in range(n_dma)]
        self.pool_i = 0
        self.n_ins = {k: 0 for k in self.ENG}

    def new_pass(self):
        self.pool_i = 0

    def dma_dom(self):
        d = self.pool[self.pool_i]
        self.pool_i += 1
        return d

    def _need(self, eng, tok, waits, own=None):
        if tok is None:
            return
        dom, val = tok
        if dom is own and dom.unit == 16:
            return
        if dom is self.doms.get(eng):
            if eng == "pe" or not SELF_SYNC:
                return
        if self.waited[eng].get(dom, 0) >= val:
            return
        self.waited[eng][dom] = val
        waits.append((dom.sem, val * dom.unit))

    def dma(self, eng, dom, items):
        final = dom.count + len(items)
        tok = None
        for fn, reads, writes in items:
            tok = self.op(eng, fn, reads, writes, dom=dom, tokval=final)
        return tok

    def op(self, eng, fn, reads=(), writes=(), dom=None, signal=True, tokval=None):
        waits = []
        if any(r.excl for r in reads):
            writes = list(writes) + [r for r in reads if r.excl]
            reads = [r for r in reads if not r.excl]
        d = dom if dom is not None else self.doms[eng]
        for r in reads:
            self._need(eng, r.w, waits, d)
        for w in writes:
            self._need(eng, w.w, waits, d)
            for dd, v in w.r.items():
                self._need(eng, (dd, v), waits, d)
        if signal:
            d.count += 1
        tok = (d, d.count if signal else d.count + 1)
        if tokval is not None:
            tok = (d, tokval)
        for r in reads:
            if r.r.get(d, 0) < tok[1]:
                r.r[d] = tok[1]
        for w in writes:
            w.w = tok
            w.r = {}
        sem, unit = d.sem, d.unit
        self.n_ins[eng] += 1

        def run(e, waits=waits, fn=fn, signal=signal, sem=sem, unit=unit):
            for s, v in waits:
                e.wait_ge(s, v)
            ins = fn(e)
            if signal:
                ins.then_inc(sem, unit)
        self.ops[eng].append(run)
        return tok

    def barrier(self, engs=None):
        alld = list(self.doms.values()) + self.pool
        for eng in (engs or self.ENG):
            waits = []
            for d in alld:
                if d.count > 0:
                    self._need(eng, (d, d.count), waits)

            def run(e, waits=waits):
                for s, v in waits:
                    e.wait_ge(s, v)
            self.ops[eng].append(run)

    def emit(self):
        nc, ops = self.nc, self.ops
        with nc.Block() as block:
            @block.tensor
            def _(e):
                for f in ops["pe"]:
                    f(e)

            @block.scalar
            def _(e):
                for f in ops["act"]:
                    f(e)

            @block.vector
            def _(e):
                for f in ops["dve"]:
                    f(e)

            @block.gpsimd
            def _(e):
                for f in ops["pool"]:
                    f(e)

            @block.sync
            def _(e):
                for f in ops["sp"]:
                    f(e)


class Slot:
    def __init__(self, ap, dom=None, excl=False):
        self.ap, self.res, self.dom = ap, Res(excl), dom


class Pass:
    _uid = [0]

    def __init__(self, G, name):
        Pass._uid[0] += 1
        self.G, self.name = G, "%s_%d" % (name, Pass._uid[0])
        self.st = ExitStack()
        self.n = 0

    def __enter__(self):
        self.st.__enter__()
        self.G.S.barrier()
        self.G.S.new_pass()
        self.banks = []
        return self

    def __exit__(self, *a):
        return self.st.__exit__(*a)

    def sb(self, shape, dt, dma=False):
        self.n += 1
        t = self.st.enter_context(self.G.nc.sbuf_tensor("%s_%d" % (self.name, self.n), list(shape), dt))
        return Slot(t, self.G.S.dma_dom() if dma else None)

    def ring(self, n, shape, dt, dma=False):
        return [self.sb(shape, dt, dma) for _ in range(n)]

    def bank(self, dt=F32):
        self.n += 1
        shape = [128, 512] if dt == F32 else [128, 1024]
        t = self.st.enter_context(self.G.nc.psum_tensor("%s_p%d" % (self.name, self.n), shape, dt))
        return Slot(t, excl=True)


class G:
    pass


def load_w_cast(g, slot, src, ncols, split=1):
    S = g.S
    nch = src.shape[1]
    step = ncols // split
    items = []
    for c in range(nch):
        for h in range(split):
            items.append((lambda e, c=c, h=h: e.dma_start(out=slot.ap[:, c, h * step:(h + 1) * step],
                                                          in_=src[:, c, h * step:(h + 1) * step]),
                          [], [slot.res]))
    S.dma("pool", slot.dom, items)


def load_simple(g, slot, src, eng="sp", dst=None):
    d = slot.ap[:] if dst is None else dst
    g.S.dma(eng, slot.dom, [(lambda e: e.dma_start(out=d, in_=src), [], [slot.res])])


def rms_stats(g, xs, ntile, width, ss, rt, rstd, junk):
    S = g.S
    for j in range(ntile):
        ap, res = xs(j)
        S.op("act", lambda e, ap=ap, j=j: e.activation(out=junk.ap[:, 0:width], in_=ap, func=AF.Square,
                                                        accum_out=ss.ap[:, j:j + 1]),
             [res], [junk.res, ss.res])
    S.op("act", lambda e: e.activation(out=rt.ap[:, 0:ntile], in_=ss.ap[:, 0:ntile], func=AF.Sqrt,
                                       scale=1.0 / width, bias=g.eps.ap[:, 0:1]),
         [ss.res, g.eps.res], [rt.res])
    S.op("dve", lambda e: e.reciprocal(out=rstd.ap[:, 0:ntile], in_=rt.ap[:, 0:ntile]), [rt.res], [rstd.res])


def transpose_tile(g, src, nchunk, tpb, dst_ap, dst_res, cp_eng):
    S = g.S
    for c in range(nchunk):
        S.op("pe", lambda e, c=c: e.transpose(out=tpb.ap[:, c * 128:(c + 1) * 128],
                                              in_=src.ap[:, c * 128:(c + 1) * 128], identity=g.ident.ap[:]),
             [src.res, g.ident.res], [tpb.res], signal=(c == nchunk - 1))
    view = tpb.ap[:, 0:nchunk * 128].rearrange("p (c t) -> p c t", c=nchunk)
    if cp_eng == "act":
        S.op("act", lambda e: e.activation(out=dst_ap, in_=view, func=AF.Copy), [tpb.res], [dst_res])
    else:
        S.op("dve", lambda e: e.tensor_copy(out=dst_ap, in_=view), [tpb.res], [dst_res])


def norm_block(g, P, xb, ntile, gb, xn_ring, tpbs, xnT, stats, blk_parity):
    S = g.S
    ss, rt, rstd, junk = stats
    rms_stats(g, lambda j: (xb.ap[:, j, :], xb.res), ntile, D, ss, rt, rstd, junk)
    for j in range(ntile):
        xn = xn_ring[j % len(xn_ring)]
        S.op("dve", lambda e, j=j, xn=xn: e.scalar_tensor_tensor(out=xn.ap[:], in0=xb.ap[:, j, :],
                                                                 scalar=rstd.ap[:, j:j + 1], in1=gb.ap[:],
                                                                 op0=ALU.mult, op1=ALU.mult),
             [xb.res, rstd.res, gb.res], [xn.res])
        tpb = tpbs[j % len(tpbs)]
        transpose_tile(g, xn, KC, tpb, xnT.ap[:, :, j * 128:(j + 1) * 128], xnT.res, "act")


def ffn_pass(g, l, which, src, dst, T):
    S = g.S
    TB, NT = 256, 2
    wg_d, wu_d, wd_d, gn_d = (g.d["ffn%d_w_gate" % which], g.d["ffn%d_w_up" % which],
                              g.d["ffn%d_w_down" % which], g.d["ffn%d_norm" % which])
    with Pass(g, "f%d%d" % (l, which)) as P:
        wg = P.sb([128, KC, FF], BF16, dma=True)
        wu = P.sb([128, KC, FF], BF16, dma=True)
        wd = P.sb([128, FC, D], BF16, dma=True)
        gb = P.sb([128, D], F32, dma=True)
        load_w_cast(g, wg, wg_d[l].rearrange("(c p) f -> p c f", p=128), FF, split=2)
        load_w_cast(g, wu, wu_d[l].rearrange("(c p) f -> p c f", p=128), FF, split=2)
        load_w_cast(g, wd, wd_d[l].rearrange("(c p) f -> p c f", p=128), D, split=1)
        load_simple(g, gb, gn_d[l].partition_broadcast(128))
        xin = P.ring(2, [128, NT, D], F32, dma=True)
        xst = [P.G.S.dma_dom() for _ in range(2)]
        xn_ring = P.ring(2, [128, D], BF16)
        xnT_ring = P.ring(2, [128, KC, TB], BF16)
        hT = P.sb([128, FC, TB], BF16)
        hres = [Res() for _ in range(FC)]
        sg_ring = P.ring(2, [128, TB], F32)
        junk = P.sb([128, D], BF16)
        stats = [(P.sb([128, 4], F32), P.sb([128, 4], F32), P.sb([128, 4], F32), junk) for _ in range(2)]
        tpbs = [P.bank(BF16), P.bank(BF16)]
        pgs = [P.bank(), P.bank()]
        pus = [P.bank(), P.bank()]
        pys = [P.bank(), P.bank()]
        srcv = src.rearrange("(n j p) d -> n p j d", j=NT, p=128)
        dstv = dst.rearrange("(n j p) d -> n p j d", j=NT, p=128)
        nblk = T // TB
        for b in range(nblk):
            xb = xin[b % 2]
            S.dma("sp", xb.dom, [(lambda e, xb=xb, b=b: e.dma_start(out=xb.ap[:], in_=srcv[b]), [], [xb.res])])
            xnT = xnT_ring[b % 2]
            norm_block(g, P, xb, NT, gb, xn_ring, tpbs, xnT, stats[b % 2], b % 2)
            for f in range(FC):
                pg, pu, sg = pgs[f % 2], pus[f % 2], sg_ring[f % 2]
                for k in range(KC):
                    S.op("pe", lambda e, k=k, f=f, pg=pg, xnT=xnT: e.matmul(
                        pg.ap[:, 0:TB], lhsT=wg.ap[:, k, f * 128:(f + 1) * 128], rhs=xnT.ap[:, k, :],
                        start=(k == 0), stop=(k == KC - 1)), [wg.res, xnT.res], [pg.res], signal=(k == KC - 1))
                for k in range(KC):
                    S.op("pe", lambda e, k=k, f=f, pu=pu, xnT=xnT: e.matmul(
                        pu.ap[:, 0:TB], lhsT=wu.ap[:, k, f * 128:(f + 1) * 128], rhs=xnT.ap[:, k, :],
                        start=(k == 0), stop=(k == KC - 1)), [wu.res, xnT.res], [pu.res], signal=(k == KC - 1))
                S.op("act", lambda e, pg=pg, sg=sg: e.activation(out=sg.ap[:], in_=pg.ap[:, 0:TB], func=AF.Silu),
                     [pg.res], [sg.res])
                S.op("dve", lambda e, pu=pu, sg=sg, f=f: e.tensor_tensor(out=hT.ap[:, f, :], in0=pu.ap[:, 0:TB],
                                                                          in1=sg.ap[:], op=ALU.mult),
                     [pu.res, sg.res], [hres[f]])
            for j in range(NT):
                for hf in range(2):
                    py = pys[(j * 2 + hf) % 2]
                    for f in range(FC):
                        S.op("pe", lambda e, f=f, j=j, hf=hf, py=py: e.matmul(
                            py.ap[:], lhsT=hT.ap[:, f, j * 128:(j + 1) * 128], rhs=wd.ap[:, f, hf * 512:(hf + 1) * 512],
                            start=(f == 0), stop=(f == FC - 1)), [hres[f], wd.res], [py.res], signal=(f == FC - 1))
                    S.op("dve", lambda e, j=j, hf=hf, py=py, xb=xb: e.scalar_tensor_tensor(
                        out=xb.ap[:, j, hf * 512:(hf + 1) * 512], in0=py.ap[:], scalar=0.5,
                        in1=xb.ap[:, j, hf * 512:(hf + 1) * 512], op0=ALU.mult, op1=ALU.add),
                         [py.res], [xb.res])
            S.dma("sp", xst[b % 2], [(lambda e, xb=xb, b=b: e.dma_start(out=dstv[b], in_=xb.ap[:]), [xb.res], [])])


def final_pass(g, src, dst, T):
    S = g.S
    TB, NT = 256, 2
    with Pass(g, "fin") as P:
        gb = P.sb([128, D], F32, dma=True)
        load_simple(g, gb, g.d["final_norm"].partition_broadcast(128))
        xin = P.ring(3, [128, NT, D], F32, dma=True)
        xo = P.ring(3, [128, NT, D], F32, dma=True)
        junk = P.sb([128, D], BF16)
        stats = [(P.sb([128, 4], F32), P.sb([128, 4], F32), P.sb([128, 4], F32), junk) for _ in range(3)]
        srcv = src.rearrange("(n j p) d -> n p j d", j=NT, p=128)
        dstv = dst.rearrange("(n j p) d -> n p j d", j=NT, p=128)
        toks = []
        for b in range(T // TB):
            xb, ob = xin[b % 3], xo[b % 3]
            ss, rt, rstd, _ = stats[b % 3]
            S.dma("sp", xb.dom, [(lambda e, xb=xb, b=b: e.dma_start(out=xb.ap[:], in_=srcv[b]), [], [xb.res])])
            rms_stats(g, lambda j: (xb.ap[:, j, :], xb.res), NT, D, ss, rt, rstd, junk)
            for j in range(NT):
                S.op("dve", lambda e, j=j, xb=xb, ob=ob, rstd=rstd: e.scalar_tensor_tensor(
                    out=ob.ap[:, j, :], in0=xb.ap[:, j, :], scalar=rstd.ap[:, j:j + 1], in1=gb.ap[:],
                    op0=ALU.mult, op1=ALU.mult), [xb.res, rstd.res, gb.res], [ob.res])
            S.dma("sp", ob.dom, [(lambda e, ob=ob, b=b: e.dma_start(out=dstv[b], in_=ob.ap[:]), [ob.res], [])])


def mixer_p1(g, l, seqs):
    S = g.S
    TB, NT = 256, 2
    d = g.d
    with Pass(g, "p1") as P:
        wzA = P.sb([128, KC, 384], BF16, dma=True)
        wkp = P.sb([128, KC, 192], BF16, dma=True)
        wdq = P.sb([128, KC, 1536], BF16, dma=True)
        wdk = P.sb([128, KC, 1536], BF16, dma=True)
        wdv = P.sb([128, KC, 1536], BF16, dma=True)
        wgt = P.sb([128, KC, 2048], BF16, dma=True)
        wq = P.sb([128, 2, NH * 192], BF16, dma=True)
        wkn = P.sb([128, 1, NH * 64], BF16, dma=True)
        wv = P.sb([128, 1, 512], BF16, dma=True)
        cv = lambda a: a[l].rearrange("(c p) f -> p c f", p=128)
        load_w_cast(g, wzA, cv(d["w_zA"]), 384)
        load_w_cast(g, wkp, cv(d["w_kpe2"]), 192)
        load_w_cast(g, wdq, cv(d["w_dq"]), 1536)
        load_w_cast(g, wdk, cv(d["w_dk"]), 1536)
        load_w_cast(g, wdv, cv(d["w_dv"]), 1536)
        load_w_cast(g, wgt, cv(d["w_gt"]), 2048)
        load_w_cast(g, wq, cv(d["w_q2"]), NH * 192)
        load_w_cast(g, wkn, cv(d["w_kn"]), NH * 64)
        load_w_cast(g, wv, cv(d["w_v"]), 512)
        gb = P.sb([128, D], F32, dma=True)
        gq = P.sb([128, 256], F32, dma=True)
        gk = P.sb([128, 128], F32, dma=True)
        bg = P.sb([128, 16], F32, dma=True)
        load_simple(g, gb, d["mix_norm"][l].partition_broadcast(128))
        load_simple(g, gq, d["q_a_norm"][l].partition_broadcast(128))
        load_simple(g, gk, d["kv_a_norm"][l].partition_broadcast(128))
        load_simple(g, bg, d["bgate2"][l])
        xin = P.ring(2, [128, NT, D], F32, dma=True)
        ctab = P.ring(2, [96, 2, TB], F32, dma=True)
        xn_ring = P.ring(2, [128, D], BF16)
        xnT_ring = P.ring(2, [128, KC, TB], BF16)
        junk = P.sb([128, D], BF16)
        stats = [(P.sb([128, 4], F32), P.sb([128, 4], F32), P.sb([128, 4], F32), junk) for _ in range(2)]
        st2 = [(P.sb([128, 4], F32), P.sb([128, 4], F32), P.sb([128, 4], F32), junk) for _ in range(2)]
        cn_ring = P.ring(2, [128, 384], BF16)
        cT = P.sb([128, 3, TB], BF16)
        kpeT = P.sb([96, TB], BF16)
        rt1 = P.ring(2, [96, TB], F32)
        rt2 = P.ring(2, [96, TB], F32)
        QT = P.ring(2, [96, NH, TB], BF16, dma=True)
        KT = P.ring(2, [96, NH, TB], BF16, dma=True)
        vst = P.ring(2, [128, NH, 128], BF16, dma=True)
        pair = P.ring(4, [128, 2, TB], BF16, dma=True)
        dvst = P.ring(4, [128, 512], BF16, dma=True)
        for v in vst:
            S.op("pool", lambda e, v=v: e.memset(v.ap[:, :, 64:128], 1.0), [], [v.res])
        tpbs = [P.bank(BF16), P.bank(BF16)]
        pbk = [P.bank() for _ in range(6)]
        nb = [0]

        def nextbank():
            nb[0] += 1
            return pbk[nb[0] % 6]
        pi = [0]
        di = [0]
        xv = g.xres.rearrange("(n j p) d -> n p j d", j=NT, p=128)
        blocks = []
        t0 = 0
        for si, Sq in enumerate(seqs):
            for b in range(Sq // TB):
                blocks.append((si, b * TB, (t0 + b * TB) // TB))
            t0 += Sq

        def st_norm(bi):
            si, s0, gblk = blocks[bi]
            b = bi
            sc = g.scr[si]
            xb, ct, xnT = xin[b % 2], ctab[b % 2], xnT_ring[b % 2]
            qTv = sc.qT.rearrange("h r s -> r h s")
            kTv = sc.kT.rearrange("h r s -> r h s")
            vmv = sc.vm.rearrange("(n p) (h c) -> n p h c", p=128, c=128)
            dvv = sc.dv.rearrange("(n p) c -> n p c", p=128)
            S.dma("sp", xb.dom, [(lambda e, xb=xb, gblk=gblk: e.dma_start(out=xb.ap[:], in_=xv[gblk]), [], [xb.res])])
            S.dma("sp", ct.dom, [(lambda e, ct=ct, s0=s0: e.dma_start(out=ct.ap[64:96], in_=d["ropetab"][:, :, s0:s0 + TB]),
                                  [], [ct.res])])
            norm_block(g, P, xb, NT, gb, xn_ring, tpbs, xnT, stats[b % 2], b % 2)

        def st_zA(bi):
            si, s0, gblk = blocks[bi]
            b = bi
            sc = g.scr[si]
            xb, ct, xnT = xin[b % 2], ctab[b % 2], xnT_ring[b % 2]
            qTv = sc.qT.rearrange("h r s -> r h s")
            kTv = sc.kT.rearrange("h r s -> r h s")
            vmv = sc.vm.rearrange("(n p) (h c) -> n p h c", p=128, c=128)
            dvv = sc.dv.rearrange("(n p) c -> n p c", p=128)
            ss2, rtt2, rstd2, _ = st2[b % 2]
            for j in range(NT):
                pz = nextbank()
                cn = cn_ring[j % 2]
                for k in range(KC):
                    S.op("pe", lambda e, k=k, j=j, pz=pz, xnT=xnT: e.matmul(
                        pz.ap[:, 0:384], lhsT=xnT.ap[:, k, j * 128:(j + 1) * 128], rhs=wzA.ap[:, k, :],
                        start=(k == 0), stop=(k == KC - 1)), [xnT.res, wzA.res], [pz.res], signal=(k == KC - 1))
                S.op("act", lambda e, pz=pz, ss2=ss2: e.activation(out=junk.ap[:, 0:256], in_=pz.ap[:, 0:256], func=AF.Square,
                                                                  accum_out=ss2.ap[:, 0:1]), [pz.res], [junk.res, ss2.res])
                S.op("act", lambda e, pz=pz, ss2=ss2: e.activation(out=junk.ap[:, 0:128], in_=pz.ap[:, 256:384], func=AF.Square,
                                                                  accum_out=ss2.ap[:, 1:2]), [pz.res], [junk.res, ss2.res])
                S.op("act", lambda e, ss2=ss2, rtt2=rtt2: e.activation(out=rtt2.ap[:, 0:1], in_=ss2.ap[:, 0:1], func=AF.Sqrt,
                                                                      scale=1.0 / 256, bias=g.eps.ap[:, 0:1]),
                     [ss2.res, g.eps.res], [rtt2.res])
                S.op("act", lambda e, ss2=ss2, rtt2=rtt2: e.activation(out=rtt2.ap[:, 1:2], in_=ss2.ap[:, 1:2], func=AF.Sqrt,
                                                                      scale=1.0 / 128, bias=g.eps.ap[:, 0:1]),
                     [ss2.res, g.eps.res], [rtt2.res])
                S.op("dve", lambda e, rtt2=rtt2, rstd2=rstd2: e.reciprocal(out=rstd2.ap[:, 0:2], in_=rtt2.ap[:, 0:2]),
                     [rtt2.res], [rstd2.res])
                S.op("dve", lambda e, pz=pz, cn=cn, rstd2=rstd2: e.scalar_tensor_tensor(
                    out=cn.ap[:, 0:256], in0=pz.ap[:, 0:256], scalar=rstd2.ap[:, 0:1], in1=gq.ap[:],
                    op0=ALU.mult, op1=ALU.mult), [pz.res, rstd2.res, gq.res], [cn.res])
                S.op("dve", lambda e, pz=pz, cn=cn, rstd2=rstd2: e.scalar_tensor_tensor(
                    out=cn.ap[:, 256:384], in0=pz.ap[:, 256:384], scalar=rstd2.ap[:, 1:2], in1=gk.ap[:],
                    op0=ALU.mult, op1=ALU.mult), [pz.res, rstd2.res, gk.res], [cn.res])
                transpose_tile(g, cn, 3, tpbs[j % 2], cT.ap[:, :, j * 128:(j + 1) * 128], cT.res, "dve")

        def st_mla(bi):
            si, s0, gblk = blocks[bi]
            b = bi
            sc = g.scr[si]
            xb, ct, xnT = xin[b % 2], ctab[b % 2], xnT_ring[b % 2]
            qTv = sc.qT.rearrange("h r s -> r h s")
            kTv = sc.kT.rearrange("h r s -> r h s")
            vmv = sc.vm.rearrange("(n p) (h c) -> n p h c", p=128, c=128)
            dvv = sc.dv.rearrange("(n p) c -> n p c", p=128)
            pk = nextbank()
            for half in range(2):
                for k in range(KC):
                    S.op("pe", lambda e, k=k, half=half, pk=pk, xnT=xnT: e.matmul(
                        pk.ap[0:96, half * TB:(half + 1) * TB], lhsT=wkp.ap[:, k, half * 96:(half + 1) * 96],
                        rhs=xnT.ap[:, k, :], start=(k == 0), stop=(k == KC - 1)),
                         [xnT.res, wkp.res], [pk.res], signal=(k == KC - 1))
            a1, a2 = rt1[0], rt2[0]
            S.op("dve", lambda e, pk=pk, a1=a1, ct=ct: e.tensor_tensor(out=a1.ap[64:96], in0=pk.ap[64:96, 0:TB], in1=ct.ap[64:96, 0, :],
                                                                      op=ALU.mult), [pk.res, ct.res], [a1.res])
            S.op("dve", lambda e, pk=pk, a2=a2, ct=ct: e.tensor_tensor(out=a2.ap[64:96], in0=pk.ap[64:96, TB:2 * TB], in1=ct.ap[64:96, 1, :],
                                                                      op=ALU.mult), [pk.res, ct.res], [a2.res])
            S.op("pool", lambda e, a1=a1, a2=a2: e.tensor_tensor(out=kpeT.ap[64:96], in0=a1.ap[64:96], in1=a2.ap[64:96], op=ALU.add),
                 [a1.res, a2.res], [kpeT.res])
            qt, kt = QT[b % 2], KT[b % 2]
            for h in range(NH):
                pq = nextbank()
                for c in range(2):
                    S.op("pe", lambda e, c=c, h=h, pq=pq: e.matmul(
                        pq.ap[0:96, 0:TB], lhsT=wq.ap[:, c, h * 192:h * 192 + 96], rhs=cT.ap[:, c, :],
                        start=(c == 0), stop=(c == 1)), [wq.res, cT.res], [pq.res], signal=False)
                for c in range(2):
                    S.op("pe", lambda e, c=c, h=h, pq=pq: e.matmul(
                        pq.ap[0:96, TB:2 * TB], lhsT=wq.ap[:, c, h * 192 + 96:h * 192 + 192], rhs=cT.ap[:, c, :],
                        start=(c == 0), stop=(c == 1)), [wq.res, cT.res], [pq.res], signal=(c == 1))
                a1, a2 = rt1[h % 2], rt2[h % 2]
                S.op("act", lambda e, pq=pq, qt=qt, h=h: e.activation(out=qt.ap[0:64, h, :], in_=pq.ap[0:64, 0:TB], func=AF.Copy),
                     [pq.res], [qt.res])
                S.op("dve", lambda e, pq=pq, a1=a1, ct=ct: e.tensor_tensor(out=a1.ap[64:96], in0=pq.ap[64:96, 0:TB], in1=ct.ap[64:96, 0, :],
                                                                          op=ALU.mult), [pq.res, ct.res], [a1.res])
                S.op("dve", lambda e, pq=pq, a2=a2, ct=ct: e.tensor_tensor(out=a2.ap[64:96], in0=pq.ap[64:96, TB:2 * TB], in1=ct.ap[64:96, 1, :],
                                                                          op=ALU.mult), [pq.res, ct.res], [a2.res])
                S.op("pool", lambda e, a1=a1, a2=a2, qt=qt, h=h: e.tensor_tensor(out=qt.ap[64:96, h, :], in0=a1.ap[64:96], in1=a2.ap[64:96],
                                                                                op=ALU.add), [a1.res, a2.res], [qt.res])
                if h % 2 == 0:
                    pkn = nextbank()
                S.op("pe", lambda e, h=h, pkn=pkn: e.matmul(
                    pkn.ap[0:64, (h % 2) * TB:(h % 2 + 1) * TB], lhsT=wkn.ap[:, 0, h * 64:(h + 1) * 64], rhs=cT.ap[:, 2, :],
                    start=True, stop=True), [wkn.res, cT.res], [pkn.res])
                if h % 2 == 1:
                    S.op("act", lambda e, pkn=pkn, kt=kt, h=h: e.activation(
                        out=kt.ap[0:64, h - 1:h + 1, :], in_=pkn.ap[0:64, :].rearrange("p (a t) -> p a t", a=2),
                        func=AF.Copy), [pkn.res], [kt.res])
                S.op("pool", lambda e, kt=kt, h=h: e.tensor_copy(out=kt.ap[64:96, h, :], in_=kpeT.ap[64:96]), [kpeT.res], [kt.res])
            S.dma("sp", qt.dom, [(lambda e, qt=qt, s0=s0: e.dma_start(out=qTv[:, :, s0:s0 + TB], in_=qt.ap[:]), [qt.res], [])])
            S.dma("sp", kt.dom, [(lambda e, kt=kt, s0=s0: e.dma_start(out=kTv[:, :, s0:s0 + TB], in_=kt.ap[:]), [kt.res], [])])
            for j in range(NT):
                pv = nextbank()
                vs = vst[j % 2]
                S.op("pe", lambda e, j=j, pv=pv: e.matmul(pv.ap[:], lhsT=cT.ap[:, 2, j * 128:(j + 1) * 128], rhs=wv.ap[:, 0, :],
                                                         start=True, stop=True), [cT.res, wv.res], [pv.res])
                S.op("act", lambda e, pv=pv, vs=vs: e.activation(out=vs.ap[:, :, 0:64],
                                                                in_=pv.ap[:].rearrange("p (h c) -> p h c", c=64), func=AF.Copy),
                     [pv.res], [vs.res])
                S.dma("sp", vs.dom, [(lambda e, vs=vs, j=j, s0=s0: e.dma_start(out=vmv[(s0 // 128) + j], in_=vs.ap[:]), [vs.res], [])])

        def st_dqk(bi):
            si, s0, gblk = blocks[bi]
            b = bi
            sc = g.scr[si]
            xb, ct, xnT = xin[b % 2], ctab[b % 2], xnT_ring[b % 2]
            qTv = sc.qT.rearrange("h r s -> r h s")
            kTv = sc.kT.rearrange("h r s -> r h s")
            vmv = sc.vm.rearrange("(n p) (h c) -> n p h c", p=128, c=128)
            dvv = sc.dv.rearrange("(n p) c -> n p c", p=128)
            for which, wsb, dstT in ((0, wdq, sc.dqT), (1, wdk, sc.dkT)):
                dview = dstT.rearrange("h p s -> p h s")
                for hp in range(DH // 2):
                    pb = nextbank()
                    for a in range(2):
                        hc = hp * 2 + a
                        for k in range(KC):
                            S.op("pe", lambda e, k=k, hc=hc, a=a, pb=pb, wsb=wsb, xnT=xnT: e.matmul(
                                pb.ap[:, a * TB:(a + 1) * TB], lhsT=wsb.ap[:, k, hc * 128:(hc + 1) * 128], rhs=xnT.ap[:, k, :],
                                start=(k == 0), stop=(k == KC - 1)), [wsb.res, xnT.res], [pb.res],
                                 signal=(k == KC - 1 and a == 1))
                    pr = pair[pi[0] % 4]
                    pi[0] += 1
                    if hp % 2 == 0:
                        S.op("act", lambda e, pb=pb, pr=pr: e.activation(out=pr.ap[:], in_=pb.ap[:].rearrange("p (a t) -> p a t", a=2),
                                                                        func=AF.Copy), [pb.res], [pr.res])
                    else:
                        S.op("dve", lambda e, pb=pb, pr=pr: e.tensor_copy(out=pr.ap[:], in_=pb.ap[:].rearrange("p (a t) -> p a t", a=2)),
                             [pb.res], [pr.res])
                    S.dma("sp", pr.dom, [(lambda e, pr=pr, hp=hp, dview=dview, s0=s0: e.dma_start(
                        out=dview[:, 2 * hp:2 * hp + 2, s0:s0 + TB], in_=pr.ap[:]), [pr.res], [])])

        def st_dv(bi):
            si, s0, gblk = blocks[bi]
            b = bi
            sc = g.scr[si]
            xb, ct, xnT = xin[b % 2], ctab[b % 2], xnT_ring[b % 2]
            qTv = sc.qT.rearrange("h r s -> r h s")
            kTv = sc.kT.rearrange("h r s -> r h s")
            vmv = sc.vm.rearrange("(n p) (h c) -> n p h c", p=128, c=128)
            dvv = sc.dv.rearrange("(n p) c -> n p c", p=128)
            for j in range(NT):
                for m in range(3):
                    pb = nextbank()
                    for k in range(KC):
                        S.op("pe", lambda e, k=k, j=j, m=m, pb=pb, xnT=xnT: e.matmul(
                            pb.ap[:], lhsT=xnT.ap[:, k, j * 128:(j + 1) * 128], rhs=wdv.ap[:, k, m * 512:(m + 1) * 512],
                            start=(k == 0), stop=(k == KC - 1)), [wdv.res, xnT.res], [pb.res], signal=(k == KC - 1))
                    dvs = dvst[di[0] % 4]
                    di[0] += 1
                    if m % 2 == 0:
                        S.op("dve", lambda e, pb=pb, dvs=dvs: e.tensor_copy(out=dvs.ap[:], in_=pb.ap[:]), [pb.res], [dvs.res])
                    else:
                        S.op("act", lambda e, pb=pb, dvs=dvs: e.activation(out=dvs.ap[:], in_=pb.ap[:], func=AF.Copy), [pb.res], [dvs.res])
                    S.dma("sp", dvs.dom, [(lambda e, dvs=dvs, j=j, m=m, s0=s0: e.dma_start(
                        out=dvv[(s0 // 128) + j][:, m * 512:(m + 1) * 512], in_=dvs.ap[:]), [dvs.res], [])])

        def st_gates(bi):
            si, s0, gblk = blocks[bi]
            b = bi
            sc = g.scr[si]
            xb, ct, xnT = xin[b % 2], ctab[b % 2], xnT_ring[b % 2]
            qTv = sc.qT.rearrange("h r s -> r h s")
            kTv = sc.kT.rearrange("h r s -> r h s")
            vmv = sc.vm.rearrange("(n p) (h c) -> n p h c", p=128, c=128)
            dvv = sc.dv.rearrange("(n p) c -> n p c", p=128)
            gview = sc.gT.rearrange("c p s -> p c s")
            for cp in range(8):
                pb = nextbank()
                for a in range(2):
                    c = cp * 2 + a
                    for k in range(KC):
                        S.op("pe", lambda e, k=k, c=c, a=a, pb=pb, xnT=xnT: e.matmul(
                            pb.ap[:, a * TB:(a + 1) * TB], lhsT=wgt.ap[:, k, c * 128:(c + 1) * 128], rhs=xnT.ap[:, k, :],
                            start=(k == 0), stop=(k == KC - 1)), [wgt.res, xnT.res], [pb.res],
                             signal=(k == KC - 1 and a == 1))
                pr = pair[pi[0] % 4]
                pi[0] += 1
                for a in range(2):
                    c = cp * 2 + a
                    S.op("act", lambda e, pb=pb, pr=pr, a=a, c=c: e.activation(
                        out=pr.ap[:, a, :], in_=pb.ap[:, a * TB:(a + 1) * TB], func=AF.Sigmoid, bias=bg.ap[:, c:c + 1]),
                         [pb.res, bg.res], [pr.res])
                S.dma("sp", pr.dom, [(lambda e, pr=pr, cp=cp, s0=s0: e.dma_start(
                    out=gview[:, 2 * cp:2 * cp + 2, s0:s0 + TB], in_=pr.ap[:]), [pr.res], [])])


        st_norm(0)
        for bi in range(len(blocks)):
            st_zA(bi)
            st_dqk(bi)
            if bi + 1 < len(blocks):
                st_norm(bi + 1)
            st_mla(bi)
            st_dv(bi)
            st_gates(bi)


def mixer_p2(g, seqs):
    S = g.S
    QB = 512
    scale = 96.0 ** -0.5
    smax = max(seqs)
    with Pass(g, "p2") as P:
        KT = P.sb([96, NH, smax], BF16, dma=True)
        VA = P.sb([128, smax // 128, NH * 128], BF16, dma=True)
        QT = P.ring(2, [96, NH, QB], BF16, dma=True)
        PT = P.ring(4, [128, QB], BF16)
        oT = P.ring(2, [64, NH, QB], BF16, dma=True)
        rl = P.ring(2, [64, QB], F32)
        pss = [P.bank() for _ in range(3)]
        pos = [P.bank() for _ in range(2)]
        LAG = 2
        it = 0
        qn = 0
        for si, Sq in enumerate(seqs):
            sc = g.scr[si]
            nkb = Sq // 128
            S.dma("sp", KT.dom, [(lambda e, sc=sc, Sq=Sq: e.dma_start(out=KT.ap[:, :, 0:Sq], in_=sc.kT.rearrange("h r s -> r h s")),
                                  [], [KT.res])])
            vv = sc.vm.rearrange("(n p) c -> p n c", p=128)
            S.dma("sp", VA.dom, [(lambda e, a=a, vv=vv: e.dma_start(out=VA.ap[:, a * 8:(a + 1) * 8, :], in_=vv[:, a * 8:(a + 1) * 8, :]),
                                  [], [VA.res]) for a in range(nkb // 8)])
            qTv = sc.qT.rearrange("h r s -> r h s")
            omv = sc.omT.rearrange("h r s -> r h s")
            for qb in range(Sq // QB):
                q0 = qb * QB
                qt = QT[qn % 2]
                ot = oT[qn % 2]
                qn += 1
                S.dma("sp", qt.dom, [(lambda e, qt=qt, q0=q0, qTv=qTv: e.dma_start(out=qt.ap[:], in_=qTv[:, :, q0:q0 + QB]), [], [qt.res])])
                for h in range(NH):
                    po = pos[h % 2]
                    for step in range(nkb + LAG):
                        if step < nkb:
                            kb = step
                            ps, pt = pss[(it + kb) % 3], PT[(it + kb) % 4]
                            S.op("pe", lambda e, ps=ps, kb=kb, h=h, qt=qt: e.matmul(
                                ps.ap[:], lhsT=KT.ap[:, h, kb * 128:(kb + 1) * 128], rhs=qt.ap[:, h, :], start=True, stop=True),
                                 [KT.res, qt.res], [ps.res])
                            S.op("act", lambda e, ps=ps, pt=pt: e.activation(out=pt.ap[:], in_=ps.ap[:], func=AF.Exp, scale=scale),
                                 [ps.res], [pt.res])
                        if step >= LAG:
                            kb = step - LAG
                            pt = PT[(it + kb) % 4]
                            S.op("pe", lambda e, po=po, kb=kb, h=h, pt=pt: e.matmul(
                                po.ap[:], lhsT=VA.ap[:, kb, h * 128:(h + 1) * 128], rhs=pt.ap[:], start=(kb == 0), stop=(kb == nkb - 1)),
                                 [VA.res, pt.res], [po.res], signal=(kb == nkb - 1))
                    it += nkb
                    r = rl[h % 2]
                    S.op("dve", lambda e, po=po, r=r: e.reciprocal(out=r.ap[:], in_=po.ap[64:128, :]), [po.res], [r.res])
                    S.op("dve", lambda e, po=po, r=r, ot=ot, h=h: e.tensor_tensor(out=ot.ap[:, h, :], in0=po.ap[0:64, :], in1=r.ap[:],
                                                                                  op=ALU.mult), [po.res, r.res], [ot.res])
                S.dma("sp", ot.dom, [(lambda e, ot=ot, q0=q0, omv=omv: e.dma_start(out=omv[:, :, q0:q0 + QB], in_=ot.ap[:]), [ot.res], [])])


def mixer_p3(g, seqs):
    S = g.S
    scale = 128.0 ** -0.5
    smax = max(seqs)
    LAG = 2
    with Pass(g, "p3") as P:
        EF = P.sb([128, DH, 256], F32, dma=True)
        E1 = P.sb([64, DH, 128], F32, dma=True)
        E2 = P.sb([64, DH, 128], F32, dma=True)
        ones = P.sb([128, 128], BF16, dma=True)
        load_simple(g, EF, g.d["e_full"])
        load_simple(g, E1, g.d["e_first"])
        load_simple(g, E2, g.d["e_last"])
        load_simple(g, ones, g.d["ones"])
        QTr = P.ring(2, [128, smax], BF16, dma=True)
        KTr = P.ring(2, [128, smax], BF16, dma=True)
        Vm = P.ring(2, [128, smax // 128, 128], BF16, dma=True)
        Vf = P.ring(2, [64, 16, 128], BF16, dma=True)
        Vl = P.ring(2, [64, 16, 128], BF16, dma=True)
        acc = P.sb([128, 2, smax], F32)
        od = P.ring(2, [128, smax], BF16, dma=True)
        lnb = P.sb([128, smax], F32)
        rcp = P.sb([128, smax], F32)
        ex = P.ring(4, [128, 256], F32)
        PT = P.ring(6, [128, 256], BF16)
        pss = [P.bank() for _ in range(3)]
        pacc = [P.bank() for _ in range(4)]
        hi = 0
        ci = 0
        ji = 0
        oi = 0
        for si, Sq in enumerate(seqs):
            sc = g.scr[si]
            for slot in range(4):
                for gi, dl in enumerate(DIL):
                    head = gi * 4 + slot
                    hc = slice(head * 128, (head + 1) * 128)
                    qt, kt, vm, vf, vl = QTr[hi % 2], KTr[hi % 2], Vm[hi % 2], Vf[hi % 2], Vl[hi % 2]
                    hi += 1
                    seg = Sq // dl
                    nqb = seg // 128
                    nmid = nqb - 1
                    S.dma("sp", qt.dom, [(lambda e, qt=qt, head=head, sc=sc, Sq=Sq: e.dma_start(out=qt.ap[:, 0:Sq], in_=sc.dqT[head]), [], [qt.res])])
                    S.dma("sp", kt.dom, [(lambda e, kt=kt, head=head, sc=sc, Sq=Sq: e.dma_start(out=kt.ap[:, 0:Sq], in_=sc.dkT[head]), [], [kt.res])])
                    vfs = sc.dv[0:64 * dl, hc].rearrange("(k r) d -> k r d", r=dl)
                    vls = sc.dv[(seg - 64) * dl:seg * dl, hc].rearrange("(k r) d -> k r d", r=dl)
                    S.dma("sp", vf.dom, [(lambda e, vf=vf, vfs=vfs, dl=dl: e.dma_start(out=vf.ap[:, 0:dl, :], in_=vfs), [], [vf.res])])
                    S.dma("sp", vl.dom, [(lambda e, vl=vl, vls=vls, dl=dl: e.dma_start(out=vl.ap[:, 0:dl, :], in_=vls), [], [vl.res])])
                    if nmid > 0:
                        vms = sc.dv[64 * dl:64 * dl + 128 * dl * nmid, hc].rearrange("(c k r) d -> k r c d", k=128, r=dl)
                        vmd = vm.ap[:, 0:dl * nmid, :].rearrange("k (r c) d -> k r c d", r=dl)
                        items = []
                        for r in range(dl):
                            for c0 in range(0, nmid, 8):
                                c1 = min(nmid, c0 + 8)
                                items.append((lambda e, r=r, c0=c0, c1=c1, vms=vms, vmd=vmd: e.dma_start(
                                    out=vmd[:, r, c0:c1, :], in_=vms[:, r, c0:c1, :]), [], [vm.res]))
                        S.dma("sp", vm.dom, items)
                    pend = []

                    def flush(upto):
                        while pend and pend[0][0] <= upto:
                            pend.pop(0)[1]()
                    idx = 0
                    for r in range(dl):
                        prev = None
                        for c in range(nqb + 1):
                            first, last = (c == 0), (c == nqb)
                            nk = 64 if (first or last) else 128
                            p0 = 0 if first else 128 * c - 64
                            qlo = 128 * (c - 1) if not first else 0
                            nq = 128 if (first or last) else 256
                            if first:
                                vap, vres, em = vf.ap[0:64, r, :], vf.res, E1.ap[:, head, :]
                            elif last:
                                vap, vres, em = vl.ap[0:64, r, :], vl.res, E2.ap[:, head, :]
                            else:
                                vap, vres, em = vm.ap[:, r * nmid + (c - 1), :], vm.res, EF.ap[:, head, :]
                            ps, xx, pt = pss[ci % 3], ex[ci % 4], PT[ci % 6]
                            ci += 1
                            kcols = kt.ap[:, p0 * dl + r:(p0 + nk - 1) * dl + r + 1:dl]
                            qcols = qt.ap[:, qlo * dl + r:(qlo + nq - 1) * dl + r + 1:dl]
                            S.op("pe", lambda e, ps=ps, kcols=kcols, qcols=qcols, nk=nk, nq=nq: e.matmul(
                                ps.ap[0:nk, 0:nq], lhsT=kcols, rhs=qcols, start=True, stop=True), [kt.res, qt.res], [ps.res])
                            S.op("act", lambda e, ps=ps, xx=xx, nk=nk, nq=nq: e.activation(out=xx.ap[0:nk, 0:nq], in_=ps.ap[0:nk, 0:nq],
                                                                                           func=AF.Exp, scale=scale), [ps.res], [xx.res])
                            S.op("pool", lambda e, xx=xx, pt=pt, em=em, nk=nk, nq=nq: e.tensor_tensor(
                                out=pt.ap[0:nk, 0:nq], in0=xx.ap[0:nk, 0:nq], in1=em, op=ALU.mult),
                                 [xx.res, EF.res, E1.res, E2.res], [pt.res])
                            cur = (pt, vap, vres, nk, first)
                            if prev is not None:
                                def job(prev=prev, cur=cur, q=c - 1, r=r, dl=dl, gi=gi):
                                    nonlocal ji
                                    ppt, pvap, pvres, pnk, pfirst = prev
                                    pt, vap, vres, nk, _ = cur
                                    pb = pacc[ji % 4]
                                    ji += 1
                                    acol = 0 if pfirst else 128
                                    S.op("pe", lambda e: e.matmul(pb.ap[:, 0:128], lhsT=pvap, rhs=ppt.ap[0:pnk, acol:acol + 128],
                                                                  start=True, stop=False), [pvres, ppt.res], [pb.res], signal=False)
                                    S.op("pe", lambda e: e.matmul(pb.ap[:, 0:128], lhsT=vap, rhs=pt.ap[0:nk, 0:128],
                                                                  start=False, stop=True), [vres, pt.res], [pb.res], signal=False)
                                    S.op("pe", lambda e: e.matmul(pb.ap[:, 128:256], lhsT=ones.ap[0:pnk, :], rhs=ppt.ap[0:pnk, acol:acol + 128],
                                                                  start=True, stop=False), [ones.res, ppt.res], [pb.res], signal=False)
                                    S.op("pe", lambda e: e.matmul(pb.ap[:, 128:256], lhsT=ones.ap[0:nk, :], rhs=pt.ap[0:nk, 0:128],
                                                                  start=False, stop=True), [ones.res, pt.res], [pb.res])
                                    lo = q * 128 * dl + r
                                    hi_ = (q * 128 + 127) * dl + r + 1
                                    dst = acc.ap[:, :, lo:hi_:dl]
                                    src = pb.ap[:, 0:256].rearrange("p (a q) -> p a q", a=2)
                                    if gi == 0:
                                        S.op("dve", lambda e: e.tensor_copy(out=dst, in_=src), [pb.res], [acc.res])
                                    else:
                                        S.op("dve", lambda e: e.tensor_tensor(out=dst, in0=src, in1=dst, op=ALU.add), [pb.res], [acc.res])
                                pend.append((idx, job))
                            prev = cur
                            flush(idx - LAG)
                            idx += 1
                    flush(idx)
                o = od[oi % 2]
                oi += 1
                S.op("act", lambda e, Sq=Sq: e.activation(out=lnb.ap[:, 0:Sq], in_=acc.ap[:, 1, 0:Sq], func=AF.Ln), [acc.res], [lnb.res])
                S.op("act", lambda e, Sq=Sq: e.activation(out=rcp.ap[:, 0:Sq], in_=lnb.ap[:, 0:Sq], func=AF.Exp, scale=-1.0), [lnb.res], [rcp.res])
                S.op("dve", lambda e, o=o, Sq=Sq: e.tensor_tensor(out=o.ap[:, 0:Sq], in0=acc.ap[:, 0, 0:Sq], in1=rcp.ap[:, 0:Sq], op=ALU.mult),
                     [acc.res, rcp.res], [o.res])
                S.dma("sp", o.dom, [(lambda e, o=o, slot=slot, sc=sc, Sq=Sq: e.dma_start(out=sc.odT[slot], in_=o.ap[:, 0:Sq]), [o.res], [])])


def mixer_p4(g, l, seqs):
    S = g.S
    TB, NT = 512, 4
    d = g.d
    with Pass(g, "p4") as P:
        wbm = P.sb([64, NH, D], BF16, dma=True)
        wbd = P.sb([128, 4, D], BF16, dma=True)
        wo = P.sb([128, KC, D], BF16, dma=True)
        load_w_cast(g, wbm, d["w_branch_mla"][l].rearrange("(h p) f -> p h f", p=64), D)
        load_w_cast(g, wbd, d["w_branch_dil"][l].rearrange("(c p) f -> p c f", p=128), D)
        load_w_cast(g, wo, d["w_out"][l].rearrange("(c p) f -> p c f", p=128), D)
        xin = P.ring(2, [128, NT, D], F32, dma=True)
        xst = [g.S.dma_dom() for _ in range(2)]
        om = P.ring(2, [64, NH, TB], BF16, dma=True)
        od = P.ring(2, [128, 4, TB], BF16, dma=True)
        gt = P.ring(2, [128, 16, TB], BF16, dma=True)
        mT = P.sb([128, KC, TB], BF16)
        mres = [Res() for _ in range(KC)]
        t1 = P.ring(2, [128, TB], F32)
        t2 = P.ring(2, [128, TB], F32)
        pbm = [P.bank(), P.bank()]
        pbd = [P.bank(), P.bank()]
        pys = [P.bank(), P.bank()]
        xv = g.xres.rearrange("(n j p) d -> n p j d", j=NT, p=128)
        t0 = 0
        bn = 0
        for si, Sq in enumerate(seqs):
            sc = g.scr[si]
            omv = sc.omT.rearrange("h r s -> r h s")
            odv = sc.odT.rearrange("i p s -> p i s")
            gv = sc.gT.rearrange("c p s -> p c s")
            for b in range(Sq // TB):
                s0 = b * TB
                gblk = (t0 + s0) // TB
                xb, o1, o2, gg = xin[bn % 2], om[bn % 2], od[bn % 2], gt[bn % 2]
                xs_dom = xst[bn % 2]
                bn += 1
                S.dma("sp", xb.dom, [(lambda e, xb=xb, gblk=gblk: e.dma_start(out=xb.ap[:], in_=xv[gblk]), [], [xb.res])])
                S.dma("sp", o1.dom, [(lambda e, o1=o1, s0=s0, omv=omv: e.dma_start(out=o1.ap[:], in_=omv[:, :, s0:s0 + TB]), [], [o1.res])])
                S.dma("sp", o2.dom, [(lambda e, o2=o2, s0=s0, odv=odv: e.dma_start(out=o2.ap[:], in_=odv[:, :, s0:s0 + TB]), [], [o2.res])])
                S.dma("sp", gg.dom, [(lambda e, gg=gg, s0=s0, gv=gv: e.dma_start(out=gg.ap[:], in_=gv[:, :, s0:s0 + TB]), [], [gg.res])])
                for c in range(KC):
                    p1, p2, a1, a2 = pbm[c % 2], pbd[c % 2], t1[c % 2], t2[c % 2]
                    for h in range(NH):
                        S.op("pe", lambda e, h=h, c=c, p1=p1, o1=o1: e.matmul(
                            p1.ap[:], lhsT=wbm.ap[:, h, c * 128:(c + 1) * 128], rhs=o1.ap[:, h, :], start=(h == 0), stop=(h == NH - 1)),
                             [wbm.res, o1.res], [p1.res], signal=(h == NH - 1))
                    for i in range(4):
                        S.op("pe", lambda e, i=i, c=c, p2=p2, o2=o2: e.matmul(
                            p2.ap[:], lhsT=wbd.ap[:, i, c * 128:(c + 1) * 128], rhs=o2.ap[:, i, :], start=(i == 0), stop=(i == 3)),
                             [wbd.res, o2.res], [p2.res], signal=(i == 3))
                    S.op("dve", lambda e, p1=p1, a1=a1, gg=gg, c=c: e.tensor_tensor(out=a1.ap[:], in0=p1.ap[:], in1=gg.ap[:, c, :], op=ALU.mult),
                         [p1.res, gg.res], [a1.res])
                    S.op("dve", lambda e, p2=p2, a2=a2, gg=gg, c=c: e.tensor_tensor(out=a2.ap[:], in0=p2.ap[:], in1=gg.ap[:, 8 + c, :], op=ALU.mult),
                         [p2.res, gg.res], [a2.res])
                    S.op("pool", lambda e, a1=a1, a2=a2, c=c: e.tensor_tensor(out=mT.ap[:, c, :], in0=a1.ap[:], in1=a2.ap[:], op=ALU.add),
                         [a1.res, a2.res], [mres[c]])
                for j in range(NT):
                    for hf in range(2):
                        py = pys[(j * 2 + hf) % 2]
                        for c in range(KC):
                            S.op("pe", lambda e, c=c, j=j, hf=hf, py=py: e.matmul(
                                py.ap[:], lhsT=mT.ap[:, c, j * 128:(j + 1) * 128], rhs=wo.ap[:, c, hf * 512:(hf + 1) * 512],
                                start=(c == 0), stop=(c == KC - 1)), [mres[c], wo.res], [py.res], signal=(c == KC - 1))
                        S.op("dve", lambda e, j=j, hf=hf, py=py, xb=xb: e.tensor_tensor(
                            out=xb.ap[:, j, hf * 512:(hf + 1) * 512], in0=py.ap[:], in1=xb.ap[:, j, hf * 512:(hf + 1) * 512], op=ALU.add),
                             [py.res], [xb.res])
                S.dma("sp", xs_dom, [(lambda e, xb=xb, gblk=gblk: e.dma_start(out=xv[gblk], in_=xb.ap[:]), [xb.res], [])])
            t0 += Sq


W_NAMES = ["ffn1_norm", "ffn1_w_gate", "ffn1_w_up", "ffn1_w_down", "mix_norm", "q_a_norm", "kv_a_norm",
           "w_branch_mla", "w_branch_dil", "w_out", "ffn2_norm", "ffn2_w_gate", "ffn2_w_up", "ffn2_w_down",
           "final_norm", "w_zA", "w_kpe2", "w_dq", "w_dk", "w_dv", "w_gt", "w_q2", "w_kn", "w_v", "bgate2",
           "ropetab", "e_full", "e_first", "e_last", "ident", "ones"]


def build_program(seqs, depth, shapes, dtypes):
    T = sum(seqs)
    nc = bass.Bass("TRN2", target_bir_lowering=False)
    g = G()
    g.nc = nc
    g.d = {}
    for n in W_NAMES:
        g.d[n] = nc.dram_tensor(n, list(shapes[n]), dtypes[n], kind="ExternalInput").ap()
    x_in = nc.dram_tensor("x", [T, D], F32, kind="ExternalInput").ap()
    y_out = nc.dram_tensor("y", [T, D], F32, kind="ExternalOutput").ap()
    g.xres = nc.dram_tensor("xres", [T, D], F32, kind="Internal").ap()
    g.scr = []
    for si, sq in enumerate(seqs):
        sc = G()
        mk = lambda n, shp: nc.dram_tensor("s%d_%s" % (si, n), shp, BF16, kind="Internal").ap()
        sc.qT = mk("qT", [NH, 96, sq])
        sc.kT = mk("kT", [NH, 96, sq])
        sc.vm = mk("vm", [sq, NH * 128])
        sc.dqT = mk("dqT", [DH, 128, sq])
        sc.dkT = mk("dkT", [DH, 128, sq])
        sc.dv = mk("dv", [sq, DH * 128])
        sc.gT = mk("gT", [16, 128, sq])
        sc.omT = mk("omT", [NH, 64, sq])
        sc.odT = mk("odT", [4, 128, sq])
        g.scr.append(sc)
    with ExitStack() as st:
        S = Sched(nc, st)
        g.S = S
        identt = st.enter_context(nc.sbuf_tensor("ident_sb", [128, 128], BF16))
        epst = st.enter_context(nc.sbuf_tensor("eps_sb", [128, 1], F32))
        g.ident = Slot(identt, S.pool[-1])
        g.eps = Slot(epst)
        S.dma("sp", g.ident.dom, [(lambda e: e.dma_start(out=identt[:], in_=g.d["ident"]), [], [g.ident.res])])
        S.op("pool", lambda e: e.memset(epst[:], EPS), [], [g.eps.res])
        S.pool = S.pool[:-1]
        for l in range(depth):
            ffn_pass(g, l, 1, x_in if l == 0 else g.xres, g.xres, T)
            mixer_p1(g, l, seqs)
            mixer_p2(g, seqs)
            mixer_p3(g, seqs)
            mixer_p4(g, l, seqs)
            ffn_pass(g, l, 2, g.xres, g.xres, T)
        final_pass(g, g.xres, y_out, T)
        S.barrier()
        S.emit()
    g.n_ins = dict(S.n_ins)
    return nc, g


def host_constants():
    pos = np.arange(SMAX, dtype=np.float32)
    inv = (1.0 / (np.float32(10000.0) ** (np.arange(0, 32, 2, dtype=np.float32) / np.float32(32)))).astype(np.float32)
    ang = (pos[:, None] * inv[None, :]).astype(np.float32)
    cos, sin = np.cos(ang).astype(np.float32).T, np.sin(ang).astype(np.float32).T
    rope = np.zeros((32, 2, SMAX), np.float32)
    rope[0:16, 0], rope[16:32, 0] = cos, cos
    rope[0:16, 1], rope[16:32, 1] = -sin, sin
    slopes = (2.0 ** (-8.0 * np.arange(1, DH + 1, dtype=np.float32) / DH)).astype(np.float32)
    k = np.arange(128)[:, None]
    q = np.arange(128)[None, :]
    e_full = np.zeros((128, DH, 256), np.float32)
    e_first = np.zeros((64, DH, 128), np.float32)
    e_last = np.zeros((64, DH, 128), np.float32)
    for h in range(DH):
        dl = DIL[h // 4]
        sl = slopes[h] * dl
        relB = k - q + 64
        e_full[:, h, 0:128] = np.where(k <= q, np.exp(-sl * np.abs(relB).astype(np.float32)), 0.0)
        relA = k - q - 64
        e_full[:, h, 128:256] = np.where(k >= q, np.exp(-sl * np.abs(relA).astype(np.float32)), 0.0)
        k6 = np.arange(64)[:, None]
        rel1 = k6 - q
        e_first[:, h, :] = np.where(np.abs(rel1) <= 64, np.exp(-sl * np.abs(rel1).astype(np.float32)), 0.0)
        rel2 = 64 + k6 - q
        e_last[:, h, :] = np.where(np.abs(rel2) <= 64, np.exp(-sl * np.abs(rel2).astype(np.float32)), 0.0)
    return {"ropetab": rope, "e_full": e_full, "e_first": e_first, "e_last": e_last,
            "ident": np.eye(128, dtype=np.float32).astype(ml_dtypes.bfloat16),
            "ones": np.ones((128, 128), np.float32).astype(ml_dtypes.bfloat16)}


def host_weights(inp):
    f = lambda a: np.ascontiguousarray(np.asarray(a, dtype=np.float32))
    w_in = f(inp["w_in"])
    L = w_in.shape[0]
    out = {}
    for n in ["ffn1_norm", "ffn1_w_gate", "ffn1_w_up", "ffn1_w_down", "mix_norm", "q_a_norm", "kv_a_norm",
              "w_branch_mla", "w_branch_dil", "w_out", "ffn2_norm", "ffn2_w_gate", "ffn2_w_up", "ffn2_w_down", "final_norm"]:
        out[n] = f(inp[n])
    out["w_zA"] = f(w_in[:, :, 0:384])
    z64 = np.zeros((L, w_in.shape[1], 64), np.float32)
    out["w_kpe2"] = f(np.concatenate([z64, w_in[:, :, 384:416], z64, w_in[:, :, 400:416], w_in[:, :, 384:400]], axis=2))
    out["w_dq"] = f(w_in[:, :, 416:1952])
    out["w_dk"] = f(w_in[:, :, 1952:3488])
    out["w_dv"] = f(w_in[:, :, 3488:5024])
    out["w_gt"] = f(w_in[:, :, 5024:7072])
    wq = f(inp["w_q_up"]).reshape(L, 256, NH, 96)
    nope, rope = wq[..., 0:64], wq[..., 64:96]
    out["w_q2"] = f(np.concatenate([nope, rope, np.zeros_like(nope), rope[..., 16:32], rope[..., 0:16]], axis=3).reshape(L, 256, NH * 192))
    wkv = f(inp["w_kv_up"]).reshape(L, 128, NH, 128)
    out["w_kn"] = f(wkv[..., 0:64].reshape(L, 128, NH * 64))
    out["w_v"] = f(wkv[..., 64:128].reshape(L, 128, NH * 64))
    out["bgate2"] = f(f(inp["b_gate"]).reshape(L, 16, 128).transpose(0, 2, 1))
    out.update(host_constants())
    return out


_CACHE = {}


def run_cores(xs, seqs, depth, W, trace=False):
    shapes = {k: v.shape for k, v in W.items()}
    dtypes = {k: (BF16 if v.dtype == ml_dtypes.bfloat16 else F32) for k, v in W.items()}
    key = (tuple(seqs), depth)
    if key not in _CACHE:
        _CACHE[key] = build_program(seqs, depth, shapes, dtypes)
    nc, g = _CACHE[key]
    in_maps = []
    for x in xs:
        m = dict(W)
        m["x"] = x
        in_maps.append(m)
    res = run_bass_kernel_spmd(nc, in_maps, core_ids=list(range(len(xs))), trace=trace)
    return [r["y"] for r in res.results], res


def kernel(**inp):
    xp = np.asarray(inp["x_prompt"], dtype=np.float32)
    xs_ = np.asarray(inp["x_sample"], dtype=np.float32)
    W = host_weights(inp)
    depth = W["w_zA"].shape[0]
    ncore = 8
    pb, sbn = xp.shape[0] // ncore, xs_.shape[0] // ncore
    seqs = [xp.shape[1]] * pb + [xs_.shape[1]] * sbn
    xs = []
    for c in range(ncore):
        xs.append(np.ascontiguousarray(np.concatenate(
            [xp[c * pb:(c + 1) * pb].reshape(-1, D), xs_[c * sbn:(c + 1) * sbn].reshape(-1, D)], axis=0)))
    ys, _ = run_cores(xs, seqs, depth, W)
    np_ = pb * xp.shape[1]
    y_prompt = np.concatenate([y[:np_].reshape(pb, xp.shape[1], D) for y in ys], axis=0)
    y_sample = np.concatenate([y[np_:].reshape(sbn, xs_.shape[1], D) for y in ys], axis=0)
    return (y_prompt.astype(np.float32), y_sample.astype(np.float32))
```

```python
import numpy as np
import ml_dtypes
from contextlib import ExitStack
import concourse.bass as bass
import concourse.mybir as mybir
from concourse.bass_utils import run_bass_kernel_spmd

F32 = mybir.dt.float32
BF16 = mybir.dt.bfloat16
AF = mybir.ActivationFunctionType
ALU = mybir.AluOpType

import os
SELF_SYNC = os.environ.get("K_SELF_SYNC", "1") == "1"
D = 1024
KC = 8
FF = 2816
FC = 22
NH = 8
DH = 12
DIL = (1, 4, 16)
SMAX = 4096
EPS = 1e-6


class Dom:
    def __init__(self, sem, unit, name):
        self.sem, self.unit, self.count, self.name = sem, unit, 0, name


class Res:
    __slots__ = ("w", "r", "excl")

    def __init__(self, excl=False):
        self.w = None
        self.r = {}
        self.excl = excl


class Sched:
    ENG = ("pe", "act", "dve", "pool", "sp")

    def __init__(self, nc, stack, n_dma=56):
        self.nc = nc
        self.ops = {k: [] for k in self.ENG}
        self.doms = {}
        for k in ("pe", "act", "dve", "pool"):
            self.doms[k] = Dom(stack.enter_context(nc.semaphore("s_" + k)), 1, k)
        self.waited = {k: {} for k in self.ENG}
        self.pool = [Dom(stack.enter_context(nc.semaphore("d%d" % i)), 16, "d%d" % i) for i in range(n_dma)]
        self.pool_i = 0
        self.n_ins = {k: 0 for k in self.ENG}

    def new_pass(self):
        self.pool_i = 0

    def dma_dom(self):
        d = self.pool[self.pool_i]
        self.pool_i += 1
        return d

    def _need(self, eng, tok, waits, own=None):
        if tok is None:
            return
        dom, val = tok
        if dom is own and dom.unit == 16:
            return
        if dom is self.doms.get(eng):
            if eng == "pe" or not SELF_SYNC:
                return
        if self.waited[eng].get(dom, 0) >= val:
            return
        self.waited[eng][dom] = val
        waits.append((dom.sem, val * dom.unit))

    def dma(self, eng, dom, items):
        final = dom.count + len(items)
        tok = None
        for fn, reads, writes in items:
            tok = self.op(eng, fn, reads, writes, dom=dom, tokval=final)
        return tok

    def op(self, eng, fn, reads=(), writes=(), dom=None, signal=True, tokval=None):
        waits = []
        if any(r.excl for r in reads):
            writes = list(writes) + [r for r in reads if r.excl]
            reads = [r for r in reads if not r.excl]
        d = dom if dom is not None else self.doms[eng]
        for r in reads:
            self._need(eng, r.w, waits, d)
        for w in writes:
            self._need(eng, w.w, waits, d)
            for dd, v in w.r.items():
                self._need(eng, (dd, v), waits, d)
        if signal:
            d.count += 1
        tok = (d, d.count if signal else d.count + 1)
        if tokval is not None:
            tok = (d, tokval)
        for r in reads:
            if r.r.get(d, 0) < tok[1]:
                r.r[d] = tok[1]
        for w in writes:
            w.w = tok
            w.r = {}
        sem, unit = d.sem, d.unit
        self.n_ins[eng] += 1

        def run(e, waits=waits, fn=fn, signal=signal, sem=sem, unit=unit):
            for s, v in waits:
                e.wait_ge(s, v)
            ins = fn(e)
            if signal:
                ins.then_inc(sem, unit)
        self.ops[eng].append(run)
        return tok

    def barrier(self, engs=None):
        alld = list(self.doms.values()) + self.pool
        for eng in (engs or self.ENG):
            waits = []
            for d in alld:
                if d.count > 0:
                    self._need(eng, (d, d.count), waits)

            def run(e, waits=waits):
                for s, v in waits:
                    e.wait_ge(s, v)
            self.ops[eng].append(run)

    def emit(self):
        nc, ops = self.nc, self.ops
        with nc.Block() as block:
            @block.tensor
            def _(e):
                for f in ops["pe"]:
                    f(e)

            @block.scalar
            def _(e):
                for f in ops["act"]:
                    f(e)

            @block.vector
            def _(e):
                for f in ops["dve"]:
                    f(e)

            @block.gpsimd
            def _(e):
                for f in ops["pool"]:
                    f(e)

            @block.sync
            def _(e):
                for f in ops["sp"]:
                    f(e)


class Slot:
    def __init__(self, ap, dom=None, excl=False):
        self.ap, self.res, self.dom = ap, Res(excl), dom


class Pass:
    _uid = [0]

    def __init__(self, G, name):
        Pass._uid[0] += 1
        self.G, self.name = G, "%s_%d" % (name, Pass._uid[0])
        self.st = ExitStack()
        self.n = 0

    def __enter__(self):
        self.st.__enter__()
        self.G.S.barrier()
        self.G.S.new_pass()
        self.banks = []
        return self

    def __exit__(self, *a):
        return self.st.__exit__(*a)

    def sb(self, shape, dt, dma=False):
        self.n += 1
        t = self.st.enter_context(self.G.nc.sbuf_tensor("%s_%d" % (self.name, self.n), list(shape), dt))
        return Slot(t, self.G.S.dma_dom() if dma else None)

    def ring(self, n, shape, dt, dma=False):
        return [self.sb(shape, dt, dma) for _ in range(n)]

    def bank(self, dt=F32):
        self.n += 1
        shape = [128, 512] if dt == F32 else [128, 1024]
        t = self.st.enter_context(self.G.nc.psum_tensor("%s_p%d" % (self.name, self.n), shape, dt))
        return Slot(t, excl=True)


class G:
    pass


def load_w_cast(g, slot, src, ncols, split=1):
    S = g.S
    nch = src.shape[1]
    step = ncols // split
    items = []
    for c in range(nch):
        for h in range(split):
            items.append((lambda e, c=c, h=h: e.dma_start(out=slot.ap[:, c, h * step:(h + 1) * step],
                                                          in_=src[:, c, h * step:(h + 1) * step]),
                          [], [slot.res]))
    S.dma("pool", slot.dom, items)


def load_simple(g, slot, src, eng="sp", dst=None):
    d = slot.ap[:] if dst is None else dst
    g.S.dma(eng, slot.dom, [(lambda e: e.dma_start(out=d, in_=src), [], [slot.res])])


def rms_stats(g, xs, ntile, width, ss, rt, rstd, junk):
    S = g.S
    for j in range(ntile):
        ap, res = xs(j)
        S.op("act", lambda e, ap=ap, j=j: e.activation(out=junk.ap[:, 0:width], in_=ap, func=AF.Square,
                                                        accum_out=ss.ap[:, j:j + 1]),
             [res], [junk.res, ss.res])
    S.op("act", lambda e: e.activation(out=rt.ap[:, 0:ntile], in_=ss.ap[:, 0:ntile], func=AF.Sqrt,
                                       scale=1.0 / width, bias=g.eps.ap[:, 0:1]),
         [ss.res, g.eps.res], [rt.res])
    S.op("dve", lambda e: e.reciprocal(out=rstd.ap[:, 0:ntile], in_=rt.ap[:, 0:ntile]), [rt.res], [rstd.res])


def transpose_tile(g, src, nchunk, tpb, dst_ap, dst_res, cp_eng):
    S = g.S
    for c in range(nchunk):
        S.op("pe", lambda e, c=c: e.transpose(out=tpb.ap[:, c * 128:(c + 1) * 128],
                                              in_=src.ap[:, c * 128:(c + 1) * 128], identity=g.ident.ap[:]),
             [src.res, g.ident.res], [tpb.res], signal=(c == nchunk - 1))
    view = tpb.ap[:, 0:nchunk * 128].rearrange("p (c t) -> p c t", c=nchunk)
    if cp_eng == "act":
        S.op("act", lambda e: e.activation(out=dst_ap, in_=view, func=AF.Copy), [tpb.res], [dst_res])
    else:
        S.op("dve", lambda e: e.tensor_copy(out=dst_ap, in_=view), [tpb.res], [dst_res])


def norm_compute(g, xb, ntile, gb, xn_ring, stats):
    S = g.S
    ss, rt, rstd, junk = stats
    rms_stats(g, lambda j: (xb.ap[:, j, :], xb.res), ntile, D, ss, rt, rstd, junk)
    for j in range(ntile):
        xn = xn_ring[j % len(xn_ring)]
        S.op("dve", lambda e, j=j, xn=xn: e.scalar_tensor_tensor(out=xn.ap[:], in0=xb.ap[:, j, :],
                                                                 scalar=rstd.ap[:, j:j + 1], in1=gb.ap[:],
                                                                 op0=ALU.mult, op1=ALU.mult),
             [xb.res, rstd.res, gb.res], [xn.res])


def norm_transpose(g, ntile, xn_ring, tpbs, xnT):
    for j in range(ntile):
        xn = xn_ring[j % len(xn_ring)]
        tpb = tpbs[j % len(tpbs)]
        transpose_tile(g, xn, KC, tpb, xnT.ap[:, :, j * 128:(j + 1) * 128], xnT.res, "act")


def ffn_pass(g, l, which, src, dst, T):
    S = g.S
    TB, NT = 256, 2
    wg_d, wu_d, wd_d, gn_d = (g.d["ffn%d_w_gate" % which], g.d["ffn%d_w_up" % which],
                              g.d["ffn%d_w_down" % which], g.d["ffn%d_norm" % which])
    with Pass(g, "f%d%d" % (l, which)) as P:
        wg = P.sb([128, KC, FF], BF16, dma=True)
        wu = P.sb([128, KC, FF], BF16, dma=True)
        wd = P.sb([128, FC, D], BF16, dma=True)
        gb = P.sb([128, D], F32, dma=True)
        load_w_cast(g, wg, wg_d[l].rearrange("(c p) f -> p c f", p=128), FF, split=2)
        load_w_cast(g, wu, wu_d[l].rearrange("(c p) f -> p c f", p=128), FF, split=2)
        load_w_cast(g, wd, wd_d[l].rearrange("(c p) f -> p c f", p=128), D, split=1)
        load_simple(g, gb, gn_d[l].partition_broadcast(128))
        xin = P.ring(2, [128, NT, D], F32, dma=True)
        xst = [P.G.S.dma_dom() for _ in range(2)]
        xn_ring = P.ring(2, [128, D], BF16)
        xnT_ring = P.ring(2, [128, KC, TB], BF16)
        hT = P.sb([128, FC, TB], BF16)
        hres = [Res() for _ in range(FC)]
        sg_ring = P.ring(2, [128, TB], F32)
        junk = P.sb([128, D], BF16)
        stats = [(P.sb([128, 4], F32), P.sb([128, 4], F32), P.sb([128, 4], F32), junk) for _ in range(2)]
        tpbs = [P.bank(BF16), P.bank(BF16)]
        pgs = [P.bank(), P.bank()]
        pus = [P.bank(), P.bank()]
        pys = [P.bank(), P.bank()]
        srcv = src.rearrange("(n j p) d -> n p j d", j=NT, p=128)
        dstv = dst.rearrange("(n j p) d -> n p j d", j=NT, p=128)
        nblk = T // TB

        def pre(b):
            xb = xin[b % 2]
            S.dma("sp", xb.dom, [(lambda e, xb=xb, b=b: e.dma_start(out=xb.ap[:], in_=srcv[b]), [], [xb.res])])
            norm_compute(g, xb, NT, gb, xn_ring, stats[b % 2])
        pre(0)
        norm_transpose(g, NT, xn_ring, tpbs, xnT_ring[0])
        for b in range(nblk):
            xb = xin[b % 2]
            xnT = xnT_ring[b % 2]
            if b + 1 < nblk:
                pre(b + 1)
            for f in range(FC):
                pg, pu, sg = pgs[f % 2], pus[f % 2], sg_ring[f % 2]
                for k in range(KC):
                    S.op("pe", lambda e, k=k, f=f, pg=pg, xnT=xnT: e.matmul(
                        pg.ap[:, 0:TB], lhsT=wg.ap[:, k, f * 128:(f + 1) * 128], rhs=xnT.ap[:, k, :],
                        start=(k == 0), stop=(k == KC - 1)), [wg.res, xnT.res], [pg.res], signal=(k == KC - 1))
                for k in range(KC):
                    S.op("pe", lambda e, k=k, f=f, pu=pu, xnT=xnT: e.matmul(
                        pu.ap[:, 0:TB], lhsT=wu.ap[:, k, f * 128:(f + 1) * 128], rhs=xnT.ap[:, k, :],
                        start=(k == 0), stop=(k == KC - 1)), [wu.res, xnT.res], [pu.res], signal=(k == KC - 1))
                S.op("act", lambda e, pg=pg, sg=sg: e.activation(out=sg.ap[:], in_=pg.ap[:, 0:TB], func=AF.Silu),
                     [pg.res], [sg.res])
                S.op("dve", lambda e, pu=pu, sg=sg, f=f: e.tensor_tensor(out=hT.ap[:, f, :], in0=pu.ap[:, 0:TB],
                                                                          in1=sg.ap[:], op=ALU.mult),
                     [pu.res, sg.res], [hres[f]])
            if b + 1 < nblk:
                norm_transpose(g, NT, xn_ring, tpbs, xnT_ring[(b + 1) % 2])
            for j in range(NT):
                for hf in range(2):
                    py = pys[(j * 2 + hf) % 2]
                    for f in range(FC):
                        S.op("pe", lambda e, f=f, j=j, hf=hf, py=py: e.matmul(
                            py.ap[:], lhsT=hT.ap[:, f, j * 128:(j + 1) * 128], rhs=wd.ap[:, f, hf * 512:(hf + 1) * 512],
                            start=(f == 0), stop=(f == FC - 1)), [hres[f], wd.res], [py.res], signal=(f == FC - 1))
                    S.op("dve", lambda e, j=j, hf=hf, py=py, xb=xb: e.scalar_tensor_tensor(
                        out=xb.ap[:, j, hf * 512:(hf + 1) * 512], in0=py.ap[:], scalar=0.5,
                        in1=xb.ap[:, j, hf * 512:(hf + 1) * 512], op0=ALU.mult, op1=ALU.add),
                         [py.res], [xb.res])
            S.dma("sp", xst[b % 2], [(lambda e, xb=xb, b=b: e.dma_start(out=dstv[b], in_=xb.ap[:]), [xb.res], [])])


def final_pass(g, src, dst, T):
    S = g.S
    TB, NT = 256, 2
    with Pass(g, "fin") as P:
        gb = P.sb([128, D], F32, dma=True)
        load_simple(g, gb, g.d["final_norm"].partition_broadcast(128))
        xin = P.ring(3, [128, NT, D], F32, dma=True)
        xo = P.ring(3, [128, NT, D], F32, dma=True)
        junk = P.sb([128, D], BF16)
        stats = [(P.sb([128, 4], F32), P.sb([128, 4], F32), P.sb([128, 4], F32), junk) for _ in range(3)]
        srcv = src.rearrange("(n j p) d -> n p j d", j=NT, p=128)
        dstv = dst.rearrange("(n j p) d -> n p j d", j=NT, p=128)
        toks = []
        for b in range(T // TB):
            xb, ob = xin[b % 3], xo[b % 3]
            ss, rt, rstd, _ = stats[b % 3]
            S.dma("sp", xb.dom, [(lambda e, xb=xb, b=b: e.dma_start(out=xb.ap[:], in_=srcv[b]), [], [xb.res])])
            rms_stats(g, lambda j: (xb.ap[:, j, :], xb.res), NT, D, ss, rt, rstd, junk)
            for j in range(NT):
                S.op("dve", lambda e, j=j, xb=xb, ob=ob, rstd=rstd: e.scalar_tensor_tensor(
                    out=ob.ap[:, j, :], in0=xb.ap[:, j, :], scalar=rstd.ap[:, j:j + 1], in1=gb.ap[:],
                    op0=ALU.mult, op1=ALU.mult), [xb.res, rstd.res, gb.res], [ob.res])
            S.dma("sp", ob.dom, [(lambda e, ob=ob, b=b: e.dma_start(out=dstv[b], in_=ob.ap[:]), [ob.res], [])])


def mixer_p1(g, l, seqs):
    S = g.S
    TB, NT = 256, 2
    d = g.d
    with Pass(g, "p1") as P:
        wzA = P.sb([128, KC, 384], BF16, dma=True)
        wkp = P.sb([128, KC, 192], BF16, dma=True)
        wdq = P.sb([128, KC, 1536], BF16, dma=True)
        wdk = P.sb([128, KC, 1536], BF16, dma=True)
        wdv = P.sb([128, KC, 1536], BF16, dma=True)
        wgt = P.sb([128, KC, 2048], BF16, dma=True)
        wq = P.sb([128, 2, NH * 192], BF16, dma=True)
        wkn = P.sb([128, 1, NH * 64], BF16, dma=True)
        wv = P.sb([128, 1, 512], BF16, dma=True)
        cv = lambda a: a[l].rearrange("(c p) f -> p c f", p=128)
        load_w_cast(g, wzA, cv(d["w_zA"]), 384)
        load_w_cast(g, wkp, cv(d["w_kpe2"]), 192)
        load_w_cast(g, wdq, cv(d["w_dq"]), 1536)
        load_w_cast(g, wdk, cv(d["w_dk"]), 1536)
        load_w_cast(g, wdv, cv(d["w_dv"]), 1536)
        load_w_cast(g, wgt, cv(d["w_gt"]), 2048)
        load_w_cast(g, wq, cv(d["w_q2"]), NH * 192)
        load_w_cast(g, wkn, cv(d["w_kn"]), NH * 64)
        load_w_cast(g, wv, cv(d["w_v"]), 512)
        gb = P.sb([128, D], F32, dma=True)
        gq = P.sb([128, 256], F32, dma=True)
        gk = P.sb([128, 128], F32, dma=True)
        bg = P.sb([128, 16], F32, dma=True)
        load_simple(g, gb, d["mix_norm"][l].partition_broadcast(128))
        load_simple(g, gq, d["q_a_norm"][l].partition_broadcast(128))
        load_simple(g, gk, d["kv_a_norm"][l].partition_broadcast(128))
        load_simple(g, bg, d["bgate2"][l])
        xin = P.ring(2, [128, NT, D], F32, dma=True)
        ctab = P.ring(2, [96, 2, TB], F32, dma=True)
        xn_ring = P.ring(2, [128, D], BF16)
        xnT_ring = P.ring(2, [128, KC, TB], BF16)
        junk = P.sb([128, D], BF16)
        stats = [(P.sb([128, 4], F32), P.sb([128, 4], F32), P.sb([128, 4], F32), junk) for _ in range(2)]
        st2 = [(P.sb([128, 4], F32), P.sb([128, 4], F32), P.sb([128, 4], F32), junk) for _ in range(2)]
        cn_ring = P.ring(2, [128, 384], BF16)
        cT = P.sb([128, 3, TB], BF16)
        kpeT = P.sb([96, TB], BF16)
        rt1 = P.ring(2, [96, TB], F32)
        rt2 = P.ring(2, [96, TB], F32)
        QT = P.ring(2, [96, NH, TB], BF16, dma=True)
        KT = P.ring(2, [96, NH, TB], BF16, dma=True)
        vst = P.ring(2, [128, NH, 128], BF16, dma=True)
        pair = P.ring(8, [128, 2, TB], BF16, dma=True)
        dvst = P.ring(6, [128, 512], BF16, dma=True)
        for v in vst:
            S.op("pool", lambda e, v=v: e.memset(v.ap[:, :, 64:128], 1.0), [], [v.res])
        tpbs = [P.bank(BF16), P.bank(BF16)]
        pbk = [P.bank() for _ in range(6)]
        nb = [0]

        def nextbank():
            nb[0] += 1
            return pbk[nb[0] % 6]
        pi = [0]
        di = [0]
        xv = g.xres.rearrange("(n j p) d -> n p j d", j=NT, p=128)
        blocks = []
        t0 = 0
        for si, Sq in enumerate(seqs):
            for b in range(Sq // TB):
                blocks.append((si, b * TB, (t0 + b * TB) // TB))
            t0 += Sq

        def st_norm(bi):
            si, s0, gblk = blocks[bi]
            b = bi
            sc = g.scr[si]
            xb, ct, xnT = xin[b % 2], ctab[b % 2], xnT_ring[b % 2]
            qTv = sc.qT.rearrange("h r s -> r h s")
            kTv = sc.kT.rearrange("h r s -> r h s")
            vmv = sc.vm.rearrange("(n p) (h c) -> n p h c", p=128, c=128)
            dvv = sc.dv.rearrange("(n p) c -> n p c", p=128)
            S.dma("sp", xb.dom, [(lambda e, xb=xb, gblk=gblk: e.dma_start(out=xb.ap[:], in_=xv[gblk]), [], [xb.res])])
            S.dma("sp", ct.dom, [(lambda e, ct=ct, s0=s0: e.dma_start(out=ct.ap[64:96], in_=d["ropetab"][:, :, s0:s0 + TB]),
                                  [], [ct.res])])
            norm_compute(g, xb, NT, gb, xn_ring, stats[b % 2])

        def st_normT(bi):
            norm_transpose(g, NT, xn_ring, tpbs, xnT_ring[bi % 2])

        def st_zA(bi):
            si, s0, gblk = blocks[bi]
            b = bi
            sc = g.scr[si]
            xb, ct, xnT = xin[b % 2], ctab[b % 2], xnT_ring[b % 2]
            qTv = sc.qT.rearrange("h r s -> r h s")
            kTv = sc.kT.rearrange("h r s -> r h s")
            vmv = sc.vm.rearrange("(n p) (h c) -> n p h c", p=128, c=128)
            dvv = sc.dv.rearrange("(n p) c -> n p c", p=128)
            ss2, rtt2, rstd2, _ = st2[b % 2]
            for j in range(NT):
                pz = nextbank()
                cn = cn_ring[j % 2]
                for k in range(KC):
                    S.op("pe", lambda e, k=k, j=j, pz=pz, xnT=xnT: e.matmul(
                        pz.ap[:, 0:384], lhsT=xnT.ap[:, k, j * 128:(j + 1) * 128], rhs=wzA.ap[:, k, :],
                        start=(k == 0), stop=(k == KC - 1)), [xnT.res, wzA.res], [pz.res], signal=(k == KC - 1))
                S.op("act", lambda e, pz=pz, ss2=ss2: e.activation(out=junk.ap[:, 0:256], in_=pz.ap[:, 0:256], func=AF.Square,
                                                                  accum_out=ss2.ap[:, 0:1]), [pz.res], [junk.res, ss2.res])
                S.op("act", lambda e, pz=pz, ss2=ss2: e.activation(out=junk.ap[:, 0:128], in_=pz.ap[:, 256:384], func=AF.Square,
                                                                  accum_out=ss2.ap[:, 1:2]), [pz.res], [junk.res, ss2.res])
                S.op("act", lambda e, ss2=ss2, rtt2=rtt2: e.activation(out=rtt2.ap[:, 0:1], in_=ss2.ap[:, 0:1], func=AF.Sqrt,
                                                                      scale=1.0 / 256, bias=g.eps.ap[:, 0:1]),
                     [ss2.res, g.eps.res], [rtt2.res])
                S.op("act", lambda e, ss2=ss2, rtt2=rtt2: e.activation(out=rtt2.ap[:, 1:2], in_=ss2.ap[:, 1:2], func=AF.Sqrt,
                                                                      scale=1.0 / 128, bias=g.eps.ap[:, 0:1]),
                     [ss2.res, g.eps.res], [rtt2.res])
                S.op("dve", lambda e, rtt2=rtt2, rstd2=rstd2: e.reciprocal(out=rstd2.ap[:, 0:2], in_=rtt2.ap[:, 0:2]),
                     [rtt2.res], [rstd2.res])
                S.op("dve", lambda e, pz=pz, cn=cn, rstd2=rstd2: e.scalar_tensor_tensor(
                    out=cn.ap[:, 0:256], in0=pz.ap[:, 0:256], scalar=rstd2.ap[:, 0:1], in1=gq.ap[:],
                    op0=ALU.mult, op1=ALU.mult), [pz.res, rstd2.res, gq.res], [cn.res])
                S.op("dve", lambda e, pz=pz, cn=cn, rstd2=rstd2: e.scalar_tensor_tensor(
                    out=cn.ap[:, 256:384], in0=pz.ap[:, 256:384], scalar=rstd2.ap[:, 1:2], in1=gk.ap[:],
                    op0=ALU.mult, op1=ALU.mult), [pz.res, rstd2.res, gk.res], [cn.res])

        def st_cT(bi):
            for j in range(NT):
                transpose_tile(g, cn_ring[j % 2], 3, tpbs[j % 2], cT.ap[:, :, j * 128:(j + 1) * 128], cT.res, "dve")

        def st_mla(bi):
            si, s0, gblk = blocks[bi]
            b = bi
            sc = g.scr[si]
            xb, ct, xnT = xin[b % 2], ctab[b % 2], xnT_ring[b % 2]
            qTv = sc.qT.rearrange("h r s -> r h s")
            kTv = sc.kT.rearrange("h r s -> r h s")
            vmv = sc.vm.rearrange("(n p) (h c) -> n p h c", p=128, c=128)
            dvv = sc.dv.rearrange("(n p) c -> n p c", p=128)
            pk = nextbank()
            for half in range(2):
                for k in range(KC):
                    S.op("pe", lambda e, k=k, half=half, pk=pk, xnT=xnT: e.matmul(
                        pk.ap[0:96, half * TB:(half + 1) * TB], lhsT=wkp.ap[:, k, half * 96:(half + 1) * 96],
                        rhs=xnT.ap[:, k, :], start=(k == 0), stop=(k == KC - 1)),
                         [xnT.res, wkp.res], [pk.res], signal=(k == KC - 1))
            a1, a2 = rt1[0], rt2[0]
            S.op("dve", lambda e, pk=pk, a1=a1, ct=ct: e.tensor_tensor(out=a1.ap[64:96], in0=pk.ap[64:96, 0:TB], in1=ct.ap[64:96, 0, :],
                                                                      op=ALU.mult), [pk.res, ct.res], [a1.res])
            S.op("dve", lambda e, pk=pk, a2=a2, ct=ct: e.tensor_tensor(out=a2.ap[64:96], in0=pk.ap[64:96, TB:2 * TB], in1=ct.ap[64:96, 1, :],
                                                                      op=ALU.mult), [pk.res, ct.res], [a2.res])
            S.op("pool", lambda e, a1=a1, a2=a2: e.tensor_tensor(out=kpeT.ap[64:96], in0=a1.ap[64:96], in1=a2.ap[64:96], op=ALU.add),
                 [a1.res, a2.res], [kpeT.res])
            qt, kt = QT[b % 2], KT[b % 2]
            for h in range(NH):
                pq = nextbank()
                for c in range(2):
                    S.op("pe", lambda e, c=c, h=h, pq=pq: e.matmul(
                        pq.ap[0:96, 0:TB], lhsT=wq.ap[:, c, h * 192:h * 192 + 96], rhs=cT.ap[:, c, :],
                        start=(c == 0), stop=(c == 1)), [wq.res, cT.res], [pq.res], signal=False)
                for c in range(2):
                    S.op("pe", lambda e, c=c, h=h, pq=pq: e.matmul(
                        pq.ap[0:96, TB:2 * TB], lhsT=wq.ap[:, c, h * 192 + 96:h * 192 + 192], rhs=cT.ap[:, c, :],
                        start=(c == 0), stop=(c == 1)), [wq.res, cT.res], [pq.res], signal=(c == 1))
                a1, a2 = rt1[h % 2], rt2[h % 2]
                S.op("act", lambda e, pq=pq, qt=qt, h=h: e.activation(out=qt.ap[0:64, h, :], in_=pq.ap[0:64, 0:TB], func=AF.Copy),
                     [pq.res], [qt.res])
                S.op("dve", lambda e, pq=pq, a1=a1, ct=ct: e.tensor_tensor(out=a1.ap[64:96], in0=pq.ap[64:96, 0:TB], in1=ct.ap[64:96, 0, :],
                                                                          op=ALU.mult), [pq.res, ct.res], [a1.res])
                S.op("dve", lambda e, pq=pq, a2=a2, ct=ct: e.tensor_tensor(out=a2.ap[64:96], in0=pq.ap[64:96, TB:2 * TB], in1=ct.ap[64:96, 1, :],
                                                                          op=ALU.mult), [pq.res, ct.res], [a2.res])
                S.op("pool", lambda e, a1=a1, a2=a2, qt=qt, h=h: e.tensor_tensor(out=qt.ap[64:96, h, :], in0=a1.ap[64:96], in1=a2.ap[64:96],
                                                                                op=ALU.add), [a1.res, a2.res], [qt.res])
                if h % 2 == 0:
                    pkn = nextbank()
                S.op("pe", lambda e, h=h, pkn=pkn: e.matmul(
                    pkn.ap[0:64, (h % 2) * TB:(h % 2 + 1) * TB], lhsT=wkn.ap[:, 0, h * 64:(h + 1) * 64], rhs=cT.ap[:, 2, :],
                    start=True, stop=True), [wkn.res, cT.res], [pkn.res])
                if h % 2 == 1:
                    S.op("act", lambda e, pkn=pkn, kt=kt, h=h: e.activation(
                        out=kt.ap[0:64, h - 1:h + 1, :], in_=pkn.ap[0:64, :].rearrange("p (a t) -> p a t", a=2),
                        func=AF.Copy), [pkn.res], [kt.res])
                S.op("pool", lambda e, kt=kt, h=h: e.tensor_copy(out=kt.ap[64:96, h, :], in_=kpeT.ap[64:96]), [kpeT.res], [kt.res])
            S.dma("sp", qt.dom, [(lambda e, qt=qt, s0=s0: e.dma_start(out=qTv[:, :, s0:s0 + TB], in_=qt.ap[:]), [qt.res], [])])
            S.dma("sp", kt.dom, [(lambda e, kt=kt, s0=s0: e.dma_start(out=kTv[:, :, s0:s0 + TB], in_=kt.ap[:]), [kt.res], [])])
            for j in range(NT):
                pv = nextbank()
                vs = vst[j % 2]
                S.op("pe", lambda e, j=j, pv=pv: e.matmul(pv.ap[:], lhsT=cT.ap[:, 2, j * 128:(j + 1) * 128], rhs=wv.ap[:, 0, :],
                                                         start=True, stop=True), [cT.res, wv.res], [pv.res])
                S.op("act", lambda e, pv=pv, vs=vs: e.activation(out=vs.ap[:, :, 0:64],
                                                                in_=pv.ap[:].rearrange("p (h c) -> p h c", c=64), func=AF.Copy),
                     [pv.res], [vs.res])
                S.dma("sp", vs.dom, [(lambda e, vs=vs, j=j, s0=s0: e.dma_start(out=vmv[(s0 // 128) + j], in_=vs.ap[:]), [vs.res], [])])

        def st_dqk(bi):
            si, s0, gblk = blocks[bi]
            b = bi
            sc = g.scr[si]
            xb, ct, xnT = xin[b % 2], ctab[b % 2], xnT_ring[b % 2]
            qTv = sc.qT.rearrange("h r s -> r h s")
            kTv = sc.kT.rearrange("h r s -> r h s")
            vmv = sc.vm.rearrange("(n p) (h c) -> n p h c", p=128, c=128)
            dvv = sc.dv.rearrange("(n p) c -> n p c", p=128)
            for which, wsb, dstT in ((0, wdq, sc.dqT), (1, wdk, sc.dkT)):
                dview = dstT.rearrange("h p s -> p h s")
                for hp in range(DH // 2):
                    pb = nextbank()
                    for a in range(2):
                        hc = hp * 2 + a
                        for k in range(KC):
                            S.op("pe", lambda e, k=k, hc=hc, a=a, pb=pb, wsb=wsb, xnT=xnT: e.matmul(
                                pb.ap[:, a * TB:(a + 1) * TB], lhsT=wsb.ap[:, k, hc * 128:(hc + 1) * 128], rhs=xnT.ap[:, k, :],
                                start=(k == 0), stop=(k == KC - 1)), [wsb.res, xnT.res], [pb.res],
                                 signal=(k == KC - 1 and a == 1))
                    pr = pair[pi[0] % 8]
                    pi[0] += 1
                    if hp % 2 == 0:
                        S.op("act", lambda e, pb=pb, pr=pr: e.activation(out=pr.ap[:], in_=pb.ap[:].rearrange("p (a t) -> p a t", a=2),
                                                                        func=AF.Copy), [pb.res], [pr.res])
                    else:
                        S.op("dve", lambda e, pb=pb, pr=pr: e.tensor_copy(out=pr.ap[:], in_=pb.ap[:].rearrange("p (a t) -> p a t", a=2)),
                             [pb.res], [pr.res])
                    S.dma("sp", pr.dom, [(lambda e, pr=pr, hp=hp, dview=dview, s0=s0: e.dma_start(
                        out=dview[:, 2 * hp:2 * hp + 2, s0:s0 + TB], in_=pr.ap[:]), [pr.res], [])])

        def st_dv(bi):
            si, s0, gblk = blocks[bi]
            b = bi
            sc = g.scr[si]
            xb, ct, xnT = xin[b % 2], ctab[b % 2], xnT_ring[b % 2]
            qTv = sc.qT.rearrange("h r s -> r h s")
            kTv = sc.kT.rearrange("h r s -> r h s")
            vmv = sc.vm.rearrange("(n p) (h c) -> n p h c", p=128, c=128)
            dvv = sc.dv.rearrange("(n p) c -> n p c", p=128)
            for j in range(NT):
                for m in range(3):
                    pb = nextbank()
                    for k in range(KC):
                        S.op("pe", lambda e, k=k, j=j, m=m, pb=pb, xnT=xnT: e.matmul(
                            pb.ap[:], lhsT=xnT.ap[:, k, j * 128:(j + 1) * 128], rhs=wdv.ap[:, k, m * 512:(m + 1) * 512],
                            start=(k == 0), stop=(k == KC - 1)), [wdv.res, xnT.res], [pb.res], signal=(k == KC - 1))
                    dvs = dvst[di[0] % 6]
                    di[0] += 1
                    if m % 2 == 0:
                        S.op("dve", lambda e, pb=pb, dvs=dvs: e.tensor_copy(out=dvs.ap[:], in_=pb.ap[:]), [pb.res], [dvs.res])
                    else:
                        S.op("act", lambda e, pb=pb, dvs=dvs: e.activation(out=dvs.ap[:], in_=pb.ap[:], func=AF.Copy), [pb.res], [dvs.res])
                    S.dma("sp", dvs.dom, [(lambda e, dvs=dvs, j=j, m=m, s0=s0: e.dma_start(
                        out=dvv[(s0 // 128) + j][:, m * 512:(m + 1) * 512], in_=dvs.ap[:]), [dvs.res], [])])

        def st_gates(bi):
            si, s0, gblk = blocks[bi]
            b = bi
            sc = g.scr[si]
            xb, ct, xnT = xin[b % 2], ctab[b % 2], xnT_ring[b % 2]
            qTv = sc.qT.rearrange("h r s -> r h s")
            kTv = sc.kT.rearrange("h r s -> r h s")
            vmv = sc.vm.rearrange("(n p) (h c) -> n p h c", p=128, c=128)
            dvv = sc.dv.rearrange("(n p) c -> n p c", p=128)
            gview = sc.gT.rearrange("c p s -> p c s")
            for cp in range(8):
                pb = nextbank()
                for a in range(2):
                    c = cp * 2 + a
                    for k in range(KC):
                        S.op("pe", lambda e, k=k, c=c, a=a, pb=pb, xnT=xnT: e.matmul(
                            pb.ap[:, a * TB:(a + 1) * TB], lhsT=wgt.ap[:, k, c * 128:(c + 1) * 128], rhs=xnT.ap[:, k, :],
                            start=(k == 0), stop=(k == KC - 1)), [wgt.res, xnT.res], [pb.res],
                             signal=(k == KC - 1 and a == 1))
                pr = pair[pi[0] % 8]
                pi[0] += 1
                for a in range(2):
                    c = cp * 2 + a
                    S.op("act", lambda e, pb=pb, pr=pr, a=a, c=c: e.activation(
                        out=pr.ap[:, a, :], in_=pb.ap[:, a * TB:(a + 1) * TB], func=AF.Sigmoid, bias=bg.ap[:, c:c + 1]),
                         [pb.res, bg.res], [pr.res])
                S.dma("sp", pr.dom, [(lambda e, pr=pr, cp=cp, s0=s0: e.dma_start(
                    out=gview[:, 2 * cp:2 * cp + 2, s0:s0 + TB], in_=pr.ap[:]), [pr.res], [])])


        st_norm(0)
        st_normT(0)
        for bi in range(len(blocks)):
            if bi + 1 < len(blocks):
                st_norm(bi + 1)
            st_zA(bi)
            st_dqk(bi)
            if bi + 1 < len(blocks):
                st_normT(bi + 1)
            st_cT(bi)
            st_mla(bi)
            st_dv(bi)
            st_gates(bi)


def mixer_p2(g, seqs):
    S = g.S
    QB = 512
    scale = 96.0 ** -0.5
    smax = max(seqs)
    with Pass(g, "p2") as P:
        KT = P.sb([96, NH, smax], BF16, dma=True)
        VA = P.sb([128, smax // 128, NH * 128], BF16, dma=True)
        QT = P.ring(2, [96, NH, QB], BF16, dma=True)
        PT = P.ring(4, [128, QB], BF16)
        oT = P.ring(2, [64, NH, QB], BF16, dma=True)
        rl = P.ring(2, [64, QB], F32)
        pss = [P.bank() for _ in range(3)]
        pos = [P.bank() for _ in range(2)]
        LAG = 2
        it = 0
        qn = 0
        hn = 0
        for si, Sq in enumerate(seqs):
            sc = g.scr[si]
            nkb = Sq // 128
            S.dma("sp", KT.dom, [(lambda e, sc=sc, Sq=Sq: e.dma_start(out=KT.ap[:, :, 0:Sq], in_=sc.kT.rearrange("h r s -> r h s")),
                                  [], [KT.res])])
            vv = sc.vm.rearrange("(n p) c -> p n c", p=128)
            S.dma("sp", VA.dom, [(lambda e, a=a, vv=vv: e.dma_start(out=VA.ap[:, a * 8:(a + 1) * 8, :], in_=vv[:, a * 8:(a + 1) * 8, :]),
                                  [], [VA.res]) for a in range(nkb // 8)])
            qTv = sc.qT.rearrange("h r s -> r h s")
            omv = sc.omT.rearrange("h r s -> r h s")
            nq = Sq // QB

            def loadq(qb, qTv=qTv):
                qt = QT[(qn + qb) % 2]
                S.dma("sp", qt.dom, [(lambda e, qt=qt, q0=qb * QB: e.dma_start(out=qt.ap[:], in_=qTv[:, :, q0:q0 + QB]), [], [qt.res])])
            loadq(0)
            for qb in range(nq):
                q0 = qb * QB
                qt = QT[(qn + qb) % 2]
                ot = oT[(qn + qb) % 2]
                if qb + 1 < nq:
                    loadq(qb + 1)
                items = [(h, kb) for h in range(NH) for kb in range(nkb)]
                n = len(items)
                for step in range(n + LAG):
                    if step < n:
                        h, kb = items[step]
                        ps, pt = pss[(it + step) % 3], PT[(it + step) % 4]
                        S.op("pe", lambda e, ps=ps, kb=kb, h=h, qt=qt: e.matmul(
                            ps.ap[:], lhsT=KT.ap[:, h, kb * 128:(kb + 1) * 128], rhs=qt.ap[:, h, :], start=True, stop=True),
                             [KT.res, qt.res], [ps.res])
                        S.op("act", lambda e, ps=ps, pt=pt: e.activation(out=pt.ap[:], in_=ps.ap[:], func=AF.Exp, scale=scale),
                             [ps.res], [pt.res])
                    if step >= LAG:
                        h, kb = items[step - LAG]
                        pt = PT[(it + step - LAG) % 4]
                        po = pos[(hn + h) % 2]
                        S.op("pe", lambda e, po=po, kb=kb, h=h, pt=pt: e.matmul(
                            po.ap[:], lhsT=VA.ap[:, kb, h * 128:(h + 1) * 128], rhs=pt.ap[:], start=(kb == 0), stop=(kb == nkb - 1)),
                             [VA.res, pt.res], [po.res], signal=(kb == nkb - 1))
                        if kb == nkb - 1:
                            r = rl[h % 2]
                            S.op("dve", lambda e, po=po, r=r: e.reciprocal(out=r.ap[:], in_=po.ap[64:128, :]), [po.res], [r.res])
                            S.op("dve", lambda e, po=po, r=r, ot=ot, h=h: e.tensor_tensor(out=ot.ap[:, h, :], in0=po.ap[0:64, :], in1=r.ap[:],
                                                                                          op=ALU.mult), [po.res, r.res], [ot.res])
                it += n
                hn += NH
                S.dma("sp", ot.dom, [(lambda e, ot=ot, q0=q0, omv=omv: e.dma_start(out=omv[:, :, q0:q0 + QB], in_=ot.ap[:]), [ot.res], [])])
            qn += nq


def mixer_p3(g, seqs):
    S = g.S
    scale = 128.0 ** -0.5
    smax = max(seqs)
    LAG = 2
    with Pass(g, "p3") as P:
        EF = P.sb([128, DH, 256], F32, dma=True)
        E1 = P.sb([64, DH, 128], F32, dma=True)
        E2 = P.sb([64, DH, 128], F32, dma=True)
        ones = P.sb([128, 128], BF16, dma=True)
        load_simple(g, EF, g.d["e_full"])
        load_simple(g, E1, g.d["e_first"])
        load_simple(g, E2, g.d["e_last"])
        load_simple(g, ones, g.d["ones"])
        QTr = P.ring(2, [128, smax], BF16, dma=True)
        KTr = P.ring(2, [128, smax], BF16, dma=True)
        Vm = P.ring(2, [128, smax // 128, 128], BF16, dma=True)
        Vf = P.ring(2, [64, 16, 128], BF16, dma=True)
        Vl = P.ring(2, [64, 16, 128], BF16, dma=True)
        acc = P.sb([128, 2, smax], F32)
        od = P.ring(2, [128, smax], BF16, dma=True)
        lnb = P.sb([128, smax], F32)
        rcp = P.sb([128, smax], F32)
        ex = P.ring(4, [128, 256], F32)
        PT = P.ring(6, [128, 256], BF16)
        pss = [P.bank() for _ in range(3)]
        pacc = [P.bank() for _ in range(4)]
        hi = 0
        ci = 0
        ji = 0
        oi = 0
        for si, Sq in enumerate(seqs):
            sc = g.scr[si]
            for slot in range(4):
                for gi, dl in enumerate(DIL):
                    head = gi * 4 + slot
                    hc = slice(head * 128, (head + 1) * 128)
                    qt, kt, vm, vf, vl = QTr[hi % 2], KTr[hi % 2], Vm[hi % 2], Vf[hi % 2], Vl[hi % 2]
                    hi += 1
                    seg = Sq // dl
                    nqb = seg // 128
                    nmid = nqb - 1
                    S.dma("sp", qt.dom, [(lambda e, qt=qt, head=head, sc=sc, Sq=Sq: e.dma_start(out=qt.ap[:, 0:Sq], in_=sc.dqT[head]), [], [qt.res])])
                    S.dma("sp", kt.dom, [(lambda e, kt=kt, head=head, sc=sc, Sq=Sq: e.dma_start(out=kt.ap[:, 0:Sq], in_=sc.dkT[head]), [], [kt.res])])
                    vfs = sc.dv[0:64 * dl, hc].rearrange("(k r) d -> k r d", r=dl)
                    vls = sc.dv[(seg - 64) * dl:seg * dl, hc].rearrange("(k r) d -> k r d", r=dl)
                    S.dma("sp", vf.dom, [(lambda e, vf=vf, vfs=vfs, dl=dl: e.dma_start(out=vf.ap[:, 0:dl, :], in_=vfs), [], [vf.res])])
                    S.dma("sp", vl.dom, [(lambda e, vl=vl, vls=vls, dl=dl: e.dma_start(out=vl.ap[:, 0:dl, :], in_=vls), [], [vl.res])])
                    if nmid > 0:
                        vms = sc.dv[64 * dl:64 * dl + 128 * dl * nmid, hc].rearrange("(c k r) d -> k r c d", k=128, r=dl)
                        vmd = vm.ap[:, 0:dl * nmid, :].rearrange("k (r c) d -> k r c d", r=dl)
                        items = []
                        for r in range(dl):
                            for c0 in range(0, nmid, 8):
                                c1 = min(nmid, c0 + 8)
                                items.append((lambda e, r=r, c0=c0, c1=c1, vms=vms, vmd=vmd: e.dma_start(
                                    out=vmd[:, r, c0:c1, :], in_=vms[:, r, c0:c1, :]), [], [vm.res]))
                        S.dma("sp", vm.dom, items)
                    pend = []

                    def flush(upto):
                        while pend and pend[0][0] <= upto:
                            pend.pop(0)[1]()
                    idx = 0
                    for r in range(dl):
                        prev = None
                        for c in range(nqb + 1):
                            first, last = (c == 0), (c == nqb)
                            nk = 64 if (first or last) else 128
                            p0 = 0 if first else 128 * c - 64
                            qlo = 128 * (c - 1) if not first else 0
                            nq = 128 if (first or last) else 256
                            if first:
                                vap, vres, em = vf.ap[0:64, r, :], vf.res, E1.ap[:, head, :]
                            elif last:
                                vap, vres, em = vl.ap[0:64, r, :], vl.res, E2.ap[:, head, :]
                            else:
                                vap, vres, em = vm.ap[:, r * nmid + (c - 1), :], vm.res, EF.ap[:, head, :]
                            ps, xx, pt = pss[ci % 3], ex[ci % 4], PT[ci % 6]
                            ci += 1
                            kcols = kt.ap[:, p0 * dl + r:(p0 + nk - 1) * dl + r + 1:dl]
                            qcols = qt.ap[:, qlo * dl + r:(qlo + nq - 1) * dl + r + 1:dl]
                            S.op("pe", lambda e, ps=ps, kcols=kcols, qcols=qcols, nk=nk, nq=nq: e.matmul(
                                ps.ap[0:nk, 0:nq], lhsT=kcols, rhs=qcols, start=True, stop=True), [kt.res, qt.res], [ps.res])
                            S.op("act", lambda e, ps=ps, xx=xx, nk=nk, nq=nq: e.activation(out=xx.ap[0:nk, 0:nq], in_=ps.ap[0:nk, 0:nq],
                                                                                           func=AF.Exp, scale=scale), [ps.res], [xx.res])
                            S.op("pool", lambda e, xx=xx, pt=pt, em=em, nk=nk, nq=nq: e.tensor_tensor(
                                out=pt.ap[0:nk, 0:nq], in0=xx.ap[0:nk, 0:nq], in1=em, op=ALU.mult),
                                 [xx.res, EF.res, E1.res, E2.res], [pt.res])
                            cur = (pt, vap, vres, nk, first)
                            if prev is not None:
                                def job(prev=prev, cur=cur, q=c - 1, r=r, dl=dl, gi=gi):
                                    nonlocal ji
                                    ppt, pvap, pvres, pnk, pfirst = prev
                                    pt, vap, vres, nk, _ = cur
                                    pb = pacc[ji % 4]
                                    ji += 1
                                    acol = 0 if pfirst else 128
                                    S.op("pe", lambda e: e.matmul(pb.ap[:, 0:128], lhsT=pvap, rhs=ppt.ap[0:pnk, acol:acol + 128],
                                                                  start=True, stop=False), [pvres, ppt.res], [pb.res], signal=False)
                                    S.op("pe", lambda e: e.matmul(pb.ap[:, 0:128], lhsT=vap, rhs=pt.ap[0:nk, 0:128],
                                                                  start=False, stop=True), [vres, pt.res], [pb.res], signal=False)
                                    S.op("pe", lambda e: e.matmul(pb.ap[:, 128:256], lhsT=ones.ap[0:pnk, :], rhs=ppt.ap[0:pnk, acol:acol + 128],
                                                                  start=True, stop=False), [ones.res, ppt.res], [pb.res], signal=False)
                                    S.op("pe", lambda e: e.matmul(pb.ap[:, 128:256], lhsT=ones.ap[0:nk, :], rhs=pt.ap[0:nk, 0:128],
                                                                  start=False, stop=True), [ones.res, pt.res], [pb.res])
                                    lo = q * 128 * dl + r
                                    hi_ = (q * 128 + 127) * dl + r + 1
                                    dst = acc.ap[:, :, lo:hi_:dl]
                                    src = pb.ap[:, 0:256].rearrange("p (a q) -> p a q", a=2)
                                    if gi == 0:
                                        S.op("dve", lambda e: e.tensor_copy(out=dst, in_=src), [pb.res], [acc.res])
                                    else:
                                        S.op("dve", lambda e: e.tensor_tensor(out=dst, in0=src, in1=dst, op=ALU.add), [pb.res], [acc.res])
                                pend.append((idx, job))
                            prev = cur
                            flush(idx - LAG)
                            idx += 1
                    flush(idx)
                o = od[oi % 2]
                oi += 1
                S.op("act", lambda e, Sq=Sq: e.activation(out=lnb.ap[:, 0:Sq], in_=acc.ap[:, 1, 0:Sq], func=AF.Ln), [acc.res], [lnb.res])
                S.op("act", lambda e, Sq=Sq: e.activation(out=rcp.ap[:, 0:Sq], in_=lnb.ap[:, 0:Sq], func=AF.Exp, scale=-1.0), [lnb.res], [rcp.res])
                S.op("dve", lambda e, o=o, Sq=Sq: e.tensor_tensor(out=o.ap[:, 0:Sq], in0=acc.ap[:, 0, 0:Sq], in1=rcp.ap[:, 0:Sq], op=ALU.mult),
                     [acc.res, rcp.res], [o.res])
                S.dma("sp", o.dom, [(lambda e, o=o, slot=slot, sc=sc, Sq=Sq: e.dma_start(out=sc.odT[slot], in_=o.ap[:, 0:Sq]), [o.res], [])])


def mixer_p4(g, l, seqs):
    S = g.S
    TB, NT = 512, 4
    d = g.d
    with Pass(g, "p4") as P:
        wbm = P.sb([64, NH, D], BF16, dma=True)
        wbd = P.sb([128, 4, D], BF16, dma=True)
        wo = P.sb([128, KC, D], BF16, dma=True)
        load_w_cast(g, wbm, d["w_branch_mla"][l].rearrange("(h p) f -> p h f", p=64), D)
        load_w_cast(g, wbd, d["w_branch_dil"][l].rearrange("(c p) f -> p c f", p=128), D)
        load_w_cast(g, wo, d["w_out"][l].rearrange("(c p) f -> p c f", p=128), D)
        xin = P.ring(2, [128, NT, D], F32, dma=True)
        xst = [g.S.dma_dom() for _ in range(2)]
        om = P.ring(2, [64, NH, TB], BF16, dma=True)
        od = P.ring(2, [128, 4, TB], BF16, dma=True)
        gt = P.ring(2, [128, 16, TB], BF16, dma=True)
        mT = P.sb([128, KC, TB], BF16)
        mres = [Res() for _ in range(KC)]
        t1 = P.ring(2, [128, TB], F32)
        t2 = P.ring(2, [128, TB], F32)
        pbm = [P.bank(), P.bank()]
        pbd = [P.bank(), P.bank()]
        pys = [P.bank(), P.bank()]
        xv = g.xres.rearrange("(n j p) d -> n p j d", j=NT, p=128)
        blocks = []
        t0 = 0
        for si, Sq in enumerate(seqs):
            for b in range(Sq // TB):
                blocks.append((si, b * TB, (t0 + b * TB) // TB))
            t0 += Sq

        def loadb(bn):
            si, s0, gblk = blocks[bn]
            sc = g.scr[si]
            omv = sc.omT.rearrange("h r s -> r h s")
            odv = sc.odT.rearrange("i p s -> p i s")
            gv = sc.gT.rearrange("c p s -> p c s")
            xb, o1, o2, gg = xin[bn % 2], om[bn % 2], od[bn % 2], gt[bn % 2]
            S.dma("sp", xb.dom, [(lambda e: e.dma_start(out=xb.ap[:], in_=xv[gblk]), [], [xb.res])])
            S.dma("sp", o1.dom, [(lambda e: e.dma_start(out=o1.ap[:], in_=omv[:, :, s0:s0 + TB]), [], [o1.res])])
            S.dma("sp", o2.dom, [(lambda e: e.dma_start(out=o2.ap[:], in_=odv[:, :, s0:s0 + TB]), [], [o2.res])])
            S.dma("sp", gg.dom, [(lambda e: e.dma_start(out=gg.ap[:], in_=gv[:, :, s0:s0 + TB]), [], [gg.res])])
        loadb(0)
        for bn in range(len(blocks)):
            if True:
                si, s0, gblk = blocks[bn]
                xb, o1, o2, gg = xin[bn % 2], om[bn % 2], od[bn % 2], gt[bn % 2]
                xs_dom = xst[bn % 2]
                if bn + 1 < len(blocks):
                    loadb(bn + 1)
                for c in range(KC):
                    p1, p2, a1, a2 = pbm[c % 2], pbd[c % 2], t1[c % 2], t2[c % 2]
                    for h in range(NH):
                        S.op("pe", lambda e, h=h, c=c, p1=p1, o1=o1: e.matmul(
                            p1.ap[:], lhsT=wbm.ap[:, h, c * 128:(c + 1) * 128], rhs=o1.ap[:, h, :], start=(h == 0), stop=(h == NH - 1)),
                             [wbm.res, o1.res], [p1.res], signal=(h == NH - 1))
                    for i in range(4):
                        S.op("pe", lambda e, i=i, c=c, p2=p2, o2=o2: e.matmul(
                            p2.ap[:], lhsT=wbd.ap[:, i, c * 128:(c + 1) * 128], rhs=o2.ap[:, i, :], start=(i == 0), stop=(i == 3)),
                             [wbd.res, o2.res], [p2.res], signal=(i == 3))
                    S.op("dve", lambda e, p1=p1, a1=a1, gg=gg, c=c: e.tensor_tensor(out=a1.ap[:], in0=p1.ap[:], in1=gg.ap[:, c, :], op=ALU.mult),
                         [p1.res, gg.res], [a1.res])
                    S.op("dve", lambda e, p2=p2, a2=a2, gg=gg, c=c: e.tensor_tensor(out=a2.ap[:], in0=p2.ap[:], in1=gg.ap[:, 8 + c, :], op=ALU.mult),
                         [p2.res, gg.res], [a2.res])
                    S.op("pool", lambda e, a1=a1, a2=a2, c=c: e.tensor_tensor(out=mT.ap[:, c, :], in0=a1.ap[:], in1=a2.ap[:], op=ALU.add),
                         [a1.res, a2.res], [mres[c]])
                for j in range(NT):
                    for hf in range(2):
                        py = pys[(j * 2 + hf) % 2]
                        for c in range(KC):
                            S.op("pe", lambda e, c=c, j=j, hf=hf, py=py: e.matmul(
                                py.ap[:], lhsT=mT.ap[:, c, j * 128:(j + 1) * 128], rhs=wo.ap[:, c, hf * 512:(hf + 1) * 512],
                                start=(c == 0), stop=(c == KC - 1)), [mres[c], wo.res], [py.res], signal=(c == KC - 1))
                        S.op("dve", lambda e, j=j, hf=hf, py=py, xb=xb: e.tensor_tensor(
                            out=xb.ap[:, j, hf * 512:(hf + 1) * 512], in0=py.ap[:], in1=xb.ap[:, j, hf * 512:(hf + 1) * 512], op=ALU.add),
                             [py.res], [xb.res])
                S.dma("sp", xs_dom, [(lambda e, xb=xb, gblk=gblk: e.dma_start(out=xv[gblk], in_=xb.ap[:]), [xb.res], [])])


W_NAMES = ["ffn1_norm", "ffn1_w_gate", "ffn1_w_up", "ffn1_w_down", "mix_norm", "q_a_norm", "kv_a_norm",
           "w_branch_mla", "w_branch_dil", "w_out", "ffn2_norm", "ffn2_w_gate", "ffn2_w_up", "ffn2_w_down",
           "final_norm", "w_zA", "w_kpe2", "w_dq", "w_dk", "w_dv", "w_gt", "w_q2", "w_kn", "w_v", "bgate2",
           "ropetab", "e_full", "e_first", "e_last", "ident", "ones"]


def build_program(seqs, depth, shapes, dtypes):
    T = sum(seqs)
    nc = bass.Bass("TRN2", target_bir_lowering=False)
    g = G()
    g.nc = nc
    g.d = {}
    for n in W_NAMES:
        g.d[n] = nc.dram_tensor(n, list(shapes[n]), dtypes[n], kind="ExternalInput").ap()
    x_in = nc.dram_tensor("x", [T, D], F32, kind="ExternalInput").ap()
    y_out = nc.dram_tensor("y", [T, D], F32, kind="ExternalOutput").ap()
    g.xres = nc.dram_tensor("xres", [T, D], F32, kind="Internal").ap()
    g.scr = []
    for si, sq in enumerate(seqs):
        sc = G()
        mk = lambda n, shp: nc.dram_tensor("s%d_%s" % (si, n), shp, BF16, kind="Internal").ap()
        sc.qT = mk("qT", [NH, 96, sq])
        sc.kT = mk("kT", [NH, 96, sq])
        sc.vm = mk("vm", [sq, NH * 128])
        sc.dqT = mk("dqT", [DH, 128, sq])
        sc.dkT = mk("dkT", [DH, 128, sq])
        sc.dv = mk("dv", [sq, DH * 128])
        sc.gT = mk("gT", [16, 128, sq])
        sc.omT = mk("omT", [NH, 64, sq])
        sc.odT = mk("odT", [4, 128, sq])
        g.scr.append(sc)
    with ExitStack() as st:
        S = Sched(nc, st)
        g.S = S
        identt = st.enter_context(nc.sbuf_tensor("ident_sb", [128, 128], BF16))
        epst = st.enter_context(nc.sbuf_tensor("eps_sb", [128, 1], F32))
        g.ident = Slot(identt, S.pool[-1])
        g.eps = Slot(epst)
        S.dma("sp", g.ident.dom, [(lambda e: e.dma_start(out=identt[:], in_=g.d["ident"]), [], [g.ident.res])])
        S.op("pool", lambda e: e.memset(epst[:], EPS), [], [g.eps.res])
        S.pool = S.pool[:-1]
        for l in range(depth):
            ffn_pass(g, l, 1, x_in if l == 0 else g.xres, g.xres, T)
            mixer_p1(g, l, seqs)
            mixer_p2(g, seqs)
            mixer_p3(g, seqs)
            mixer_p4(g, l, seqs)
            ffn_pass(g, l, 2, g.xres, g.xres, T)
        final_pass(g, g.xres, y_out, T)
        S.barrier()
        S.emit()
    g.n_ins = dict(S.n_ins)
    return nc, g


def host_constants():
    pos = np.arange(SMAX, dtype=np.float32)
    inv = (1.0 / (np.float32(10000.0) ** (np.arange(0, 32, 2, dtype=np.float32) / np.float32(32)))).astype(np.float32)
    ang = (pos[:, None] * inv[None, :]).astype(np.float32)
    cos, sin = np.cos(ang).astype(np.float32).T, np.sin(ang).astype(np.float32).T
    rope = np.zeros((32, 2, SMAX), np.float32)
    rope[0:16, 0], rope[16:32, 0] = cos, cos
    rope[0:16, 1], rope[16:32, 1] = -sin, sin
    slopes = (2.0 ** (-8.0 * np.arange(1, DH + 1, dtype=np.float32) / DH)).astype(np.float32)
    k = np.arange(128)[:, None]
    q = np.arange(128)[None, :]
    e_full = np.zeros((128, DH, 256), np.float32)
    e_first = np.zeros((64, DH, 128), np.float32)
    e_last = np.zeros((64, DH, 128), np.float32)
    for h in range(DH):
        dl = DIL[h // 4]
        sl = slopes[h] * dl
        relB = k - q + 64
        e_full[:, h, 0:128] = np.where(k <= q, np.exp(-sl * np.abs(relB).astype(np.float32)), 0.0)
        relA = k - q - 64
        e_full[:, h, 128:256] = np.where(k >= q, np.exp(-sl * np.abs(relA).astype(np.float32)), 0.0)
        k6 = np.arange(64)[:, None]
        rel1 = k6 - q
        e_first[:, h, :] = np.where(np.abs(rel1) <= 64, np.exp(-sl * np.abs(rel1).astype(np.float32)), 0.0)
        rel2 = 64 + k6 - q
        e_last[:, h, :] = np.where(np.abs(rel2) <= 64, np.exp(-sl * np.abs(rel2).astype(np.float32)), 0.0)
    return {"ropetab": rope, "e_full": e_full, "e_first": e_first, "e_last": e_last,
            "ident": np.eye(128, dtype=np.float32).astype(ml_dtypes.bfloat16),
            "ones": np.ones((128, 128), np.float32).astype(ml_dtypes.bfloat16)}


def host_weights(inp):
    f = lambda a: np.ascontiguousarray(np.asarray(a, dtype=np.float32))
    w_in = f(inp["w_in"])
    L = w_in.shape[0]
    out = {}
    for n in ["ffn1_norm", "ffn1_w_gate", "ffn1_w_up", "ffn1_w_down", "mix_norm", "q_a_norm", "kv_a_norm",
              "w_branch_mla", "w_branch_dil", "w_out", "ffn2_norm", "ffn2_w_gate", "ffn2_w_up", "ffn2_w_down", "final_norm"]:
        out[n] = f(inp[n])
    out["w_zA"] = f(w_in[:, :, 0:384])
    z64 = np.zeros((L, w_in.shape[1], 64), np.float32)
    out["w_kpe2"] = f(np.concatenate([z64, w_in[:, :, 384:416], z64, w_in[:, :, 400:416], w_in[:, :, 384:400]], axis=2))
    out["w_dq"] = f(w_in[:, :, 416:1952])
    out["w_dk"] = f(w_in[:, :, 1952:3488])
    out["w_dv"] = f(w_in[:, :, 3488:5024])
    out["w_gt"] = f(w_in[:, :, 5024:7072])
    wq = f(inp["w_q_up"]).reshape(L, 256, NH, 96)
    nope, rope = wq[..., 0:64], wq[..., 64:96]
    out["w_q2"] = f(np.concatenate([nope, rope, np.zeros_like(nope), rope[..., 16:32], rope[..., 0:16]], axis=3).reshape(L, 256, NH * 192))
    wkv = f(inp["w_kv_up"]).reshape(L, 128, NH, 128)
    out["w_kn"] = f(wkv[..., 0:64].reshape(L, 128, NH * 64))
    out["w_v"] = f(wkv[..., 64:128].reshape(L, 128, NH * 64))
    out["bgate2"] = f(f(inp["b_gate"]).reshape(L, 16, 128).transpose(0, 2, 1))
    out.update(host_constants())
    return out


_CACHE = {}


def run_cores(xs, seqs, depth, W, trace=False):
    shapes = {k: v.shape for k, v in W.items()}
    dtypes = {k: (BF16 if v.dtype == ml_dtypes.bfloat16 else F32) for k, v in W.items()}
    key = (tuple(seqs), depth)
    if key not in _CACHE:
        _CACHE[key] = build_program(seqs, depth, shapes, dtypes)
    nc, g = _CACHE[key]
    in_maps = []
    for x in xs:
        m = dict(W)
        m["x"] = x
        in_maps.append(m)
    res = run_bass_kernel_spmd(nc, in_maps, core_ids=list(range(len(xs))), trace=trace)
    return [r["y"] for r in res.results], res


def kernel(**inp):
    xp = np.asarray(inp["x_prompt"], dtype=np.float32)
    xs_ = np.asarray(inp["x_sample"], dtype=np.float32)
    W = host_weights(inp)
    depth = W["w_zA"].shape[0]
    ncore = 8
    pb, sbn = xp.shape[0] // ncore, xs_.shape[0] // ncore
    seqs = [xp.shape[1]] * pb + [xs_.shape[1]] * sbn
    xs = []
    for c in range(ncore):
        xs.append(np.ascontiguousarray(np.concatenate(
            [xp[c * pb:(c + 1) * pb].reshape(-1, D), xs_[c * sbn:(c + 1) * sbn].reshape(-1, D)], axis=0)))
    ys, _ = run_cores(xs, seqs, depth, W)
    np_ = pb * xp.shape[1]
    y_prompt = np.concatenate([y[:np_].reshape(pb, xp.shape[1], D) for y in ys], axis=0)
    y_sample = np.concatenate([y[np_:].reshape(sbn, xs_.shape[1], D) for y in ys], axis=0)
    return (y_prompt.astype(np.float32), y_sample.astype(np.float32))
```

```python
import numpy as np
import ml_dtypes
from contextlib import ExitStack
import concourse.bass as bass
import concourse.mybir as mybir
from concourse.bass_utils import run_bass_kernel_spmd

F32 = mybir.dt.float32
BF16 = mybir.dt.bfloat16
AF = mybir.ActivationFunctionType
ALU = mybir.AluOpType

import os
SELF_SYNC = os.environ.get("K_SELF_SYNC", "1") == "1"
D = 1024
KC = 8
FF = 2816
FC = 22
NH = 8
DH = 12
DIL = (1, 4, 16)
SMAX = 4096
EPS = 1e-6


class Dom:
    def __init__(self, sem, unit, name):
        self.sem, self.unit, self.count, self.name = sem, unit, 0, name


class Res:
    __slots__ = ("w", "r", "excl")

    def __init__(self, excl=False):
        self.w = None
        self.r = {}
        self.excl = excl


class Sched:
    ENG = ("pe", "act", "dve", "pool", "sp")

    def __init__(self, nc, stack, n_dma=56):
        self.nc = nc
        self.ops = {k: [] for k in self.ENG}
        self.doms = {}
        for k in ("pe", "act", "dve", "pool"):
            self.doms[k] = Dom(stack.enter_context(nc.semaphore("s_" + k)), 1, k)
        self.waited = {k: {} for k in self.ENG}
        self.pool = [Dom(stack.enter_context(nc.semaphore("d%d" % i)), 16, "d%d" % i) for i in range(n_dma)]
        self.pool_i = 0
        self.swpool = [Dom(stack.enter_context(nc.semaphore("w%d" % i)), 16, "w%d" % i) for i in range(10)]
        self.sw_i = 0
        self.n_ins = {k: 0 for k in self.ENG}

    def new_pass(self):
        self.pool_i = 0
        self.sw_i = 0

    def sw_dom(self):
        d = self.swpool[self.sw_i]
        self.sw_i += 1
        return d

    def dma_dom(self):
        d = self.pool[self.pool_i]
        self.pool_i += 1
        return d

    def _need(self, eng, tok, waits, own=None):
        if tok is None:
            return
        dom, val = tok
        if dom is own and dom.unit == 16:
            return
        if dom is self.doms.get(eng):
            if eng == "pe" or not SELF_SYNC:
                return
        if self.waited[eng].get(dom, 0) >= val:
            return
        self.waited[eng][dom] = val
        waits.append((dom.sem, val * dom.unit))

    def dma(self, eng, dom, items):
        final = dom.count + len(items)
        tok = None
        for fn, reads, writes in items:
            tok = self.op(eng, fn, reads, writes, dom=dom, tokval=final)
        return tok

    def op(self, eng, fn, reads=(), writes=(), dom=None, signal=True, tokval=None):
        waits = []
        if any(r.excl for r in reads):
            writes = list(writes) + [r for r in reads if r.excl]
            reads = [r for r in reads if not r.excl]
        d = dom if dom is not None else self.doms[eng]
        for r in reads:
            self._need(eng, r.w, waits, d)
        for w in writes:
            self._need(eng, w.w, waits, d)
            for dd, v in w.r.items():
                self._need(eng, (dd, v), waits, d)
        if signal:
            d.count += 1
        tok = (d, d.count if signal else d.count + 1)
        if tokval is not None:
            tok = (d, tokval)
        for r in reads:
            if r.r.get(d, 0) < tok[1]:
                r.r[d] = tok[1]
        for w in writes:
            w.w = tok
            w.r = {}
        sem, unit = d.sem, d.unit
        self.n_ins[eng] += 1

        def run(e, waits=waits, fn=fn, signal=signal, sem=sem, unit=unit):
            for s, v in waits:
                e.wait_ge(s, v)
            ins = fn(e)
            if signal:
                ins.then_inc(sem, unit)
        self.ops[eng].append(run)
        return tok

    def barrier(self, engs=None):
        alld = list(self.doms.values()) + self.pool + self.swpool
        for eng in (engs or self.ENG):
            waits = []
            for d in alld:
                if d.count > 0:
                    self._need(eng, (d, d.count), waits)

            def run(e, waits=waits):
                for s, v in waits:
                    e.wait_ge(s, v)
            self.ops[eng].append(run)

    def emit(self):
        nc, ops = self.nc, self.ops
        with nc.Block() as block:
            @block.tensor
            def _(e):
                for f in ops["pe"]:
                    f(e)

            @block.scalar
            def _(e):
                for f in ops["act"]:
                    f(e)

            @block.vector
            def _(e):
                for f in ops["dve"]:
                    f(e)

            @block.gpsimd
            def _(e):
                for f in ops["pool"]:
                    f(e)

            @block.sync
            def _(e):
                for f in ops["sp"]:
                    f(e)


class Slot:
    def __init__(self, ap, dom=None, excl=False):
        self.ap, self.res, self.dom = ap, Res(excl), dom


class Pass:
    _uid = [0]

    def __init__(self, G, name):
        Pass._uid[0] += 1
        self.G, self.name = G, "%s_%d" % (name, Pass._uid[0])
        self.st = ExitStack()
        self.n = 0

    def __enter__(self):
        self.st.__enter__()
        self.G.S.barrier()
        self.G.S.new_pass()
        self.banks = []
        return self

    def __exit__(self, *a):
        return self.st.__exit__(*a)

    def sb(self, shape, dt, dma=False):
        self.n += 1
        t = self.st.enter_context(self.G.nc.sbuf_tensor("%s_%d" % (self.name, self.n), list(shape), dt))
        if dma == "sw":
            return Slot(t, self.G.S.sw_dom())
        return Slot(t, self.G.S.dma_dom() if dma else None)

    def ring(self, n, shape, dt, dma=False):
        return [self.sb(shape, dt, dma) for _ in range(n)]

    def bank2(self):
        self.n += 1
        t = self.st.enter_context(self.G.nc.psum_tensor("%s_q%d" % (self.name, self.n), [128, 1024], F32))
        return Slot(t, excl=True)

    def bank(self, dt=F32):
        self.n += 1
        shape = [128, 512] if dt == F32 else [128, 1024]
        t = self.st.enter_context(self.G.nc.psum_tensor("%s_p%d" % (self.name, self.n), shape, dt))
        return Slot(t, excl=True)


class G:
    pass


def load_w_cast(g, slot, src, ncols, split=1):
    S = g.S
    nch = src.shape[1]
    step = ncols // split
    items = []
    for c in range(nch):
        for h in range(split):
            items.append((lambda e, c=c, h=h: e.dma_start(out=slot.ap[:, c, h * step:(h + 1) * step],
                                                          in_=src[:, c, h * step:(h + 1) * step]),
                          [], [slot.res]))
    S.dma("pool", slot.dom, items)


def load_simple(g, slot, src, eng="sp", dst=None):
    d = slot.ap[:] if dst is None else dst
    g.S.dma(eng, slot.dom, [(lambda e: e.dma_start(out=d, in_=src), [], [slot.res])])


def rms_stats(g, xs, ntile, width, ss, rt, rstd, junk):
    S = g.S
    for j in range(ntile):
        ap, res = xs(j)
        S.op("act", lambda e, ap=ap, j=j: e.activation(out=junk.ap[:, 0:width], in_=ap, func=AF.Square,
                                                        accum_out=ss.ap[:, j:j + 1]),
             [res], [junk.res, ss.res])
    S.op("act", lambda e: e.activation(out=rt.ap[:, 0:ntile], in_=ss.ap[:, 0:ntile], func=AF.Sqrt,
                                       scale=1.0 / width, bias=g.eps.ap[:, 0:1]),
         [ss.res, g.eps.res], [rt.res])
    S.op("dve", lambda e: e.reciprocal(out=rstd.ap[:, 0:ntile], in_=rt.ap[:, 0:ntile]), [rt.res], [rstd.res])


def transpose_tile(g, src, nchunk, tpb, dst_ap, dst_res, cp_eng):
    S = g.S
    for c in range(nchunk):
        S.op("pe", lambda e, c=c: e.transpose(out=tpb.ap[:, c * 128:(c + 1) * 128],
                                              in_=src.ap[:, c * 128:(c + 1) * 128], identity=g.ident.ap[:]),
             [src.res, g.ident.res], [tpb.res], signal=(c == nchunk - 1))
    view = tpb.ap[:, 0:nchunk * 128].rearrange("p (c t) -> p c t", c=nchunk)
    if cp_eng == "act":
        S.op("act", lambda e: e.activation(out=dst_ap, in_=view, func=AF.Copy), [tpb.res], [dst_res])
    else:
        S.op("dve", lambda e: e.tensor_copy(out=dst_ap, in_=view), [tpb.res], [dst_res])


def norm_compute(g, xb, ntile, gb, xn_ring, stats):
    S = g.S
    ss, rt, rstd, junk = stats
    rms_stats(g, lambda j: (xb.ap[:, j, :], xb.res), ntile, D, ss, rt, rstd, junk)
    for j in range(ntile):
        xn = xn_ring[j % len(xn_ring)]
        S.op("dve", lambda e, j=j, xn=xn: e.scalar_tensor_tensor(out=xn.ap[:], in0=xb.ap[:, j, :],
                                                                 scalar=rstd.ap[:, j:j + 1], in1=gb.ap[:],
                                                                 op0=ALU.mult, op1=ALU.mult),
             [xb.res, rstd.res, gb.res], [xn.res])


def norm_transpose(g, ntile, xn_ring, tpbs, xnT):
    for j in range(ntile):
        xn = xn_ring[j % len(xn_ring)]
        tpb = tpbs[j % len(tpbs)]
        transpose_tile(g, xn, KC, tpb, xnT.ap[:, :, j * 128:(j + 1) * 128], xnT.res, "act")


def ffn_pass(g, l, which, src, dst, T):
    S = g.S
    TB, NT = 256, 2
    wg_d, wu_d, wd_d, gn_d = (g.d["ffn%d_w_gate" % which], g.d["ffn%d_w_up" % which],
                              g.d["ffn%d_w_down" % which], g.d["ffn%d_norm" % which])
    with Pass(g, "f%d%d" % (l, which)) as P:
        wg = P.sb([128, KC, FF], BF16, dma="sw")
        wu = P.sb([128, KC, FF], BF16, dma="sw")
        wd = P.sb([128, FC, D], BF16, dma="sw")
        gb = P.sb([128, D], F32, dma=True)
        load_w_cast(g, wg, wg_d[l].rearrange("(c p) f -> p c f", p=128), FF, split=2)
        load_w_cast(g, wu, wu_d[l].rearrange("(c p) f -> p c f", p=128), FF, split=2)
        load_w_cast(g, wd, wd_d[l].rearrange("(c p) f -> p c f", p=128), D, split=1)
        load_simple(g, gb, gn_d[l].partition_broadcast(128))
        xin = P.ring(2, [128, NT, D], F32, dma=True)
        xst = [P.G.S.dma_dom() for _ in range(2)]
        xn_ring = P.ring(2, [128, D], BF16)
        xnT_ring = P.ring(2, [128, KC, TB], BF16)
        hT = P.sb([128, FC, TB], BF16)
        hres = [Res() for _ in range(FC)]
        sg_ring = P.ring(2, [128, TB], F32)
        junk = P.sb([128, D], BF16)
        stats = [(P.sb([128, 4], F32), P.sb([128, 4], F32), P.sb([128, 4], F32), junk) for _ in range(2)]
        tpbs = [P.bank(BF16), P.bank(BF16)]
        pgs = [P.bank(), P.bank()]
        pus = [P.bank(), P.bank()]
        pys = [P.bank(), P.bank()]
        srcv = src.rearrange("(n j p) d -> n p j d", j=NT, p=128)
        dstv = dst.rearrange("(n j p) d -> n p j d", j=NT, p=128)
        nblk = T // TB

        def pre(b):
            xb = xin[b % 2]
            S.dma("sp", xb.dom, [(lambda e, xb=xb, b=b: e.dma_start(out=xb.ap[:], in_=srcv[b]), [], [xb.res])])
            norm_compute(g, xb, NT, gb, xn_ring, stats[b % 2])
        pre(0)
        norm_transpose(g, NT, xn_ring, tpbs, xnT_ring[0])
        for b in range(nblk):
            xb = xin[b % 2]
            xnT = xnT_ring[b % 2]
            if b + 1 < nblk:
                pre(b + 1)
            for f in range(FC):
                pg, pu, sg = pgs[f % 2], pus[f % 2], sg_ring[f % 2]
                for k in range(KC):
                    S.op("pe", lambda e, k=k, f=f, pg=pg, xnT=xnT: e.matmul(
                        pg.ap[:, 0:TB], lhsT=wg.ap[:, k, f * 128:(f + 1) * 128], rhs=xnT.ap[:, k, :],
                        start=(k == 0), stop=(k == KC - 1)), [wg.res, xnT.res], [pg.res], signal=(k == KC - 1))
                for k in range(KC):
                    S.op("pe", lambda e, k=k, f=f, pu=pu, xnT=xnT: e.matmul(
                        pu.ap[:, 0:TB], lhsT=wu.ap[:, k, f * 128:(f + 1) * 128], rhs=xnT.ap[:, k, :],
                        start=(k == 0), stop=(k == KC - 1)), [wu.res, xnT.res], [pu.res], signal=(k == KC - 1))
                S.op("act", lambda e, pg=pg, sg=sg: e.activation(out=sg.ap[:], in_=pg.ap[:, 0:TB], func=AF.Silu),
                     [pg.res], [sg.res])
                S.op("dve", lambda e, pu=pu, sg=sg, f=f: e.tensor_tensor(out=hT.ap[:, f, :], in0=pu.ap[:, 0:TB],
                                                                          in1=sg.ap[:], op=ALU.mult),
                     [pu.res, sg.res], [hres[f]])
            if b + 1 < nblk:
                norm_transpose(g, NT, xn_ring, tpbs, xnT_ring[(b + 1) % 2])
            for j in range(NT):
                for hf in range(2):
                    py = pys[(j * 2 + hf) % 2]
                    for f in range(FC):
                        S.op("pe", lambda e, f=f, j=j, hf=hf, py=py: e.matmul(
                            py.ap[:], lhsT=hT.ap[:, f, j * 128:(j + 1) * 128], rhs=wd.ap[:, f, hf * 512:(hf + 1) * 512],
                            start=(f == 0), stop=(f == FC - 1)), [hres[f], wd.res], [py.res], signal=(f == FC - 1))
                    S.op("dve", lambda e, j=j, hf=hf, py=py, xb=xb: e.scalar_tensor_tensor(
                        out=xb.ap[:, j, hf * 512:(hf + 1) * 512], in0=py.ap[:], scalar=0.5,
                        in1=xb.ap[:, j, hf * 512:(hf + 1) * 512], op0=ALU.mult, op1=ALU.add),
                         [py.res], [xb.res])
            S.dma("sp", xst[b % 2], [(lambda e, xb=xb, b=b: e.dma_start(out=dstv[b], in_=xb.ap[:]), [xb.res], [])])


def final_pass(g, src, dst, T):
    S = g.S
    TB, NT = 256, 2
    with Pass(g, "fin") as P:
        gb = P.sb([128, D], F32, dma=True)
        load_simple(g, gb, g.d["final_norm"].partition_broadcast(128))
        xin = P.ring(3, [128, NT, D], F32, dma=True)
        xo = P.ring(3, [128, NT, D], F32, dma=True)
        junk = P.sb([128, D], BF16)
        stats = [(P.sb([128, 4], F32), P.sb([128, 4], F32), P.sb([128, 4], F32), junk) for _ in range(3)]
        srcv = src.rearrange("(n j p) d -> n p j d", j=NT, p=128)
        dstv = dst.rearrange("(n j p) d -> n p j d", j=NT, p=128)
        toks = []
        for b in range(T // TB):
            xb, ob = xin[b % 3], xo[b % 3]
            ss, rt, rstd, _ = stats[b % 3]
            S.dma("sp", xb.dom, [(lambda e, xb=xb, b=b: e.dma_start(out=xb.ap[:], in_=srcv[b]), [], [xb.res])])
            rms_stats(g, lambda j: (xb.ap[:, j, :], xb.res), NT, D, ss, rt, rstd, junk)
            for j in range(NT):
                S.op("dve", lambda e, j=j, xb=xb, ob=ob, rstd=rstd: e.scalar_tensor_tensor(
                    out=ob.ap[:, j, :], in0=xb.ap[:, j, :], scalar=rstd.ap[:, j:j + 1], in1=gb.ap[:],
                    op0=ALU.mult, op1=ALU.mult), [xb.res, rstd.res, gb.res], [ob.res])
            S.dma("sp", ob.dom, [(lambda e, ob=ob, b=b: e.dma_start(out=dstv[b], in_=ob.ap[:]), [ob.res], [])])


def mixer_p1(g, l, seqs):
    S = g.S
    TB, NT = 256, 2
    d = g.d
    with Pass(g, "p1") as P:
        wzA = P.sb([128, KC, 384], BF16, dma="sw")
        wkp = P.sb([128, KC, 192], BF16, dma="sw")
        wdq = P.sb([128, KC, 1536], BF16, dma="sw")
        wdk = P.sb([128, KC, 1536], BF16, dma="sw")
        wdv = P.sb([128, KC, 1536], BF16, dma="sw")
        wgt = P.sb([128, KC, 2048], BF16, dma="sw")
        wq = P.sb([128, 2, NH * 192], BF16, dma="sw")
        wkn = P.sb([128, 1, NH * 64], BF16, dma="sw")
        wv = P.sb([128, 1, 512], BF16, dma="sw")
        cv = lambda a: a[l].rearrange("(c p) f -> p c f", p=128)
        load_w_cast(g, wzA, cv(d["w_zA"]), 384)
        load_w_cast(g, wkp, cv(d["w_kpe2"]), 192)
        load_w_cast(g, wdq, cv(d["w_dq"]), 1536)
        load_w_cast(g, wdk, cv(d["w_dk"]), 1536)
        load_w_cast(g, wdv, cv(d["w_dv"]), 1536)
        load_w_cast(g, wgt, cv(d["w_gt"]), 2048)
        load_w_cast(g, wq, cv(d["w_q2"]), NH * 192)
        load_w_cast(g, wkn, cv(d["w_kn"]), NH * 64)
        load_w_cast(g, wv, cv(d["w_v"]), 512)
        gb = P.sb([128, D], F32, dma=True)
        gq = P.sb([128, 256], F32, dma=True)
        gk = P.sb([128, 128], F32, dma=True)
        bg = P.sb([128, 16], F32, dma=True)
        load_simple(g, gb, d["mix_norm"][l].partition_broadcast(128))
        load_simple(g, gq, d["q_a_norm"][l].partition_broadcast(128))
        load_simple(g, gk, d["kv_a_norm"][l].partition_broadcast(128))
        load_simple(g, bg, d["bgate2"][l])
        xin = P.ring(2, [128, NT, D], F32, dma=True)
        ctab = P.ring(2, [96, 2, TB], F32, dma=True)
        xn_ring = P.ring(2, [128, D], BF16)
        xnT_ring = P.ring(2, [128, KC, TB], BF16)
        junk = P.sb([128, D], BF16)
        stats = [(P.sb([128, 4], F32), P.sb([128, 4], F32), P.sb([128, 4], F32), junk) for _ in range(2)]
        st2 = [(P.sb([128, 4], F32), P.sb([128, 4], F32), P.sb([128, 4], F32), junk) for _ in range(2)]
        cn_ring = P.ring(2, [128, 384], BF16)
        cT = P.sb([128, 3, TB], BF16)
        kpeT = P.sb([96, TB], BF16)
        rt1 = P.ring(2, [96, TB], F32)
        rt2 = P.ring(2, [96, TB], F32)
        QT = P.ring(2, [96, NH, TB], BF16, dma=True)
        KT = P.ring(2, [96, NH, TB], BF16, dma=True)
        vst = P.ring(2, [128, NH, 128], BF16, dma=True)
        pair = P.ring(8, [128, 2, TB], BF16, dma=True)
        dvst = P.ring(6, [128, 512], BF16, dma=True)
        for v in vst:
            S.op("pool", lambda e, v=v: e.memset(v.ap[:, :, 64:128], 1.0), [], [v.res])
        tpbs = [P.bank(BF16), P.bank(BF16)]
        pbk = [P.bank() for _ in range(6)]
        nb = [0]

        def nextbank():
            nb[0] += 1
            return pbk[nb[0] % 6]
        pi = [0]
        di = [0]
        xv = g.xres.rearrange("(n j p) d -> n p j d", j=NT, p=128)
        blocks = []
        t0 = 0
        for si, Sq in enumerate(seqs):
            for b in range(Sq // TB):
                blocks.append((si, b * TB, (t0 + b * TB) // TB))
            t0 += Sq

        def st_norm(bi):
            si, s0, gblk = blocks[bi]
            b = bi
            sc = g.scr[si]
            xb, ct, xnT = xin[b % 2], ctab[b % 2], xnT_ring[b % 2]
            qTv = sc.qT.rearrange("h r s -> r h s")
            kTv = sc.kT.rearrange("h r s -> r h s")
            vmv = sc.vm.rearrange("(n p) (h c) -> n p h c", p=128, c=128)
            dvv = sc.dv.rearrange("(n p) c -> n p c", p=128)
            S.dma("sp", xb.dom, [(lambda e, xb=xb, gblk=gblk: e.dma_start(out=xb.ap[:], in_=xv[gblk]), [], [xb.res])])
            S.dma("sp", ct.dom, [(lambda e, ct=ct, s0=s0: e.dma_start(out=ct.ap[64:96], in_=d["ropetab"][:, :, s0:s0 + TB]),
                                  [], [ct.res])])
            norm_compute(g, xb, NT, gb, xn_ring, stats[b % 2])

        def st_normT(bi):
            norm_transpose(g, NT, xn_ring, tpbs, xnT_ring[bi % 2])

        def st_zA(bi):
            si, s0, gblk = blocks[bi]
            b = bi
            sc = g.scr[si]
            xb, ct, xnT = xin[b % 2], ctab[b % 2], xnT_ring[b % 2]
            qTv = sc.qT.rearrange("h r s -> r h s")
            kTv = sc.kT.rearrange("h r s -> r h s")
            vmv = sc.vm.rearrange("(n p) (h c) -> n p h c", p=128, c=128)
            dvv = sc.dv.rearrange("(n p) c -> n p c", p=128)
            ss2, rtt2, rstd2, _ = st2[b % 2]
            for j in range(NT):
                pz = nextbank()
                cn = cn_ring[j % 2]
                for k in range(KC):
                    S.op("pe", lambda e, k=k, j=j, pz=pz, xnT=xnT: e.matmul(
                        pz.ap[:, 0:384], lhsT=xnT.ap[:, k, j * 128:(j + 1) * 128], rhs=wzA.ap[:, k, :],
                        start=(k == 0), stop=(k == KC - 1)), [xnT.res, wzA.res], [pz.res], signal=(k == KC - 1))
                S.op("act", lambda e, pz=pz, ss2=ss2: e.activation(out=junk.ap[:, 0:256], in_=pz.ap[:, 0:256], func=AF.Square,
                                                                  accum_out=ss2.ap[:, 0:1]), [pz.res], [junk.res, ss2.res])
                S.op("act", lambda e, pz=pz, ss2=ss2: e.activation(out=junk.ap[:, 0:128], in_=pz.ap[:, 256:384], func=AF.Square,
                                                                  accum_out=ss2.ap[:, 1:2]), [pz.res], [junk.res, ss2.res])
                S.op("act", lambda e, ss2=ss2, rtt2=rtt2: e.activation(out=rtt2.ap[:, 0:1], in_=ss2.ap[:, 0:1], func=AF.Sqrt,
                                                                      scale=1.0 / 256, bias=g.eps.ap[:, 0:1]),
                     [ss2.res, g.eps.res], [rtt2.res])
                S.op("act", lambda e, ss2=ss2, rtt2=rtt2: e.activation(out=rtt2.ap[:, 1:2], in_=ss2.ap[:, 1:2], func=AF.Sqrt,
                                                                      scale=1.0 / 128, bias=g.eps.ap[:, 0:1]),
                     [ss2.res, g.eps.res], [rtt2.res])
                S.op("dve", lambda e, rtt2=rtt2, rstd2=rstd2: e.reciprocal(out=rstd2.ap[:, 0:2], in_=rtt2.ap[:, 0:2]),
                     [rtt2.res], [rstd2.res])
                S.op("dve", lambda e, pz=pz, cn=cn, rstd2=rstd2: e.scalar_tensor_tensor(
                    out=cn.ap[:, 0:256], in0=pz.ap[:, 0:256], scalar=rstd2.ap[:, 0:1], in1=gq.ap[:],
                    op0=ALU.mult, op1=ALU.mult), [pz.res, rstd2.res, gq.res], [cn.res])
                S.op("dve", lambda e, pz=pz, cn=cn, rstd2=rstd2: e.scalar_tensor_tensor(
                    out=cn.ap[:, 256:384], in0=pz.ap[:, 256:384], scalar=rstd2.ap[:, 1:2], in1=gk.ap[:],
                    op0=ALU.mult, op1=ALU.mult), [pz.res, rstd2.res, gk.res], [cn.res])

        def st_cT(bi):
            for j in range(NT):
                transpose_tile(g, cn_ring[j % 2], 3, tpbs[j % 2], cT.ap[:, :, j * 128:(j + 1) * 128], cT.res, "dve")

        def st_mla(bi):
            si, s0, gblk = blocks[bi]
            b = bi
            sc = g.scr[si]
            xb, ct, xnT = xin[b % 2], ctab[b % 2], xnT_ring[b % 2]
            qTv = sc.qT.rearrange("h r s -> r h s")
            kTv = sc.kT.rearrange("h r s -> r h s")
            vmv = sc.vm.rearrange("(n p) (h c) -> n p h c", p=128, c=128)
            dvv = sc.dv.rearrange("(n p) c -> n p c", p=128)
            pk = nextbank()
            for half in range(2):
                for k in range(KC):
                    S.op("pe", lambda e, k=k, half=half, pk=pk, xnT=xnT: e.matmul(
                        pk.ap[0:96, half * TB:(half + 1) * TB], lhsT=wkp.ap[:, k, half * 96:(half + 1) * 96],
                        rhs=xnT.ap[:, k, :], start=(k == 0), stop=(k == KC - 1)),
                         [xnT.res, wkp.res], [pk.res], signal=(k == KC - 1))
            a1, a2 = rt1[0], rt2[0]
            S.op("dve", lambda e, pk=pk, a1=a1, ct=ct: e.tensor_tensor(out=a1.ap[64:96], in0=pk.ap[64:96, 0:TB], in1=ct.ap[64:96, 0, :],
                                                                      op=ALU.mult), [pk.res, ct.res], [a1.res])
            S.op("dve", lambda e, pk=pk, a2=a2, ct=ct: e.tensor_tensor(out=a2.ap[64:96], in0=pk.ap[64:96, TB:2 * TB], in1=ct.ap[64:96, 1, :],
                                                                      op=ALU.mult), [pk.res, ct.res], [a2.res])
            S.op("pool", lambda e, a1=a1, a2=a2: e.tensor_tensor(out=kpeT.ap[64:96], in0=a1.ap[64:96], in1=a2.ap[64:96], op=ALU.add),
                 [a1.res, a2.res], [kpeT.res])
            qt, kt = QT[b % 2], KT[b % 2]
            for h in range(NH):
                pq = nextbank()
                for c in range(2):
                    S.op("pe", lambda e, c=c, h=h, pq=pq: e.matmul(
                        pq.ap[0:96, 0:TB], lhsT=wq.ap[:, c, h * 192:h * 192 + 96], rhs=cT.ap[:, c, :],
                        start=(c == 0), stop=(c == 1)), [wq.res, cT.res], [pq.res], signal=False)
                for c in range(2):
                    S.op("pe", lambda e, c=c, h=h, pq=pq: e.matmul(
                        pq.ap[0:96, TB:2 * TB], lhsT=wq.ap[:, c, h * 192 + 96:h * 192 + 192], rhs=cT.ap[:, c, :],
                        start=(c == 0), stop=(c == 1)), [wq.res, cT.res], [pq.res], signal=(c == 1))
                a1, a2 = rt1[h % 2], rt2[h % 2]
                S.op("act", lambda e, pq=pq, qt=qt, h=h: e.activation(out=qt.ap[0:64, h, :], in_=pq.ap[0:64, 0:TB], func=AF.Copy),
                     [pq.res], [qt.res])
                S.op("dve", lambda e, pq=pq, a1=a1, ct=ct: e.tensor_tensor(out=a1.ap[64:96], in0=pq.ap[64:96, 0:TB], in1=ct.ap[64:96, 0, :],
                                                                          op=ALU.mult), [pq.res, ct.res], [a1.res])
                S.op("dve", lambda e, pq=pq, a2=a2, ct=ct: e.tensor_tensor(out=a2.ap[64:96], in0=pq.ap[64:96, TB:2 * TB], in1=ct.ap[64:96, 1, :],
                                                                          op=ALU.mult), [pq.res, ct.res], [a2.res])
                S.op("pool", lambda e, a1=a1, a2=a2, qt=qt, h=h: e.tensor_tensor(out=qt.ap[64:96, h, :], in0=a1.ap[64:96], in1=a2.ap[64:96],
                                                                                op=ALU.add), [a1.res, a2.res], [qt.res])
                if h % 2 == 0:
                    pkn = nextbank()
                S.op("pe", lambda e, h=h, pkn=pkn: e.matmul(
                    pkn.ap[0:64, (h % 2) * TB:(h % 2 + 1) * TB], lhsT=wkn.ap[:, 0, h * 64:(h + 1) * 64], rhs=cT.ap[:, 2, :],
                    start=True, stop=True), [wkn.res, cT.res], [pkn.res])
                if h % 2 == 1:
                    S.op("act", lambda e, pkn=pkn, kt=kt, h=h: e.activation(
                        out=kt.ap[0:64, h - 1:h + 1, :], in_=pkn.ap[0:64, :].rearrange("p (a t) -> p a t", a=2),
                        func=AF.Copy), [pkn.res], [kt.res])
                S.op("pool", lambda e, kt=kt, h=h: e.tensor_copy(out=kt.ap[64:96, h, :], in_=kpeT.ap[64:96]), [kpeT.res], [kt.res])
            S.dma("sp", qt.dom, [(lambda e, qt=qt, s0=s0: e.dma_start(out=qTv[:, :, s0:s0 + TB], in_=qt.ap[:]), [qt.res], [])])
            S.dma("sp", kt.dom, [(lambda e, kt=kt, s0=s0: e.dma_start(out=kTv[:, :, s0:s0 + TB], in_=kt.ap[:]), [kt.res], [])])
            for j in range(NT):
                pv = nextbank()
                vs = vst[j % 2]
                S.op("pe", lambda e, j=j, pv=pv: e.matmul(pv.ap[:], lhsT=cT.ap[:, 2, j * 128:(j + 1) * 128], rhs=wv.ap[:, 0, :],
                                                         start=True, stop=True), [cT.res, wv.res], [pv.res])
                S.op("act", lambda e, pv=pv, vs=vs: e.activation(out=vs.ap[:, :, 0:64],
                                                                in_=pv.ap[:].rearrange("p (h c) -> p h c", c=64), func=AF.Copy),
                     [pv.res], [vs.res])
                S.dma("sp", vs.dom, [(lambda e, vs=vs, j=j, s0=s0: e.dma_start(out=vmv[(s0 // 128) + j], in_=vs.ap[:]), [vs.res], [])])

        def st_dqk(bi):
            si, s0, gblk = blocks[bi]
            b = bi
            sc = g.scr[si]
            xb, ct, xnT = xin[b % 2], ctab[b % 2], xnT_ring[b % 2]
            qTv = sc.qT.rearrange("h r s -> r h s")
            kTv = sc.kT.rearrange("h r s -> r h s")
            vmv = sc.vm.rearrange("(n p) (h c) -> n p h c", p=128, c=128)
            dvv = sc.dv.rearrange("(n p) c -> n p c", p=128)
            for which, wsb, dstT in ((0, wdq, sc.dqT), (1, wdk, sc.dkT)):
                dview = dstT.rearrange("h p s -> p h s")
                for hp in range(DH // 2):
                    pb = nextbank()
                    for a in range(2):
                        hc = hp * 2 + a
                        for k in range(KC):
                            S.op("pe", lambda e, k=k, hc=hc, a=a, pb=pb, wsb=wsb, xnT=xnT: e.matmul(
                                pb.ap[:, a * TB:(a + 1) * TB], lhsT=wsb.ap[:, k, hc * 128:(hc + 1) * 128], rhs=xnT.ap[:, k, :],
                                start=(k == 0), stop=(k == KC - 1)), [wsb.res, xnT.res], [pb.res],
                                 signal=(k == KC - 1 and a == 1))
                    pr = pair[pi[0] % 8]
                    pi[0] += 1
                    if hp % 2 == 0:
                        S.op("act", lambda e, pb=pb, pr=pr: e.activation(out=pr.ap[:], in_=pb.ap[:].rearrange("p (a t) -> p a t", a=2),
                                                                        func=AF.Copy), [pb.res], [pr.res])
                    else:
                        S.op("dve", lambda e, pb=pb, pr=pr: e.tensor_copy(out=pr.ap[:], in_=pb.ap[:].rearrange("p (a t) -> p a t", a=2)),
                             [pb.res], [pr.res])
                    S.dma("sp", pr.dom, [(lambda e, pr=pr, hp=hp, dview=dview, s0=s0: e.dma_start(
                        out=dview[:, 2 * hp:2 * hp + 2, s0:s0 + TB], in_=pr.ap[:]), [pr.res], [])])

        def st_dv(bi):
            si, s0, gblk = blocks[bi]
            b = bi
            sc = g.scr[si]
            xb, ct, xnT = xin[b % 2], ctab[b % 2], xnT_ring[b % 2]
            qTv = sc.qT.rearrange("h r s -> r h s")
            kTv = sc.kT.rearrange("h r s -> r h s")
            vmv = sc.vm.rearrange("(n p) (h c) -> n p h c", p=128, c=128)
            dvv = sc.dv.rearrange("(n p) c -> n p c", p=128)
            for j in range(NT):
                for m in range(3):
                    pb = nextbank()
                    for k in range(KC):
                        S.op("pe", lambda e, k=k, j=j, m=m, pb=pb, xnT=xnT: e.matmul(
                            pb.ap[:], lhsT=xnT.ap[:, k, j * 128:(j + 1) * 128], rhs=wdv.ap[:, k, m * 512:(m + 1) * 512],
                            start=(k == 0), stop=(k == KC - 1)), [wdv.res, xnT.res], [pb.res], signal=(k == KC - 1))
                    dvs = dvst[di[0] % 6]
                    di[0] += 1
                    if m % 2 == 0:
                        S.op("dve", lambda e, pb=pb, dvs=dvs: e.tensor_copy(out=dvs.ap[:], in_=pb.ap[:]), [pb.res], [dvs.res])
                    else:
                        S.op("act", lambda e, pb=pb, dvs=dvs: e.activation(out=dvs.ap[:], in_=pb.ap[:], func=AF.Copy), [pb.res], [dvs.res])
                    S.dma("sp", dvs.dom, [(lambda e, dvs=dvs, j=j, m=m, s0=s0: e.dma_start(
                        out=dvv[(s0 // 128) + j][:, m * 512:(m + 1) * 512], in_=dvs.ap[:]), [dvs.res], [])])

        def st_gates(bi):
            si, s0, gblk = blocks[bi]
            b = bi
            sc = g.scr[si]
            xb, ct, xnT = xin[b % 2], ctab[b % 2], xnT_ring[b % 2]
            qTv = sc.qT.rearrange("h r s -> r h s")
            kTv = sc.kT.rearrange("h r s -> r h s")
            vmv = sc.vm.rearrange("(n p) (h c) -> n p h c", p=128, c=128)
            dvv = sc.dv.rearrange("(n p) c -> n p c", p=128)
            gview = sc.gT.rearrange("c p s -> p c s")
            for cp in range(8):
                pb = nextbank()
                for a in range(2):
                    c = cp * 2 + a
                    for k in range(KC):
                        S.op("pe", lambda e, k=k, c=c, a=a, pb=pb, xnT=xnT: e.matmul(
                            pb.ap[:, a * TB:(a + 1) * TB], lhsT=wgt.ap[:, k, c * 128:(c + 1) * 128], rhs=xnT.ap[:, k, :],
                            start=(k == 0), stop=(k == KC - 1)), [wgt.res, xnT.res], [pb.res],
                             signal=(k == KC - 1 and a == 1))
                pr = pair[pi[0] % 8]
                pi[0] += 1
                for a in range(2):
                    c = cp * 2 + a
                    S.op("act", lambda e, pb=pb, pr=pr, a=a, c=c: e.activation(
                        out=pr.ap[:, a, :], in_=pb.ap[:, a * TB:(a + 1) * TB], func=AF.Sigmoid, bias=bg.ap[:, c:c + 1]),
                         [pb.res, bg.res], [pr.res])
                S.dma("sp", pr.dom, [(lambda e, pr=pr, cp=cp, s0=s0: e.dma_start(
                    out=gview[:, 2 * cp:2 * cp + 2, s0:s0 + TB], in_=pr.ap[:]), [pr.res], [])])


        st_norm(0)
        st_normT(0)
        for bi in range(len(blocks)):
            if bi + 1 < len(blocks):
                st_norm(bi + 1)
            st_zA(bi)
            st_dqk(bi)
            if bi + 1 < len(blocks):
                st_normT(bi + 1)
            st_cT(bi)
            st_mla(bi)
            st_dv(bi)
            st_gates(bi)


def mixer_p2(g, seqs):
    S = g.S
    QB = 512
    scale = 96.0 ** -0.5
    smax = max(seqs)
    with Pass(g, "p2") as P:
        KT = P.sb([96, NH, smax], BF16, dma=True)
        VA = P.sb([128, smax // 128, NH * 128], BF16, dma=True)
        QT = P.ring(2, [96, NH, QB], BF16, dma=True)
        PT = P.ring(4, [128, 2 * QB], BF16)
        oT = P.ring(2, [64, NH, QB], BF16, dma=True)
        rl = P.ring(2, [64, QB], F32)
        pss = [P.bank2() for _ in range(3)]
        pos = [P.bank() for _ in range(2)]
        LAG = 2
        it = 0
        qn = 0
        hn = 0
        for si, Sq in enumerate(seqs):
            sc = g.scr[si]
            nkb = Sq // 128
            S.dma("sp", KT.dom, [(lambda e, sc=sc, Sq=Sq: e.dma_start(out=KT.ap[:, :, 0:Sq], in_=sc.kT.rearrange("h r s -> r h s")),
                                  [], [KT.res])])
            vv = sc.vm.rearrange("(n p) c -> p n c", p=128)
            S.dma("sp", VA.dom, [(lambda e, a=a, vv=vv: e.dma_start(out=VA.ap[:, a * 8:(a + 1) * 8, :], in_=vv[:, a * 8:(a + 1) * 8, :]),
                                  [], [VA.res]) for a in range(nkb // 8)])
            qTv = sc.qT.rearrange("h r s -> r h s")
            omv = sc.omT.rearrange("h r s -> r h s")
            nq = Sq // QB

            def loadq(qb, qTv=qTv):
                qt = QT[(qn + qb) % 2]
                S.dma("sp", qt.dom, [(lambda e, qt=qt, q0=qb * QB: e.dma_start(out=qt.ap[:], in_=qTv[:, :, q0:q0 + QB]), [], [qt.res])])
            loadq(0)
            for qb in range(nq):
                q0 = qb * QB
                qt = QT[(qn + qb) % 2]
                ot = oT[(qn + qb) % 2]
                if qb + 1 < nq:
                    loadq(qb + 1)
                items = [(h, kp) for h in range(NH) for kp in range(nkb // 2)]
                n = len(items)
                for step in range(n + LAG):
                    if step < n:
                        h, kp = items[step]
                        ps, pt = pss[(it + step) % 3], PT[(it + step) % 4]
                        for a in range(2):
                            kb = 2 * kp + a
                            S.op("pe", lambda e, ps=ps, kb=kb, h=h, qt=qt, a=a: e.matmul(
                                ps.ap[:, a * QB:(a + 1) * QB], lhsT=KT.ap[:, h, kb * 128:(kb + 1) * 128], rhs=qt.ap[:, h, :],
                                start=True, stop=True), [KT.res, qt.res], [ps.res], signal=(a == 1))
                        S.op("act", lambda e, ps=ps, pt=pt: e.activation(out=pt.ap[:], in_=ps.ap[:], func=AF.Exp, scale=scale),
                             [ps.res], [pt.res])
                    if step >= LAG:
                        h, kp = items[step - LAG]
                        pt = PT[(it + step - LAG) % 4]
                        po = pos[(hn + h) % 2]
                        for a in range(2):
                            kb = 2 * kp + a
                            S.op("pe", lambda e, po=po, kb=kb, h=h, pt=pt, a=a: e.matmul(
                                po.ap[:], lhsT=VA.ap[:, kb, h * 128:(h + 1) * 128], rhs=pt.ap[:, a * QB:(a + 1) * QB],
                                start=(kb == 0), stop=(kb == nkb - 1)),
                                 [VA.res, pt.res], [po.res], signal=(kb == nkb - 1))
                        if kp == nkb // 2 - 1:
                            r = rl[h % 2]
                            S.op("dve", lambda e, po=po, r=r: e.reciprocal(out=r.ap[:], in_=po.ap[64:128, :]), [po.res], [r.res])
                            S.op("dve", lambda e, po=po, r=r, ot=ot, h=h: e.tensor_tensor(out=ot.ap[:, h, :], in0=po.ap[0:64, :], in1=r.ap[:],
                                                                                          op=ALU.mult), [po.res, r.res], [ot.res])
                it += n
                hn += NH
                S.dma("sp", ot.dom, [(lambda e, ot=ot, q0=q0, omv=omv: e.dma_start(out=omv[:, :, q0:q0 + QB], in_=ot.ap[:]), [ot.res], [])])
            qn += nq


def mixer_p3(g, seqs):
    S = g.S
    scale = 128.0 ** -0.5
    smax = max(seqs)
    LAG = 2
    with Pass(g, "p3") as P:
        EF = P.sb([128, DH, 256], F32, dma=True)
        E1 = P.sb([64, DH, 128], F32, dma=True)
        E2 = P.sb([64, DH, 128], F32, dma=True)
        ones = P.sb([128, 128], BF16, dma=True)
        load_simple(g, EF, g.d["e_full"])
        load_simple(g, E1, g.d["e_first"])
        load_simple(g, E2, g.d["e_last"])
        load_simple(g, ones, g.d["ones"])
        QTr = P.ring(2, [128, smax], BF16, dma=True)
        KTr = P.ring(2, [128, smax], BF16, dma=True)
        Vm = P.ring(2, [128, smax // 128, 128], BF16, dma=True)
        Vf = P.ring(2, [64, 16, 128], BF16, dma=True)
        Vl = P.ring(2, [64, 16, 128], BF16, dma=True)
        acc = P.sb([128, 2, smax], F32)
        od = P.ring(2, [128, smax], BF16, dma=True)
        lnb = P.sb([128, smax], F32)
        rcp = P.sb([128, smax], F32)
        ex = P.ring(4, [128, 256], F32)
        PT = P.ring(6, [128, 256], BF16)
        pss = [P.bank() for _ in range(3)]
        pacc = [P.bank() for _ in range(4)]
        hi = 0
        ci = 0
        ji = 0
        oi = 0
        for si, Sq in enumerate(seqs):
            sc = g.scr[si]
            for slot in range(4):
                for gi, dl in enumerate(DIL):
                    head = gi * 4 + slot
                    hc = slice(head * 128, (head + 1) * 128)
                    qt, kt, vm, vf, vl = QTr[hi % 2], KTr[hi % 2], Vm[hi % 2], Vf[hi % 2], Vl[hi % 2]
                    hi += 1
                    seg = Sq // dl
                    nqb = seg // 128
                    nmid = nqb - 1
                    S.dma("sp", qt.dom, [(lambda e, qt=qt, head=head, sc=sc, Sq=Sq: e.dma_start(out=qt.ap[:, 0:Sq], in_=sc.dqT[head]), [], [qt.res])])
                    S.dma("sp", kt.dom, [(lambda e, kt=kt, head=head, sc=sc, Sq=Sq: e.dma_start(out=kt.ap[:, 0:Sq], in_=sc.dkT[head]), [], [kt.res])])
                    vfs = sc.dv[0:64 * dl, hc].rearrange("(k r) d -> k r d", r=dl)
                    vls = sc.dv[(seg - 64) * dl:seg * dl, hc].rearrange("(k r) d -> k r d", r=dl)
                    S.dma("sp", vf.dom, [(lambda e, vf=vf, vfs=vfs, dl=dl: e.dma_start(out=vf.ap[:, 0:dl, :], in_=vfs), [], [vf.res])])
                    S.dma("sp", vl.dom, [(lambda e, vl=vl, vls=vls, dl=dl: e.dma_start(out=vl.ap[:, 0:dl, :], in_=vls), [], [vl.res])])
                    if nmid > 0:
                        vms = sc.dv[64 * dl:64 * dl + 128 * dl * nmid, hc].rearrange("(c k r) d -> k r c d", k=128, r=dl)
                        vmd = vm.ap[:, 0:dl * nmid, :].rearrange("k (r c) d -> k r c d", r=dl)
                        items = []
                        for r in range(dl):
                            for c0 in range(0, nmid, 8):
                                c1 = min(nmid, c0 + 8)
                                items.append((lambda e, r=r, c0=c0, c1=c1, vms=vms, vmd=vmd: e.dma_start(
                                    out=vmd[:, r, c0:c1, :], in_=vms[:, r, c0:c1, :]), [], [vm.res]))
                        S.dma("sp", vm.dom, items)
                    pend = []

                    def flush(upto):
                        while pend and pend[0][0] <= upto:
                            pend.pop(0)[1]()
                    idx = 0
                    for r in range(dl):
                        prev = None
                        for c in range(nqb + 1):
                            first, last = (c == 0), (c == nqb)
                            nk = 64 if (first or last) else 128
                            p0 = 0 if first else 128 * c - 64
                            qlo = 128 * (c - 1) if not first else 0
                            nq = 128 if (first or last) else 256
                            if first:
                                vap, vres, em = vf.ap[0:64, r, :], vf.res, E1.ap[:, head, :]
                            elif last:
                                vap, vres, em = vl.ap[0:64, r, :], vl.res, E2.ap[:, head, :]
                            else:
                                vap, vres, em = vm.ap[:, r * nmid + (c - 1), :], vm.res, EF.ap[:, head, :]
                            ps, xx, pt = pss[ci % 3], ex[ci % 4], PT[ci % 6]
                            ci += 1
                            kcols = kt.ap[:, p0 * dl + r:(p0 + nk - 1) * dl + r + 1:dl]
                            qcols = qt.ap[:, qlo * dl + r:(qlo + nq - 1) * dl + r + 1:dl]
                            S.op("pe", lambda e, ps=ps, kcols=kcols, qcols=qcols, nk=nk, nq=nq: e.matmul(
                                ps.ap[0:nk, 0:nq], lhsT=kcols, rhs=qcols, start=True, stop=True), [kt.res, qt.res], [ps.res])
                            S.op("act", lambda e, ps=ps, xx=xx, nk=nk, nq=nq: e.activation(out=xx.ap[0:nk, 0:nq], in_=ps.ap[0:nk, 0:nq],
                                                                                           func=AF.Exp, scale=scale), [ps.res], [xx.res])
                            S.op("pool", lambda e, xx=xx, pt=pt, em=em, nk=nk, nq=nq: e.tensor_tensor(
                                out=pt.ap[0:nk, 0:nq], in0=xx.ap[0:nk, 0:nq], in1=em, op=ALU.mult),
                                 [xx.res, EF.res, E1.res, E2.res], [pt.res])
                            cur = (pt, vap, vres, nk, first)
                            if prev is not None:
                                def job(prev=prev, cur=cur, q=c - 1, r=r, dl=dl, gi=gi):
                                    nonlocal ji
                                    ppt, pvap, pvres, pnk, pfirst = prev
                                    pt, vap, vres, nk, _ = cur
                                    pb = pacc[ji % 4]
                                    ji += 1
                                    acol = 0 if pfirst else 128
                                    S.op("pe", lambda e: e.matmul(pb.ap[:, 0:128], lhsT=pvap, rhs=ppt.ap[0:pnk, acol:acol + 128],
                                                                  start=True, stop=False), [pvres, ppt.res], [pb.res], signal=False)
                                    S.op("pe", lambda e: e.matmul(pb.ap[:, 0:128], lhsT=vap, rhs=pt.ap[0:nk, 0:128],
                                                                  start=False, stop=True), [vres, pt.res], [pb.res], signal=False)
                                    S.op("pe", lambda e: e.matmul(pb.ap[:, 128:256], lhsT=ones.ap[0:pnk, :], rhs=ppt.ap[0:pnk, acol:acol + 128],
                                                                  start=True, stop=False), [ones.res, ppt.res], [pb.res], signal=False)
                                    S.op("pe", lambda e: e.matmul(pb.ap[:, 128:256], lhsT=ones.ap[0:nk, :], rhs=pt.ap[0:nk, 0:128],
                                                                  start=False, stop=True), [ones.res, pt.res], [pb.res])
                                    lo = q * 128 * dl + r
                                    hi_ = (q * 128 + 127) * dl + r + 1
                                    dst = acc.ap[:, :, lo:hi_:dl]
                                    src = pb.ap[:, 0:256].rearrange("p (a q) -> p a q", a=2)
                                    if gi == 0:
                                        S.op("dve", lambda e: e.tensor_copy(out=dst, in_=src), [pb.res], [acc.res])
                                    else:
                                        S.op("dve", lambda e: e.tensor_tensor(out=dst, in0=src, in1=dst, op=ALU.add), [pb.res], [acc.res])
                                pend.append((idx, job))
                            prev = cur
                            flush(idx - LAG)
                            idx += 1
                    flush(idx)
                o = od[oi % 2]
                oi += 1
                S.op("act", lambda e, Sq=Sq: e.activation(out=lnb.ap[:, 0:Sq], in_=acc.ap[:, 1, 0:Sq], func=AF.Ln), [acc.res], [lnb.res])
                S.op("act", lambda e, Sq=Sq: e.activation(out=rcp.ap[:, 0:Sq], in_=lnb.ap[:, 0:Sq], func=AF.Exp, scale=-1.0), [lnb.res], [rcp.res])
                S.op("dve", lambda e, o=o, Sq=Sq: e.tensor_tensor(out=o.ap[:, 0:Sq], in0=acc.ap[:, 0, 0:Sq], in1=rcp.ap[:, 0:Sq], op=ALU.mult),
                     [acc.res, rcp.res], [o.res])
                S.dma("sp", o.dom, [(lambda e, o=o, slot=slot, sc=sc, Sq=Sq: e.dma_start(out=sc.odT[slot], in_=o.ap[:, 0:Sq]), [o.res], [])])


def mixer_p4(g, l, seqs):
    S = g.S
    TB, NT = 512, 4
    d = g.d
    with Pass(g, "p4") as P:
        wbm = P.sb([64, NH, D], BF16, dma="sw")
        wbd = P.sb([128, 4, D], BF16, dma="sw")
        wo = P.sb([128, KC, D], BF16, dma="sw")
        load_w_cast(g, wbm, d["w_branch_mla"][l].rearrange("(h p) f -> p h f", p=64), D)
        load_w_cast(g, wbd, d["w_branch_dil"][l].rearrange("(c p) f -> p c f", p=128), D)
        load_w_cast(g, wo, d["w_out"][l].rearrange("(c p) f -> p c f", p=128), D)
        xin = P.ring(2, [128, NT, D], F32, dma=True)
        xst = [g.S.dma_dom() for _ in range(2)]
        om = P.ring(2, [64, NH, TB], BF16, dma=True)
        od = P.ring(2, [128, 4, TB], BF16, dma=True)
        gt = P.ring(2, [128, 16, TB], BF16, dma=True)
        mT = P.sb([128, KC, TB], BF16)
        mres = [Res() for _ in range(KC)]
        t1 = P.ring(2, [128, TB], F32)
        t2 = P.ring(2, [128, TB], F32)
        pbm = [P.bank(), P.bank()]
        pbd = [P.bank(), P.bank()]
        pys = [P.bank(), P.bank()]
        xv = g.xres.rearrange("(n j p) d -> n p j d", j=NT, p=128)
        blocks = []
        t0 = 0
        for si, Sq in enumerate(seqs):
            for b in range(Sq // TB):
                blocks.append((si, b * TB, (t0 + b * TB) // TB))
            t0 += Sq

        def loadb(bn):
            si, s0, gblk = blocks[bn]
            sc = g.scr[si]
            omv = sc.omT.rearrange("h r s -> r h s")
            odv = sc.odT.rearrange("i p s -> p i s")
            gv = sc.gT.rearrange("c p s -> p c s")
            xb, o1, o2, gg = xin[bn % 2], om[bn % 2], od[bn % 2], gt[bn % 2]
            S.dma("sp", xb.dom, [(lambda e: e.dma_start(out=xb.ap[:], in_=xv[gblk]), [], [xb.res])])
            S.dma("sp", o1.dom, [(lambda e: e.dma_start(out=o1.ap[:], in_=omv[:, :, s0:s0 + TB]), [], [o1.res])])
            S.dma("sp", o2.dom, [(lambda e: e.dma_start(out=o2.ap[:], in_=odv[:, :, s0:s0 + TB]), [], [o2.res])])
            S.dma("sp", gg.dom, [(lambda e: e.dma_start(out=gg.ap[:], in_=gv[:, :, s0:s0 + TB]), [], [gg.res])])
        loadb(0)
        for bn in range(len(blocks)):
            if True:
                si, s0, gblk = blocks[bn]
                xb, o1, o2, gg = xin[bn % 2], om[bn % 2], od[bn % 2], gt[bn % 2]
                xs_dom = xst[bn % 2]
                if bn + 1 < len(blocks):
                    loadb(bn + 1)
                for c in range(KC):
                    p1, p2, a1, a2 = pbm[c % 2], pbd[c % 2], t1[c % 2], t2[c % 2]
                    for h in range(NH):
                        S.op("pe", lambda e, h=h, c=c, p1=p1, o1=o1: e.matmul(
                            p1.ap[:], lhsT=wbm.ap[:, h, c * 128:(c + 1) * 128], rhs=o1.ap[:, h, :], start=(h == 0), stop=(h == NH - 1)),
                             [wbm.res, o1.res], [p1.res], signal=(h == NH - 1))
                    for i in range(4):
                        S.op("pe", lambda e, i=i, c=c, p2=p2, o2=o2: e.matmul(
                            p2.ap[:], lhsT=wbd.ap[:, i, c * 128:(c + 1) * 128], rhs=o2.ap[:, i, :], start=(i == 0), stop=(i == 3)),
                             [wbd.res, o2.res], [p2.res], signal=(i == 3))
                    S.op("dve", lambda e, p1=p1, a1=a1, gg=gg, c=c: e.tensor_tensor(out=a1.ap[:], in0=p1.ap[:], in1=gg.ap[:, c, :], op=ALU.mult),
                         [p1.res, gg.res], [a1.res])
                    S.op("dve", lambda e, p2=p2, a2=a2, gg=gg, c=c: e.tensor_tensor(out=a2.ap[:], in0=p2.ap[:], in1=gg.ap[:, 8 + c, :], op=ALU.mult),
                         [p2.res, gg.res], [a2.res])
                    S.op("pool", lambda e, a1=a1, a2=a2, c=c: e.tensor_tensor(out=mT.ap[:, c, :], in0=a1.ap[:], in1=a2.ap[:], op=ALU.add),
                         [a1.res, a2.res], [mres[c]])
                for j in range(NT):
                    for hf in range(2):
                        py = pys[(j * 2 + hf) % 2]
                        for c in range(KC):
                            S.op("pe", lambda e, c=c, j=j, hf=hf, py=py: e.matmul(
                                py.ap[:], lhsT=mT.ap[:, c, j * 128:(j + 1) * 128], rhs=wo.ap[:, c, hf * 512:(hf + 1) * 512],
                                start=(c == 0), stop=(c == KC - 1)), [mres[c], wo.res], [py.res], signal=(c == KC - 1))
                        S.op("dve", lambda e, j=j, hf=hf, py=py, xb=xb: e.tensor_tensor(
                            out=xb.ap[:, j, hf * 512:(hf + 1) * 512], in0=py.ap[:], in1=xb.ap[:, j, hf * 512:(hf + 1) * 512], op=ALU.add),
                             [py.res], [xb.res])
                S.dma("sp", xs_dom, [(lambda e, xb=xb, gblk=gblk: e.dma_start(out=xv[gblk], in_=xb.ap[:]), [xb.res], [])])


W_NAMES = ["ffn1_norm", "ffn1_w_gate", "ffn1_w_up", "ffn1_w_down", "mix_norm", "q_a_norm", "kv_a_norm",
           "w_branch_mla", "w_branch_dil", "w_out", "ffn2_norm", "ffn2_w_gate", "ffn2_w_up", "ffn2_w_down",
           "final_norm", "w_zA", "w_kpe2", "w_dq", "w_dk", "w_dv", "w_gt", "w_q2", "w_kn", "w_v", "bgate2",
           "ropetab", "e_full", "e_first", "e_last", "ident", "ones"]


def build_program(seqs, depth, shapes, dtypes):
    T = sum(seqs)
    nc = bass.Bass("TRN2", target_bir_lowering=False)
    g = G()
    g.nc = nc
    g.d = {}
    for n in W_NAMES:
        g.d[n] = nc.dram_tensor(n, list(shapes[n]), dtypes[n], kind="ExternalInput").ap()
    x_in = nc.dram_tensor("x", [T, D], F32, kind="ExternalInput").ap()
    y_out = nc.dram_tensor("y", [T, D], F32, kind="ExternalOutput").ap()
    g.xres = nc.dram_tensor("xres", [T, D], F32, kind="Internal").ap()
    g.scr = []
    for si, sq in enumerate(seqs):
        sc = G()
        mk = lambda n, shp: nc.dram_tensor("s%d_%s" % (si, n), shp, BF16, kind="Internal").ap()
        sc.qT = mk("qT", [NH, 96, sq])
        sc.kT = mk("kT", [NH, 96, sq])
        sc.vm = mk("vm", [sq, NH * 128])
        sc.dqT = mk("dqT", [DH, 128, sq])
        sc.dkT = mk("dkT", [DH, 128, sq])
        sc.dv = mk("dv", [sq, DH * 128])
        sc.gT = mk("gT", [16, 128, sq])
        sc.omT = mk("omT", [NH, 64, sq])
        sc.odT = mk("odT", [4, 128, sq])
        g.scr.append(sc)
    with ExitStack() as st:
        S = Sched(nc, st)
        g.S = S
        identt = st.enter_context(nc.sbuf_tensor("ident_sb", [128, 128], BF16))
        epst = st.enter_context(nc.sbuf_tensor("eps_sb", [128, 1], F32))
        g.ident = Slot(identt, S.pool[-1])
        g.eps = Slot(epst)
        S.dma("sp", g.ident.dom, [(lambda e: e.dma_start(out=identt[:], in_=g.d["ident"]), [], [g.ident.res])])
        S.op("pool", lambda e: e.memset(epst[:], EPS), [], [g.eps.res])
        S.pool = S.pool[:-1]
        for l in range(depth):
            ffn_pass(g, l, 1, x_in if l == 0 else g.xres, g.xres, T)
            mixer_p1(g, l, seqs)
            mixer_p2(g, seqs)
            mixer_p3(g, seqs)
            mixer_p4(g, l, seqs)
            ffn_pass(g, l, 2, g.xres, g.xres, T)
        final_pass(g, g.xres, y_out, T)
        S.barrier()
        S.emit()
    g.n_ins = dict(S.n_ins)
    return nc, g


def host_constants():
    pos = np.arange(SMAX, dtype=np.float32)
    inv = (1.0 / (np.float32(10000.0) ** (np.arange(0, 32, 2, dtype=np.float32) / np.float32(32)))).astype(np.float32)
    ang = (pos[:, None] * inv[None, :]).astype(np.float32)
    cos, sin = np.cos(ang).astype(np.float32).T, np.sin(ang).astype(np.float32).T
    rope = np.zeros((32, 2, SMAX), np.float32)
    rope[0:16, 0], rope[16:32, 0] = cos, cos
    rope[0:16, 1], rope[16:32, 1] = -sin, sin
    slopes = (2.0 ** (-8.0 * np.arange(1, DH + 1, dtype=np.float32) / DH)).astype(np.float32)
    k = np.arange(128)[:, None]
    q = np.arange(128)[None, :]
    e_full = np.zeros((128, DH, 256), np.float32)
    e_first = np.zeros((64, DH, 128), np.float32)
    e_last = np.zeros((64, DH, 128), np.float32)
    for h in range(DH):
        dl = DIL[h // 4]
        sl = slopes[h] * dl
        relB = k - q + 64
        e_full[:, h, 0:128] = np.where(k <= q, np.exp(-sl * np.abs(relB).astype(np.float32)), 0.0)
        relA = k - q - 64
        e_full[:, h, 128:256] = np.where(k >= q, np.exp(-sl * np.abs(relA).astype(np.float32)), 0.0)
        k6 = np.arange(64)[:, None]
        rel1 = k6 - q
        e_first[:, h, :] = np.where(np.abs(rel1) <= 64, np.exp(-sl * np.abs(rel1).astype(np.float32)), 0.0)
        rel2 = 64 + k6 - q
        e_last[:, h, :] = np.where(np.abs(rel2) <= 64, np.exp(-sl * np.abs(rel2).astype(np.float32)), 0.0)
    return {"ropetab": rope, "e_full": e_full, "e_first": e_first, "e_last": e_last,
            "ident": np.eye(128, dtype=np.float32).astype(ml_dtypes.bfloat16),
            "ones": np.ones((128, 128), np.float32).astype(ml_dtypes.bfloat16)}


def host_weights(inp):
    f = lambda a: np.ascontiguousarray(np.asarray(a, dtype=np.float32))
    w_in = f(inp["w_in"])
    L = w_in.shape[0]
    out = {}
    for n in ["ffn1_norm", "ffn1_w_gate", "ffn1_w_up", "ffn1_w_down", "mix_norm", "q_a_norm", "kv_a_norm",
              "w_branch_mla", "w_branch_dil", "w_out", "ffn2_norm", "ffn2_w_gate", "ffn2_w_up", "ffn2_w_down", "final_norm"]:
        out[n] = f(inp[n])
    out["w_zA"] = f(w_in[:, :, 0:384])
    z64 = np.zeros((L, w_in.shape[1], 64), np.float32)
    out["w_kpe2"] = f(np.concatenate([z64, w_in[:, :, 384:416], z64, w_in[:, :, 400:416], w_in[:, :, 384:400]], axis=2))
    out["w_dq"] = f(w_in[:, :, 416:1952])
    out["w_dk"] = f(w_in[:, :, 1952:3488])
    out["w_dv"] = f(w_in[:, :, 3488:5024])
    out["w_gt"] = f(w_in[:, :, 5024:7072])
    wq = f(inp["w_q_up"]).reshape(L, 256, NH, 96)
    nope, rope = wq[..., 0:64], wq[..., 64:96]
    out["w_q2"] = f(np.concatenate([nope, rope, np.zeros_like(nope), rope[..., 16:32], rope[..., 0:16]], axis=3).reshape(L, 256, NH * 192))
    wkv = f(inp["w_kv_up"]).reshape(L, 128, NH, 128)
    out["w_kn"] = f(wkv[..., 0:64].reshape(L, 128, NH * 64))
    out["w_v"] = f(wkv[..., 64:128].reshape(L, 128, NH * 64))
    out["bgate2"] = f(f(inp["b_gate"]).reshape(L, 16, 128).transpose(0, 2, 1))
    out.update(host_constants())
    return out


_CACHE = {}


def run_cores(xs, seqs, depth, W, trace=False):
    shapes = {k: v.shape for k, v in W.items()}
    dtypes = {k: (BF16 if v.dtype == ml_dtypes.bfloat16 else F32) for k, v in W.items()}
    key = (tuple(seqs), depth)
    if key not in _CACHE:
        _CACHE[key] = build_program(seqs, depth, shapes, dtypes)
    nc, g = _CACHE[key]
    in_maps = []
    for x in xs:
        m = dict(W)
        m["x"] = x
        in_maps.append(m)
    res = run_bass_kernel_spmd(nc, in_maps, core_ids=list(range(len(xs))), trace=trace)
    return [r["y"] for r in res.results], res


def kernel(**inp):
    xp = np.asarray(inp["x_prompt"], dtype=np.float32)
    xs_ = np.asarray(inp["x_sample"], dtype=np.float32)
    W = host_weights(inp)
    depth = W["w_zA"].shape[0]
    ncore = 8
    pb, sbn = xp.shape[0] // ncore, xs_.shape[0] // ncore
    seqs = [xp.shape[1]] * pb + [xs_.shape[1]] * sbn
    xs = []
    for c in range(ncore):
        xs.append(np.ascontiguousarray(np.concatenate(
            [xp[c * pb:(c + 1) * pb].reshape(-1, D), xs_[c * sbn:(c + 1) * sbn].reshape(-1, D)], axis=0)))
    ys, _ = run_cores(xs, seqs, depth, W)
    np_ = pb * xp.shape[1]
    y_prompt = np.concatenate([y[:np_].reshape(pb, xp.shape[1], D) for y in ys], axis=0)
    y_sample = np.concatenate([y[np_:].reshape(sbn, xs_.shape[1], D) for y in ys], axis=0)
    return (y_prompt.astype(np.float32), y_sample.astype(np.float32))
```
